# Optimizing a Trainium2 kernel written in Bass

```python
import math
import jax
import jax.numpy as jnp
from jax import lax
import numpy as np

D_MODEL = 2048
BATCH = 16
SEQ = 256
DEPTH = 1
DEC_BATCH = 8
DEC_SEQ = 2048
PAST_LEN = 512

GRID_W = 64
H_R = 8
DK_R = 256
DV_R = 512
H_D = 16
DK_D = 128
DV_D = 128
D_FF = 5632
SHORT_CONV = 3
FFN_CONV = 3
RET_CHUNK = 128
DN_CHUNK = 64
ROPE_BASE = 10000.0
LN_EPS = 1e-5
RMS_EPS = 1e-6
L2_EPS = 1e-6
DEEPNORM_ALPHA = (2.0 * DEPTH) ** 0.25
DEEPNORM_BETA = (8.0 * DEPTH) ** -0.25

RQ = H_R * DK_R
RV = H_R * DV_R
DQK = H_D * DK_D
DVD = H_D * DV_D
DN_QKV = 2 * DQK + DVD
IN_SIZES = (RQ, RQ, RV, RV, DN_QKV, DVD, 2 * H_D, 2 * H_D, D_MODEL, D_MODEL)
D_IN = sum(IN_SIZES)
IN_SPLITS = tuple(int(s) for s in np.cumsum(IN_SIZES)[:-1])

kernel_name = 'hybrid_retention_deltanet_diffusion_step'


def _layernorm(x):
    xf = x.astype(jnp.float32)
    mu = jnp.mean(xf, axis=-1, keepdims=True)
    var = jnp.mean(jnp.square(xf - mu), axis=-1, keepdims=True)
    return ((xf - mu) * lax.rsqrt(var + LN_EPS)).astype(x.dtype)


def _rmsnorm(x):
    xf = x.astype(jnp.float32)
    return (xf * lax.rsqrt(jnp.mean(jnp.square(xf), axis=-1, keepdims=True) + RMS_EPS)).astype(x.dtype)


def _l2norm(x):
    xf = x.astype(jnp.float32)
    return (xf * lax.rsqrt(jnp.sum(jnp.square(xf), axis=-1, keepdims=True) + L2_EPS)).astype(x.dtype)


def _dwconv_centered(x, w):
    k_w = w.shape[0]
    pad = k_w // 2
    t = x.shape[1]
    xp = jnp.pad(x, ((0, 0), (pad, pad), (0, 0)))
    return sum(xp[:, i:i + t] * w[i] for i in range(k_w))


def _grid_rope_tables(n_tokens):
    rows = n_tokens // GRID_W
    row = jnp.repeat(jnp.arange(rows, dtype=jnp.float32), GRID_W)
    col = jnp.tile(jnp.arange(GRID_W, dtype=jnp.float32), rows)
    n_freq = DK_R // 4
    inv_freq = ROPE_BASE ** (-jnp.arange(n_freq, dtype=jnp.float32) / n_freq)
    ang = jnp.concatenate([row[:, None] * inv_freq, col[:, None] * inv_freq], axis=-1)
    return jnp.cos(ang), jnp.sin(ang)


def _apply_rope(x, cos, sin):
    c = cos[None, :, None, :].astype(x.dtype)
    s = sin[None, :, None, :].astype(x.dtype)
    x1 = x[..., 0::2]
    x2 = x[..., 1::2]
    return jnp.stack([x1 * c - x2 * s, x1 * s + x2 * c], axis=-1).reshape(x.shape)


def _to_chunks(x, chunk):
    b, t, h = x.shape[:3]
    rest = x.shape[3:]
    x = x.astype(jnp.float32).reshape((b, t // chunk, chunk, h) + rest)
    return jnp.moveaxis(jnp.moveaxis(x, 1, 0), 3, 2)


def _from_chunks(o):
    o = jnp.moveaxis(jnp.moveaxis(o, 2, 3), 0, 1)
    n_b, n_n, c = o.shape[:3]
    return o.reshape((n_b, n_n * c) + o.shape[3:])


def _retention(q, k, v, log_gamma, s0):
    qc = _to_chunks(q, RET_CHUNK)
    kc = _to_chunks(k, RET_CHUNK)
    vc = _to_chunks(v, RET_CHUNK)
    lg = log_gamma.astype(jnp.float32)
    pos = jnp.arange(RET_CHUNK, dtype=jnp.float32)
    diff = pos[:, None] - pos[None, :]
    intra = jnp.where(diff >= 0, jnp.exp(lg[:, None, None] * jnp.maximum(diff, 0.0)), 0.0)
    q_dec = jnp.exp(lg[:, None] * (pos + 1.0))[:, :, None]
    k_dec = jnp.exp(lg[:, None] * (RET_CHUNK - 1.0 - pos))[:, :, None]
    c_dec = jnp.exp(lg * RET_CHUNK)[:, None, None]

    def step(s, xs):
        qb, kb, vb = xs
        scores = jnp.einsum('bhik,bhjk->bhij', qb, kb) * intra
        o = jnp.einsum('bhij,bhjv->bhiv', scores, vb) + jnp.einsum('bhik,bhkv->bhiv', qb * q_dec, s)
        s = s * c_dec + jnp.einsum('bhjk,bhjv->bhkv', kb * k_dec, vb)
        return s, o

    s_fin, o = lax.scan(step, s0.astype(jnp.float32), (qc, kc, vc))
    return _from_chunks(o), s_fin


def _gated_delta(q, k, v, beta, g, s0):
    qc = _to_chunks(q, DN_CHUNK)
    kc = _to_chunks(k, DN_CHUNK)
    vc = _to_chunks(v, DN_CHUNK)
    bc = _to_chunks(beta, DN_CHUNK)
    gc = jnp.cumsum(_to_chunks(g, DN_CHUNK), axis=-1)
    idx = jnp.arange(DN_CHUNK)
    incl = idx[:, None] >= idx[None, :]
    strict = idx[:, None] > idx[None, :]
    decay = jnp.exp(jnp.where(incl, gc[..., :, None] - gc[..., None, :], -jnp.inf))
    kb = kc * bc[..., None]
    lmat = jnp.where(strict, jnp.einsum('...ik,...jk->...ij', kb, kc) * decay, 0.0)
    eye = jnp.eye(DN_CHUNK, dtype=jnp.float32)
    tmat = lax.linalg.triangular_solve(eye + lmat, jnp.broadcast_to(eye, lmat.shape),
                                       left_side=True, lower=True, unit_diagonal=True)
    u = jnp.einsum('...ij,...jv->...iv', tmat, vc * bc[..., None])
    w = jnp.einsum('...ij,...jk->...ik', tmat, kb * jnp.exp(gc)[..., None])
    attn = jnp.einsum('...ik,...jk->...ij', qc, kc) * decay
    qg = qc * jnp.exp(gc)[..., None]
    kg = kc * jnp.exp(gc[..., -1:] - gc)[..., None]
    g_last = jnp.exp(gc[..., -1])[..., None, None]

    def step(s, xs):
        attn_b, u_b, w_b, qg_b, kg_b, gl_b = xs
        v_new = u_b - jnp.einsum('bhik,bhkv->bhiv', w_b, s)
        o = jnp.einsum('bhik,bhkv->bhiv', qg_b, s) + jnp.einsum('bhij,bhjv->bhiv', attn_b, v_new)
        s = s * gl_b + jnp.einsum('bhjk,bhjv->bhkv', kg_b, v_new)
        return s, o

    s_fin, o = lax.scan(step, s0.astype(jnp.float32), (attn, u, w, qg, kg, g_last))
    return _from_chunks(o), s_fin


def _token_mixer(h, s0_ret, s0_dn, rope, lp):
    b, t, _ = h.shape
    p = h @ lp['w_in']
    rq, rk, rv, rg, dqkv, dz, db, da, gr, gd = jnp.split(p, IN_SPLITS, axis=-1)

    rq = rq.reshape(b, t, H_R, DK_R)
    rk = rk.reshape(b, t, H_R, DK_R) * (DK_R ** -0.5)
    rv = rv.reshape(b, t, H_R, DV_R)
    if rope is not None:
        rq = _apply_rope(rq, rope[0], rope[1])
        rk = _apply_rope(rk, rope[0], rope[1])
    o_rf, s_rf = _retention(rq, rk, rv, lp['ret_log_decay'][0], s0_ret[:, 0])
    o_rb, s_rb = _retention(rq[:, ::-1], rk[:, ::-1], rv[:, ::-1], lp['ret_log_decay'][1], s0_ret[:, 1])
    o_r = _layernorm(o_rf + o_rb[:, ::-1]).astype(h.dtype) * lp['ret_norm_w'].reshape(H_R, DV_R)
    y_r = (o_r.reshape(b, t, RV) * jax.nn.silu(rg)) @ lp['w_br']

    dqkv = jax.nn.silu(_dwconv_centered(dqkv, lp['dn_conv_w']))
    dq, dk, dv = jnp.split(dqkv, (DQK, 2 * DQK), axis=-1)
    dq = _l2norm(dq.reshape(b, t, H_D, DK_D)) * (DK_D ** -0.5)
    dk = _l2norm(dk.reshape(b, t, H_D, DK_D))
    dv = dv.reshape(b, t, H_D, DV_D)
    beta = jax.nn.sigmoid(db.astype(jnp.float32)).reshape(b, t, 2, H_D)
    g = -jnp.exp(lp['dn_A_log'].astype(jnp.float32)) * jax.nn.softplus(
        da.astype(jnp.float32).reshape(b, t, 2, H_D) + lp['dn_dt_bias'].astype(jnp.float32))
    o_df, s_df = _gated_delta(dq, dk, dv, beta[:, :, 0], g[:, :, 0], s0_dn[:, 0])
    o_db, s_db = _gated_delta(dq[:, ::-1], dk[:, ::-1], dv[:, ::-1], beta[:, ::-1, 1], g[:, ::-1, 1], s0_dn[:, 1])
    o_d = _rmsnorm(o_df + o_db[:, ::-1]).astype(h.dtype) * lp['dn_norm_w'].reshape(H_D, DV_D)
    y_d = (o_d.reshape(b, t, DVD) * jax.nn.silu(dz)) @ lp['w_bd']

    y = jax.nn.sigmoid(gr) * y_r + jax.nn.sigmoid(gd) * y_d
    out = y @ lp['w_o']
    s_ret = jnp.stack([s_rf, s_rb], axis=1)
    s_dn = jnp.stack([s_df, s_db], axis=1)
    return out, s_ret, s_dn


def _conv_ffn(h, lp):
    u = _dwconv_centered(h @ lp['w_up'], lp['ffn_conv_w']) + lp['ffn_conv_b']
    a, v = jnp.split(u, 2, axis=-1)
    return (jax.nn.silu(a) * v) @ lp['w_down']


def _layer(x, cond, s0_ret, s0_dn, rope, lp):
    mod = jax.nn.silu(cond) @ lp['w_ada'] + lp['b_ada']
    sh1, sc1, g1, sh2, sc2, g2 = jnp.split(mod[:, None, :], 6, axis=-1)
    h = _layernorm(x) * (1.0 + sc1) + sh1
    mix, s_ret, s_dn = _token_mixer(h, s0_ret, s0_dn, rope, lp)
    x = _layernorm(DEEPNORM_ALPHA * x + g1 * mix) * lp['ln1_g'] + lp['ln1_b']
    h = _layernorm(x) * (1.0 + sc2) + sh2
    x = _layernorm(DEEPNORM_ALPHA * x + g2 * _conv_ffn(h, lp)) * lp['ln2_g'] + lp['ln2_b']
    return x, s_ret, s_dn


def setup_inputs(seed: int = 0) -> dict:
    key = jax.random.key(seed)
    ks = jax.random.split(key, 32)
    f32 = jnp.float32

    def nrm(k, shape, scale):
        return jax.random.normal(k, shape, f32) * scale

    x_prompt = nrm(ks[0], (BATCH, SEQ, D_MODEL), 1.0)
    x_sample = nrm(ks[1], (DEC_BATCH, DEC_SEQ, D_MODEL), 1.0)
    state_ret = nrm(ks[2], (DEC_BATCH, DEPTH, 2, H_R, DK_R, DV_R), 0.5)
    state_dn = nrm(ks[3], (DEC_BATCH, DEPTH, 2, H_D, DK_D, DV_D), 0.1)
    c = nrm(ks[4], (DEC_BATCH, D_MODEL), 1.0)
    c_ctx = nrm(ks[5], (D_MODEL,), 1.0)
    w_ada = nrm(ks[6], (DEPTH, D_MODEL, 6 * D_MODEL), 0.5 * D_MODEL ** -0.5)
    b_ada = nrm(ks[7], (DEPTH, 6 * D_MODEL), 0.02)
    w_in = nrm(ks[8], (DEPTH, D_MODEL, D_IN), D_MODEL ** -0.5)
    dn_conv_w = nrm(ks[9], (DEPTH, SHORT_CONV, DN_QKV), SHORT_CONV ** -0.5)
    base_decay = jnp.log1p(-jnp.exp2(-5.0 - jnp.arange(H_R, dtype=f32)))
    ret_log_decay = base_decay * (1.0 + 0.05 * jax.random.normal(ks[10], (DEPTH, 2, H_R), f32))
    ret_norm_w = 1.0 + nrm(ks[11], (DEPTH, RV), 0.02)
    dn_A_log = jnp.log(jax.random.uniform(ks[12], (DEPTH, 2, H_D), f32, minval=1.0, maxval=16.0))
    dt = jnp.exp(jax.random.uniform(ks[13], (DEPTH, 2, H_D), f32, minval=math.log(1e-3), maxval=math.log(1e-1)))
    dn_dt_bias = dt + jnp.log(-jnp.expm1(-dt))
    dn_norm_w = 1.0 + nrm(ks[14], (DEPTH, DVD), 0.02)
    w_br = nrm(ks[15], (DEPTH, RV, D_MODEL), RV ** -0.5)
    w_bd = nrm(ks[16], (DEPTH, DVD, D_MODEL), DVD ** -0.5)
    w_o = nrm(ks[17], (DEPTH, D_MODEL, D_MODEL), D_MODEL ** -0.5 * DEEPNORM_BETA)
    ln1_g = 1.0 + nrm(ks[18], (DEPTH, D_MODEL), 0.02)
    ln1_b = nrm(ks[19], (DEPTH, D_MODEL), 0.02)
    w_up = nrm(ks[20], (DEPTH, D_MODEL, 2 * D_FF), D_MODEL ** -0.5)
    ffn_conv_w = nrm(ks[21], (DEPTH, FFN_CONV, 2 * D_FF), FFN_CONV ** -0.5)
    ffn_conv_b = nrm(ks[22], (DEPTH, 2 * D_FF), 0.02)
    w_down = nrm(ks[23], (DEPTH, D_FF, D_MODEL), D_FF ** -0.5 * DEEPNORM_BETA)
    ln2_g = 1.0 + nrm(ks[24], (DEPTH, D_MODEL), 0.02)
    ln2_b = nrm(ks[25], (DEPTH, D_MODEL), 0.02)
    return {'x_prompt': x_prompt, 'x_sample': x_sample, 'state_ret': state_ret, 'state_dn': state_dn,
            'c': c, 'c_ctx': c_ctx, 'w_ada': w_ada, 'b_ada': b_ada, 'w_in': w_in, 'dn_conv_w': dn_conv_w,
            'ret_log_decay': ret_log_decay, 'ret_norm_w': ret_norm_w, 'dn_A_log': dn_A_log,
            'dn_dt_bias': dn_dt_bias, 'dn_norm_w': dn_norm_w, 'w_br': w_br, 'w_bd': w_bd, 'w_o': w_o,
            'ln1_g': ln1_g, 'ln1_b': ln1_b, 'w_up': w_up, 'ffn_conv_w': ffn_conv_w, 'ffn_conv_b': ffn_conv_b,
            'w_down': w_down, 'ln2_g': ln2_g, 'ln2_b': ln2_b}


def reference(x_prompt, x_sample, state_ret, state_dn, c, c_ctx, w_ada, b_ada, w_in, dn_conv_w,
              ret_log_decay, ret_norm_w, dn_A_log, dn_dt_bias, dn_norm_w, w_br, w_bd, w_o,
              ln1_g, ln1_b, w_up, ffn_conv_w, ffn_conv_b, w_down, ln2_g, ln2_b):
    n_lat = x_sample.shape[1]
    rope = _grid_rope_tables(n_lat)
    n_p = x_prompt.shape[0]
    zero_ret = jnp.zeros((n_p, 2, H_R, DK_R, DV_R), jnp.float32)
    zero_dn = jnp.zeros((n_p, 2, H_D, DK_D, DV_D), jnp.float32)
    ctx_cond = c_ctx[None, :]
    y_prompt = x_prompt
    y_sample = x_sample
    new_ret = []
    new_dn = []
    for l in range(DEPTH):
        lp = {'w_ada': w_ada[l], 'b_ada': b_ada[l], 'w_in': w_in[l], 'dn_conv_w': dn_conv_w[l],
              'ret_log_decay': ret_log_decay[l], 'ret_norm_w': ret_norm_w[l], 'dn_A_log': dn_A_log[l],
              'dn_dt_bias': dn_dt_bias[l], 'dn_norm_w': dn_norm_w[l], 'w_br': w_br[l], 'w_bd': w_bd[l],
              'w_o': w_o[l], 'ln1_g': ln1_g[l], 'ln1_b': ln1_b[l], 'w_up': w_up[l],
              'ffn_conv_w': ffn_conv_w[l], 'ffn_conv_b': ffn_conv_b[l], 'w_down': w_down[l],
              'ln2_g': ln2_g[l], 'ln2_b': ln2_b[l]}
        y_prompt, s_ret, s_dn = _layer(y_prompt, ctx_cond, zero_ret, zero_dn, None, lp)
        new_ret.append(s_ret.astype(x_prompt.dtype))
        new_dn.append(s_dn.astype(x_prompt.dtype))
        y_sample, _, _ = _layer(y_sample, c, state_ret[:, l], state_dn[:, l], rope, lp)
    return (y_prompt, y_sample, jnp.stack(new_ret, axis=1), jnp.stack(new_dn, axis=1))
```

```python
from contextlib import ExitStack
import numpy as np
import ml_dtypes
import concourse.bass as bass
import concourse.mybir as mybir
from concourse.bass_utils import run_bass_kernel_spmd

F32 = mybir.dt.float32
BF16 = mybir.dt.bfloat16
AF = mybir.ActivationFunctionType
ALU = mybir.AluOpType
AX = mybir.AxisListType

D = 2048
NTOK = 2560
KC = 16
LN_EPS = 1e-5

ENGS = ["pe", "act", "dve", "pool", "sp"]
NRING = 24
STQ = "act"


class Buf:
    __slots__ = ("name", "w", "r")

    def __init__(self, name=""):
        self.name = name
        self.w = None
        self.r = {}


class Sync:
    def __init__(self, nc, es):
        self.nc = nc
        self.esem = {e: es.enter_context(nc.semaphore("s_" + e)) for e in ENGS if e != "sp"}
        self.ring = [es.enter_context(nc.semaphore("r_%d" % i)) for i in range(NRING)]
        self.ebase = {e: 0 for e in self.esem}
        self.rcount = [0] * NRING
        self.ndma = 0


class Prog:
    def __init__(self, sync, selfsync=("act", "dve", "pool")):
        self.s = sync
        self.ops = {e: [] for e in ENGS}
        self.selfsync = set(selfsync)
        self.ring_last = {}

    def _deps(self, reads, writes):
        d = {}

        def add(t):
            if t is None:
                return
            k = (t[0], t[1])
            if k not in d or d[k][2] < t[2]:
                d[k] = t

        for b in reads:
            add(b.w)
        for b in writes:
            add(b.w)
            for t in b.r.values():
                add(t)
        return d

    def _finish(self, tok, reads, writes):
        k = (tok[0], tok[1])
        for b in reads:
            b.r[k] = tok
        for b in writes:
            b.w = tok
            b.r = {}

    def op(self, eng, fn, reads=(), writes=()):
        d = self._deps(reads, writes)
        if eng not in self.selfsync:
            d.pop(("e", eng), None)
        tok = ("e", eng, len(self.ops[eng]))
        self.ops[eng].append({"fn": fn, "deps": list(d.values()), "needed": False, "ring": None})
        self._finish(tok, reads, writes)
        return tok

    def dma(self, eng, fn, reads=(), writes=()):
        s = self.s
        ri = s.ndma % NRING
        s.ndma += 1
        s.rcount[ri] += 1
        tok = ("s", ri, 16 * s.rcount[ri])
        d = self._deps(reads, writes)
        if eng not in self.selfsync:
            d.pop(("e", eng), None)
        deps = list(d.values())
        prev = self.ring_last.get(ri)
        if prev is not None:
            deps.append(prev)
        self.ring_last[ri] = tok
        self.ops[eng].append({"fn": fn, "deps": deps, "needed": False, "ring": ri})
        self._finish(tok, reads, writes)
        return tok

    def wait_all_dma(self, eng="sp"):
        self.ops[eng].append({"fn": None, "deps": list(self.ring_last.values()), "needed": False, "ring": None})

    def emit(self, block):
        s = self.s
        for e in ENGS:
            for o in self.ops[e]:
                for t in o["deps"]:
                    if t[0] == "e":
                        self.ops[t[1]][t[2]]["needed"] = True
        for e in ENGS:
            if e == "sp":
                continue
            c = s.ebase[e]
            for o in self.ops[e]:
                if o["needed"]:
                    c += 1
                    o["val"] = c
            s.ebase[e] = c

        def resolve(t):
            if t[0] == "e":
                return s.esem[t[1]], self.ops[t[1]][t[2]]["val"]
            return s.ring[t[1]], t[2]

        def emit_engine(ename, eh):
            waited = {}
            for o in self.ops[ename]:
                ws = {}
                for t in o["deps"]:
                    sem, v = resolve(t)
                    k = id(sem)
                    if waited.get(k, 0) >= v:
                        continue
                    if k not in ws or ws[k][1] < v:
                        ws[k] = (sem, v)
                wl = list(ws.values())
                for sem, v in wl:
                    waited[id(sem)] = v
                if o["fn"] is None:
                    for sem, v in wl:
                        eh.wait_ge(sem, v)
                    continue
                embed = None
                if wl and ename != "pe":
                    embed = wl.pop()
                for sem, v in wl:
                    eh.wait_ge(sem, v)
                inst = o["fn"](eh)
                if embed is not None:
                    inst._wait_ge(embed[0], embed[1])
                if o["ring"] is not None:
                    inst.then_inc(s.ring[o["ring"]], 16)
                elif o["needed"]:
                    inst.then_inc(s.esem[ename], 1)

        @block.tensor
        def _(eh):
            emit_engine("pe", eh)

        @block.scalar
        def _(eh):
            emit_engine("act", eh)

        @block.vector
        def _(eh):
            emit_engine("dve", eh)

        @block.gpsimd
        def _(eh):
            emit_engine("pool", eh)

        @block.sync
        def _(eh):
            emit_engine("sp", eh)


class Ctx:
    CNT = [0]

    def __init__(self, nc, es):
        self.nc = nc
        self.es = es

    def sb(self, shape, dt, name=None):
        Ctx.CNT[0] += 1
        return self.es.enter_context(self.nc.sbuf_tensor("%s_%d" % (name or "t", Ctx.CNT[0]), list(shape), dt))

    def ps(self, shape, dt, name=None):
        Ctx.CNT[0] += 1
        return self.es.enter_context(self.nc.psum_tensor("%s_%d" % (name or "p", Ctx.CNT[0]), list(shape), dt))


def stage_ada(nc, sync, G):
    with ExitStack() as es:
        cx = Ctx(nc, es)
        P = Prog(sync)
        scT = cx.sb([128, KC, 2], F32, "scT")
        brow = cx.sb([2, 6 * D], F32, "brow")
        mrow = cx.sb([2, 6 * D], F32, "mrow")
        sel = cx.sb([2, 2, 128], F32, "sel")
        wb = [cx.sb([128, KC, 512], F32, "wada") for _ in range(3)]
        ps = [cx.ps([128, 512], F32, "psA") for _ in range(2)]
        pt = cx.ps([128, 512], F32, "psT")
        b_sc, b_br, b_mr, b_sel, b_pt = Buf(), Buf(), Buf(), Buf(), Buf()
        b_wb = [Buf() for _ in range(3)]
        b_ps = [Buf(), Buf()]
        modT = G["modT"]
        b_mod = Buf()
        P.dma("sp", lambda e: e.dma_start(out=scT[:], in_=G["condT"][:, :, :]), writes=[b_sc])
        P.dma("sp", lambda e: e.dma_start(out=brow[:], in_=G["b_adarow"][:, :]), writes=[b_br])
        P.dma("sp", lambda e: e.dma_start(out=sel[:], in_=G["sel"][:, :, :]), writes=[b_sel])
        P.op("act", lambda e: e.activation(out=scT[:], in_=scT[:], func=AF.Silu), reads=[b_sc], writes=[b_sc])
        wv = G["w_ada"].rearrange("(kc p) n -> p kc n", p=128)
        for nb in range(24):
            sl = nb % 3
            k = nb % 2
            P.dma("sp", lambda e, nb=nb, sl=sl: e.dma_start(out=wb[sl][:], in_=wv[:, :, nb * 512:(nb + 1) * 512]),
                  writes=[b_wb[sl]])
            mm_acc(P, ps[k][0:2, :], b_ps[k], [(scT[:, kc, :], wb[sl][:, kc, :]) for kc in range(KC)],
                   [b_wb[sl], b_sc])
            P.op("dve", lambda e, nb=nb, k=k: e.tensor_tensor(out=mrow[:, nb * 512:(nb + 1) * 512], in0=ps[k][0:2, :],
                                                               in1=brow[:, nb * 512:(nb + 1) * 512], op=ALU.add),
                 reads=[b_ps[k], b_br], writes=[b_mr])
        P.dma(STQ, lambda e: e.dma_start(out=G["modrow"][:, 0:2048], in_=mrow[:, 4096:6144]), reads=[b_mr])
        P.dma(STQ, lambda e: e.dma_start(out=G["modrow"][:, 2048:4096], in_=mrow[:, 10240:12288]), reads=[b_mr])
        for j in range(96):
            P.op("pe", lambda e, j=j: e.transpose(out=pt[:, 2 * j:2 * j + 2], in_=mrow[:, j * 128:(j + 1) * 128],
                                                  identity=sel[:, :, 0]), reads=[b_mr, b_sel], writes=[b_pt])
        P.op("dve", lambda e: e.tensor_copy(out=modT[:], in_=pt[:, 0:192].rearrange("p (j c) -> p j c", c=2)),
             reads=[b_pt], writes=[b_mod])
        for lo in (16, 64):
            P.op("dve", lambda e, lo=lo: e.tensor_scalar_add(out=modT[:, lo:lo + 16, :], in0=modT[:, lo:lo + 16, :],
                                                             scalar1=1.0), reads=[b_mod], writes=[b_mod])
        P.wait_all_dma()
        with nc.Block() as block:
            P.emit(block)


def ln_rows(P, cx, x_t, b_x, out_t, b_out, scratch):
    st, mv, rstd = scratch["st"], scratch["mv"], scratch["rstd"]
    b_st = scratch["b_st"]
    for c in range(4):
        P.op("dve", lambda e, c=c: e.bn_stats(out=st[:, c, :], in_=x_t[:, c * 512:(c + 1) * 512]),
             reads=[b_x], writes=[b_st])
    P.op("dve", lambda e: e.bn_aggr(out=mv[:], in_=st[:]), reads=[b_st], writes=[b_st])
    P.op("act", lambda e: e.activation(out=rstd[:], in_=mv[:, 1:2], func=AF.Sqrt, bias=LN_EPS, scale=1.0),
         reads=[b_st], writes=[b_st])
    P.op("dve", lambda e: e.reciprocal(out=rstd[:], in_=rstd[:]), reads=[b_st], writes=[b_st])
    P.op("dve", lambda e: e.tensor_scalar(out=out_t[:], in0=x_t[:], scalar1=mv[:, 0:1], scalar2=rstd[:, 0:1],
                                           op0=ALU.subtract, op1=ALU.mult), reads=[b_x, b_st], writes=[b_out])


def stage_ln1(nc, sync, G, hT, tok0, ntiles, cond, dbg=None):
    with ExitStack() as es:
        cx = Ctx(nc, es)
        P = Prog(sync)
        ident = cx.sb([128, 128], BF16, "ident")
        b_id = Buf()
        P.dma("sp", lambda e: e.dma_start(out=ident[:], in_=G["ident_bf"][:, :]), writes=[b_id])
        xt = [cx.sb([128, D], F32, "x") for _ in range(2)]
        b_xt = [Buf(), Buf()]
        xn = [cx.sb([128, D], BF16, "xn") for _ in range(2)]
        b_xn = [Buf(), Buf()]
        scr = {"st": cx.sb([128, 4, 6], F32), "mv": cx.sb([128, 2], F32), "rstd": cx.sb([128, 1], F32), "b_st": Buf()}
        pst = [cx.ps([128, 1024], BF16, "pT") for _ in range(2)]
        b_pst = [Buf(), Buf()]
        modT = G["modT"]
        b_h = Buf()
        for t in range(ntiles):
            sl = t % 2
            c = cond
            P.dma("sp", lambda e, t=t, sl=sl: e.dma_start(
                out=xt[sl][:], in_=G["x"][tok0 + t * 128:tok0 + (t + 1) * 128, :]), writes=[b_xt[sl]])
            ln_rows(P, cx, xt[sl], b_xt[sl], xn[sl], b_xn[sl], scr)
            for g in range(2):
                for j in range(8):
                    kc = g * 8 + j
                    P.op("pe", lambda e, g=g, j=j, kc=kc, sl=sl: e.transpose(
                        out=pst[g][:, j * 128:(j + 1) * 128], in_=xn[sl][:, kc * 128:(kc + 1) * 128],
                        identity=ident[:]), reads=[b_xn[sl], b_id], writes=[b_pst[g]])
                for j in range(8):
                    kc = g * 8 + j
                    P.op("act", lambda e, g=g, j=j, kc=kc, t=t, c=c: e.activation(
                        out=hT[:, kc, t * 128:(t + 1) * 128], in_=pst[g][:, j * 128:(j + 1) * 128],
                        func=AF.Identity, scale=modT[:, 16 + kc, c:c + 1], bias=modT[:, kc, c:c + 1]),
                        reads=[b_pst[g]], writes=[b_h])
        if dbg is not None:
            P.dma(STQ, lambda e: e.dma_start(
                out=dbg.rearrange("(kc p) t -> p kc t", p=128)[:, :, tok0:tok0 + ntiles * 128],
                in_=hT[:, :, 0:ntiles * 128]), reads=[b_h])
        P.wait_all_dma()
        with nc.Block() as block:
            P.emit(block)


class WStream:
    CH = 2

    def __init__(self, P, cx, nbuf=3, nstg=4, cols=512, engines=("pool",)):
        self.P = P
        self.engines = list(engines)
        self.stg = [cx.sb([128, self.CH, cols], F32, "stg") for _ in range(nstg)]
        self.b_stg = [Buf() for _ in range(nstg)]
        self.wb = [cx.sb([128, KC, cols], BF16, "wb") for _ in range(nbuf)]
        self.b_wb = [Buf() for _ in range(nbuf)]
        self.i = 0
        self.j = 0

    def load(self, w_ap, ncols, kcn=KC):
        P = self.P
        CH = self.CH
        sl = self.i % len(self.wb)
        self.i += 1
        wv = w_ap.rearrange("(kc p) n -> p kc n", p=128)
        wb, b_wb = self.wb[sl], self.b_wb[sl]
        for g in range(kcn // CH):
            st = self.j % len(self.stg)
            self.j += 1
            stg, b_stg = self.stg[st], self.b_stg[st]
            P.dma("sp", lambda e, stg=stg, g=g: e.dma_start(out=stg[:, :, 0:ncols], in_=wv[:, g * CH:(g + 1) * CH, :]),
                  writes=[b_stg])
            ce = self.engines[self.j % len(self.engines)]
            if ce == "act":
                P.op("act", lambda e, stg=stg, wb=wb, g=g: e.copy(out=wb[:, g * CH:(g + 1) * CH, 0:ncols],
                                                                  in_=stg[:, :, 0:ncols]), reads=[b_stg], writes=[b_wb])
            else:
                P.op(ce, lambda e, stg=stg, wb=wb, g=g: e.tensor_copy(out=wb[:, g * CH:(g + 1) * CH, 0:ncols],
                                                                      in_=stg[:, :, 0:ncols]),
                     reads=[b_stg], writes=[b_wb])
        return wb, b_wb


def mm_acc(P, ps, b_ps, pairs, reads):
    n = len(pairs)
    for i, (l, r) in enumerate(pairs):
        P.op("pe", lambda e, l=l, r=r, i=i: e.matmul(ps, lhsT=l, rhs=r, start=(i == 0), stop=(i == n - 1)),
             reads=reads, writes=[b_ps])


def stage_ret(nc, sync, G, hT, b_hT_unused, NT, seqs, tok0, rope):
    with ExitStack() as es:
        cx = Ctx(nc, es)
        P = Prog(sync)
        W = WStream(P, cx, nbuf=3, nstg=4, engines=("act", "pool", "act"))
        b_h = Buf()
        NCH = NT // 128
        qT = cx.sb([128, 2, NT], BF16, "qT")
        kT = cx.sb([128, 2, NT], BF16, "kT")
        vt = cx.sb([128, NCH, 512], BF16, "v")
        b_q, b_k, b_v = Buf(), Buf(), Buf()
        Sst = [cx.sb([128, 2, 512], F32, "S") for _ in range(2)]
        Sbf = [cx.sb([128, 2, 512], BF16, "Sbf") for _ in range(2)]
        b_S = [Buf(), Buf()]
        b_Sbf = [Buf(), Buf()]
        sbb = [cx.sb([128, 2, 512], BF16, "sbb") for _ in range(2)]
        b_sbb = [Buf(), Buf()]
        tmp1 = cx.sb([128, 512], F32, "tmp1")
        tmp2 = cx.sb([128, 512], F32, "tmp2")
        b_t1, b_t2 = Buf(), Buf()
        cs = cx.sb([128, 2, 512], F32, "cs")
        b_cs = Buf()
        mask = cx.sb([128, 128], F32, "mask")
        mtmp = cx.sb([128, 128], F32, "mtmp")
        qdr = cx.sb([128, 2, 128], F32, "qdr")
        b_hc = Buf()
        rnw = cx.sb([128, 512], F32, "rnw")
        b_rnw = Buf()
        gate = cx.sb([128, 512], F32, "gate")
        b_gate = Buf()
        ogs = [cx.sb([128, 512], BF16, "og") for _ in range(2)]
        b_ogs = [Buf(), Buf()]
        ogT = [cx.sb([128, 4, 128], BF16, "ogT") for _ in range(2)]
        b_ogT = [Buf(), Buf()]
        ktok = cx.sb([128, 256], BF16, "ktok")
        b_ktok = Buf()
        sm = cx.sb([128, 128], BF16, "sm")
        b_sm = Buf()
        qfb = cx.sb([128, 2, 2, 128], BF16, "qfb")
        b_qfb = Buf()
        scr = {"st": cx.sb([128, 1, 6], F32), "mv": cx.sb([128, 2], F32), "rstd": cx.sb([128, 1], F32), "b_st": Buf()}
        ident = cx.sb([128, 128], BF16, "ident")
        lg = cx.sb([128, 16], F32, "lg")
        cst = cx.sb([128, 6, 128], F32, "cst")
        pcol = cx.sb([128, 3], F32, "pcol")
        dec = cx.sb([128, 8, 4], F32, "dec")
        b_c = Buf()
        P.dma("sp", lambda e: e.dma_start(out=ident[:], in_=G["ident_bf"][:, :]), writes=[b_c])
        P.dma("sp", lambda e: e.dma_start(out=lg[:], in_=G["lg_bc"][:, :]), writes=[b_c])
        P.dma("sp", lambda e: e.dma_start(out=cst[:], in_=G["ret_cst"][:, :, :]), writes=[b_c])
        P.dma("sp", lambda e: e.dma_start(out=pcol[:], in_=G["ret_pcol"][:, :]), writes=[b_c])
        for h in range(8):
            for d in range(2):
                l = lg[:, d * 8 + h:d * 8 + h + 1]
                P.op("act", lambda e, h=h, d=d, l=l: e.activation(out=dec[:, h, d:d + 1], in_=pcol[:, d:d + 1],
                                                                   func=AF.Exp, scale=l), reads=[b_c], writes=[b_c])
                P.op("act", lambda e, h=h, d=d, l=l: e.activation(out=dec[:, h, 2 + d:3 + d], in_=pcol[:, 2:3],
                                                                   func=AF.Exp, scale=l), reads=[b_c], writes=[b_c])
        P.op("dve", lambda e: e.tensor_scalar_mul(out=dec[:, :, 0:2], in0=dec[:, :, 0:2], scalar1=1.0 / 16.0),
             reads=[b_c], writes=[b_c])
        pA = cx.ps([128, 512], F32, "pA")
        pB = cx.ps([128, 512], F32, "pB")
        pO = cx.ps([128, 512], F32, "pO")
        pG = cx.ps([128, 512], F32, "pG")
        pU = [cx.ps([128, 512], F32, "pU") for _ in range(2)]
        pT = cx.ps([128, 1024], BF16, "pT")
        pS = cx.ps([128, 512], F32, "pS")
        b_pA, b_pB, b_pO, b_pG, b_pT, b_pS = Buf(), Buf(), Buf(), Buf(), Buf(), Buf()
        b_pU = [Buf(), Buf()]
        win = G["w_in"]
        sb_scr = G["sb_scr"]
        b_scr = [Buf() for _ in range(16)]

        def ktok_make(h, d, c0):
            for dkb in range(2):
                P.op("pe", lambda e, dkb=dkb: e.transpose(out=pT[:, dkb * 128:(dkb + 1) * 128],
                                                          in_=kT[:, dkb, c0:c0 + 128], identity=ident[:]),
                     reads=[b_k, b_c], writes=[b_pT])
            P.op("dve", lambda e: e.tensor_scalar(out=ktok[:], in0=pT[:, 0:256], scalar1=dec[:, h, d:d + 1],
                                                   scalar2=None, op0=ALU.mult), reads=[b_pT, b_c], writes=[b_ktok])

        def state_update(h, d, n_local):
            for dkb in range(2):
                P.op("pe", lambda e, dkb=dkb: e.matmul(pU[dkb][:, :], lhsT=ktok[:, dkb * 128:(dkb + 1) * 128],
                                                       rhs=vt[:, n_local, :], start=True, stop=True),
                     reads=[b_ktok, b_v], writes=[b_pU[dkb]])
            for dkb in range(2):
                P.op("dve", lambda e, dkb=dkb: e.scalar_tensor_tensor(
                    out=Sst[d][:, dkb, :], in0=Sst[d][:, dkb, :], scalar=dec[:, h, 2 + d:3 + d], in1=pU[dkb][:, :],
                    op0=ALU.mult, op1=ALU.add), reads=[b_pU[dkb], b_c, b_S[d]], writes=[b_S[d]])
            P.op("act", lambda e: e.copy(out=Sbf[d][:], in_=Sst[d][:]), reads=[b_S[d]], writes=[b_Sbf[d]])

        nxt = (W.load(win[:, 0:256], 256), W.load(win[:, 2048:2048 + 256], 256))
        for h in range(8):
            (wq, b_wq), (wk, b_wk) = nxt
            wv, b_wv = W.load(win[:, 4096 + h * 512:4096 + (h + 1) * 512], 512)
            lf = lg[:, h:h + 1]
            lb = lg[:, 8 + h:9 + h]
            P.op("act", lambda e, lf=lf: e.activation(out=mask[:], in_=cst[:, 0, :], func=AF.Exp, scale=lf),
                 reads=[b_c, b_sm], writes=[b_hc])
            P.op("act", lambda e, lb=lb: e.activation(out=mtmp[:], in_=cst[:, 1, :], func=AF.Exp, scale=lb),
                 reads=[b_c], writes=[b_hc])
            P.op("dve", lambda e: e.tensor_tensor(out=mask[:], in0=mask[:], in1=cst[:, 2, :], op=ALU.mult),
                 reads=[b_hc, b_c], writes=[b_hc])
            P.op("dve", lambda e: e.tensor_tensor(out=mtmp[:], in0=mtmp[:], in1=cst[:, 3, :], op=ALU.mult),
                 reads=[b_hc, b_c], writes=[b_hc])
            P.op("dve", lambda e: e.tensor_tensor(out=mask[:], in0=mask[:], in1=mtmp[:], op=ALU.add),
                 reads=[b_hc], writes=[b_hc])
            P.op("act", lambda e, lf=lf: e.activation(out=qdr[:, 0, :], in_=cst[:, 4, :], func=AF.Exp, scale=lf),
                 reads=[b_c, b_qfb], writes=[b_hc])
            P.op("act", lambda e, lb=lb: e.activation(out=qdr[:, 1, :], in_=cst[:, 5, :], func=AF.Exp, scale=lb),
                 reads=[b_c], writes=[b_hc])
            P.dma("sp", lambda e, h=h: e.dma_start(out=rnw[:], in_=G["rnw_bc"][:, h * 512:(h + 1) * 512]),
                  writes=[b_rnw])
            for tt in range(NT // 512):
                tsl = slice(tt * 512, (tt + 1) * 512)
                if rope:
                    P.dma("sp", lambda e, tsl=tsl: e.dma_start(out=cs[:], in_=G["rope_cs"][:, :, tsl]), writes=[b_cs])
                for (wt, b_w, dst, b_dst) in ((wq, b_wq, qT, b_q), (wk, b_wk, kT, b_k)):
                    for dkb, (ps, b_ps) in enumerate(((pA, b_pA), (pB, b_pB))):
                        mm_acc(P, ps[:, :], b_ps,
                               [(wt[:, kc, dkb * 128:(dkb + 1) * 128], hT[:, kc, tsl]) for kc in range(KC)],
                               [b_w, b_h])
                    if rope:
                        P.op("dve", lambda e: e.tensor_tensor(out=tmp1[:], in0=pA[:, :], in1=cs[:, 0, :], op=ALU.mult),
                             reads=[b_pA, b_cs], writes=[b_t1])
                        P.op("dve", lambda e: e.tensor_tensor(out=tmp2[:], in0=pB[:, :], in1=cs[:, 1, :], op=ALU.mult),
                             reads=[b_pB, b_cs], writes=[b_t2])
                        P.op("pool", lambda e, dst=dst, tsl=tsl: e.tensor_tensor(out=dst[:, 0, tsl], in0=tmp1[:],
                                                                                   in1=tmp2[:], op=ALU.subtract),
                             reads=[b_t1, b_t2], writes=[b_dst])
                        P.op("dve", lambda e: e.tensor_tensor(out=tmp1[:], in0=pA[:, :], in1=cs[:, 1, :], op=ALU.mult),
                             reads=[b_pA, b_cs], writes=[b_t1])
                        P.op("dve", lambda e: e.tensor_tensor(out=tmp2[:], in0=pB[:, :], in1=cs[:, 0, :], op=ALU.mult),
                             reads=[b_pB, b_cs], writes=[b_t2])
                        P.op("pool", lambda e, dst=dst, tsl=tsl: e.tensor_tensor(out=dst[:, 1, tsl], in0=tmp1[:],
                                                                                   in1=tmp2[:], op=ALU.add),
                             reads=[b_t1, b_t2], writes=[b_dst])
                    else:
                        P.op("act", lambda e, dst=dst, tsl=tsl: e.copy(out=dst[:, 0, tsl], in_=pA[:, :]),
                             reads=[b_pA], writes=[b_dst])
                        P.op("act", lambda e, dst=dst, tsl=tsl: e.copy(out=dst[:, 1, tsl], in_=pB[:, :]),
                             reads=[b_pB], writes=[b_dst])
            wg, b_wg = W.load(win[:, 8192 + h * 512:8192 + (h + 1) * 512], 512)
            for t in range(NCH):
                mm_acc(P, pO[:, :], b_pO, [(hT[:, kc, t * 128:(t + 1) * 128], wv[:, kc, :]) for kc in range(KC)],
                       [b_wv, b_h])
                P.op("act", lambda e, t=t: e.copy(out=vt[:, t, :], in_=pO[:, :]), reads=[b_pO], writes=[b_v])
            if h < 7:
                nxt = (W.load(win[:, (h + 1) * 256:(h + 2) * 256], 256),
                       W.load(win[:, 2048 + (h + 1) * 256:2048 + (h + 2) * 256], 256))
            for (s0, T, kind, sidx) in seqs:
                N = T // 128
                for d in range(2):
                    if kind == "S":
                        P.dma("sp", lambda e, d=d, h=h: e.dma_start(
                            out=Sst[d][:], in_=G["state_ret"][d, h].rearrange("(b p) v -> p b v", p=128)),
                            writes=[b_S[d]])
                    else:
                        P.op("pool", lambda e, d=d: e.memset(Sst[d][:], 0.0), writes=[b_S[d]])
                    P.op("act", lambda e, d=d: e.copy(out=Sbf[d][:], in_=Sst[d][:]), reads=[b_S[d]], writes=[b_Sbf[d]])
                for n in range(N - 1, -1, -1):
                    c0 = s0 + n * 128
                    P.dma(STQ, lambda e, n=n: e.dma_start(out=sb_scr[n].rearrange("b p v -> p b v"), in_=Sbf[1][:]),
                          reads=[b_Sbf[1]], writes=[b_scr[n]])
                    ktok_make(h, 1, c0)
                    state_update(h, 1, c0 // 128)
                if kind == "P":
                    P.dma(STQ, lambda e, h=h, sidx=sidx: e.dma_start(
                        out=G["new_ret"][sidx, 1, h].rearrange("(b p) v -> p b v", p=128), in_=Sst[1][:]),
                        reads=[b_S[1]])
                pOs, b_pOs = [pO, pA], [b_pO, b_pA]
                pGs, b_pGs = [pG, pB], [b_pG, b_pB]
                gates, b_gates = [gate[:], cs[:, 0, :]], [b_gate, b_cs]
                tmps, b_tmps = [tmp1, tmp2], [b_t1, b_t2]

                def fin_pe(n):
                    c0 = s0 + n * 128
                    sl = n % 2
                    og_, b_og_ = ogs[sl], b_ogs[sl]
                    for fb in range(4):
                        P.op("pe", lambda e, fb=fb: e.transpose(out=pT[:, 256 + fb * 128:256 + (fb + 1) * 128],
                                                                in_=og_[:, fb * 128:(fb + 1) * 128], identity=ident[:]),
                             reads=[b_og_, b_c], writes=[b_pT])
                    o2 = ogT[sl]
                    P.op("act", lambda e: e.copy(out=o2[:].rearrange("p a b -> p (a b)"), in_=pT[:, 256:768]),
                         reads=[b_pT], writes=[b_ogT[sl]])
                    P.dma(STQ, lambda e, h=h: e.dma_start(
                        out=G["og"][h * 512:(h + 1) * 512, tok0 + c0:tok0 + c0 + 128].rearrange("(a p) t -> p a t", p=128),
                        in_=o2[:]), reads=[b_ogT[sl]])

                def ln_chain(n):
                    sl = n % 2
                    pO_, b_pO_ = pOs[sl], b_pOs[sl]
                    gate_, b_gate_ = gates[sl], b_gates[sl]
                    tmp_, b_tmp_ = tmps[sl], b_tmps[sl]
                    og_, b_og_ = ogs[sl], b_ogs[sl]
                    st, mv, rstd, b_st = scr["st"], scr["mv"], scr["rstd"], scr["b_st"]
                    P.op("dve", lambda e, tmp_=tmp_: e.bn_stats(out=st[:, 0, :], in_=tmp_[:]), reads=[b_tmp_], writes=[b_st])
                    P.op("dve", lambda e: e.bn_aggr(out=mv[:], in_=st[:]), reads=[b_st], writes=[b_st])
                    P.op("act", lambda e: e.activation(out=rstd[:], in_=mv[:, 1:2], func=AF.Sqrt, bias=LN_EPS, scale=1.0),
                         reads=[b_st], writes=[b_st])
                    P.op("dve", lambda e: e.reciprocal(out=rstd[:], in_=rstd[:]), reads=[b_st], writes=[b_st])
                    P.op("dve", lambda e, tmp_=tmp_: e.tensor_scalar(
                        out=tmp_[:], in0=tmp_[:], scalar1=mv[:, 0:1], scalar2=rstd[:, 0:1], op0=ALU.subtract,
                        op1=ALU.mult), reads=[b_tmp_, b_st], writes=[b_tmp_])
                    P.op("dve", lambda e, tmp_=tmp_: e.tensor_tensor(out=tmp_[:], in0=tmp_[:], in1=rnw[:], op=ALU.mult),
                         reads=[b_tmp_, b_rnw], writes=[b_tmp_])
                    P.op("dve", lambda e, tmp_=tmp_, gate_=gate_, og_=og_: e.tensor_tensor(
                        out=og_[:], in0=tmp_[:], in1=gate_, op=ALU.mult), reads=[b_tmp_, b_gate_], writes=[b_og_])

                def pre(n):
                    c0 = s0 + n * 128
                    mm_acc(P, pS[:, 0:128], b_pS, [(kT[:, dkb, c0:c0 + 128], qT[:, dkb, c0:c0 + 128]) for dkb in range(2)],
                           [b_q, b_k])
                    P.op("dve", lambda e: e.tensor_tensor(out=sm[:], in0=pS[:, 0:128], in1=mask[:], op=ALU.mult),
                         reads=[b_pS, b_hc], writes=[b_sm])
                    for d in range(2):
                        P.op("dve", lambda e, d=d: e.tensor_tensor(
                            out=qfb[:, d, :, :], in0=qT[:, :, c0:c0 + 128],
                            in1=qdr[:, d, :].unsqueeze(1).broadcast_to([128, 2, 128]), op=ALU.mult),
                            reads=[b_q, b_hc], writes=[b_qfb])

                pre(0)
                for n in range(N):
                    c0 = s0 + n * 128
                    nl = c0 // 128
                    sl = n % 2
                    pO_, b_pO_ = pOs[sl], b_pOs[sl]
                    pG_, b_pG_ = pGs[sl], b_pGs[sl]
                    gate_, b_gate_ = gates[sl], b_gates[sl]
                    P.dma("sp", lambda e, n=n, sl=sl: e.dma_start(out=sbb[sl][:], in_=sb_scr[n].rearrange("b p v -> p b v")),
                          reads=[b_scr[n]], writes=[b_sbb[sl]])
                    pairs = [(sm[:], vt[:, nl, :])]
                    pairs += [(qfb[:, 0, dkb, :], Sbf[0][:, dkb, :]) for dkb in range(2)]
                    pairs += [(qfb[:, 1, dkb, :], sbb[sl][:, dkb, :]) for dkb in range(2)]
                    mm_acc(P, pO_[:, :], b_pO_, pairs, [b_sm, b_v, b_qfb, b_Sbf[0], b_sbb[sl]])
                    tmp_, b_tmp_ = tmps[sl], b_tmps[sl]
                    P.op("act", lambda e, tmp_=tmp_, pO_=pO_: e.copy(out=tmp_[:], in_=pO_[:, :]), reads=[b_pO_],
                         writes=[b_tmp_])
                    ktok_make(h, 0, c0)
                    if n >= 2:
                        fin_pe(n - 2)
                    mm_acc(P, pG_[:, :], b_pG_, [(hT[:, kc, c0:c0 + 128], wg[:, kc, :]) for kc in range(KC)], [b_wg, b_h])
                    P.op("act", lambda e, gate_=gate_, pG_=pG_: e.activation(out=gate_, in_=pG_[:, :], func=AF.Silu),
                         reads=[b_pG_], writes=[b_gate_])
                    state_update(h, 0, nl)
                    if n + 1 < N:
                        pre(n + 1)
                    if n > 0:
                        ln_chain(n - 1)
                ln_chain(N - 1)
                if N >= 2:
                    fin_pe(N - 2)
                fin_pe(N - 1)
                if kind == "P":
                    P.dma(STQ, lambda e, h=h, sidx=sidx: e.dma_start(
                        out=G["new_ret"][sidx, 0, h].rearrange("(b p) v -> p b v", p=128), in_=Sst[0][:]),
                        reads=[b_S[0]])
        P.wait_all_dma()
        with nc.Block() as block:
            P.emit(block)


L2_EPS = 1e-6
RMS_EPS = 1e-6


def stage_gates(nc, sync, G, hT, NT, tok0):
    with ExitStack() as es:
        cx = Ctx(nc, es)
        P = Prog(sync)
        W = WStream(P, cx, nbuf=3, nstg=4, engines=("dve", "dve", "pool"))
        b_h = Buf()
        ps = [cx.ps([128, 512], F32, "pg") for _ in range(4)]
        b_ps = [Buf() for _ in range(4)]
        TT = min(NT, 512)
        sgt = [cx.sb([128, 4, NT], BF16, "sgt") for _ in range(2)]
        b_sgt = [Buf(), Buf()]
        for blk in range(8):
            w, b_w = W.load(G["w_in"][:, 20544 + blk * 512:20544 + (blk + 1) * 512], 512)
            o2, b_o2 = sgt[blk % 2], b_sgt[blk % 2]
            for tt in range(NT // TT):
                tsl = slice(tt * TT, (tt + 1) * TT)
                for cb in range(4):
                    mm_acc(P, ps[cb][:, 0:TT], b_ps[cb],
                           [(w[:, kc, cb * 128:(cb + 1) * 128], hT[:, kc, tsl]) for kc in range(KC)], [b_w, b_h])
                    P.op("act", lambda e, cb=cb, o2=o2, tsl=tsl: e.activation(out=o2[:, cb, tsl], in_=ps[cb][:, 0:TT],
                                                                               func=AF.Sigmoid),
                         reads=[b_ps[cb]], writes=[b_o2])
            P.dma(STQ, lambda e, blk=blk, o2=o2: e.dma_start(
                out=G["sg"][blk * 512:(blk + 1) * 512, tok0:tok0 + NT].rearrange("(a p) t -> p a t", p=128), in_=o2[:]),
                reads=[b_o2])
        P.wait_all_dma()
        with nc.Block() as block:
            P.emit(block)


def stage_dnproj(nc, sync, G, hT, NT, seqs, tok0):
    with ExitStack() as es:
        cx = Ctx(nc, es)
        P = Prog(sync)
        W = WStream(P, cx, nbuf=3, nstg=4, engines=("pool", "act"))
        b_h = Buf()
        pA = [cx.ps([128, 512], F32, "pA") for _ in range(4)]
        b_pA = [Buf() for _ in range(4)]
        pNs = [cx.ps([128, 512], F32, "pN") for _ in range(2)]
        b_pNs = [Buf(), Buf()]
        pZ = cx.ps([128, 512], F32, "pZ")
        b_pZ = Buf()
        ys = [cx.sb([128, NT + 2], F32, "y") for _ in range(2)]
        zs_ = [cx.sb([128, NT], F32, "z") for _ in range(2)]
        sqs = [cx.sb([128, NT], BF16, "sq") for _ in range(2)]
        rins = [cx.sb([128, NT], F32, "rin") for _ in range(2)]
        b_rins = [Buf(), Buf()]
        xo = [cx.sb([128, NT], BF16, "xo") for _ in range(2)]
        b_ys, b_zs, b_sqs = [Buf(), Buf()], [Buf(), Buf()], [Buf(), Buf()]
        b_xo = [Buf(), Buf()]
        ones = cx.sb([128, 128], BF16, "ones")
        cw = cx.sb([128, 48, 3], F32, "cw")
        b_c = Buf()
        P.dma("sp", lambda e: e.dma_start(out=ones[:], in_=G["ones_bf"][:, :]), writes=[b_c])
        P.dma("sp", lambda e: e.dma_start(out=cw[:], in_=G["dn_cwT"][:, :, :]), writes=[b_c])
        nx = 0
        pi = [0]
        for X in range(3):
            for gi in range(4):
                w, b_w = W.load(G["w_in"][:, 12288 + X * 2048 + gi * 512:12288 + X * 2048 + (gi + 1) * 512], 512)
                for hh in range(4):
                    H = gi * 4 + hh
                    blk = X * 16 + H
                    o2, b_o2 = xo[nx % 2], b_xo[nx % 2]
                    y, z, sq = ys[nx % 2], zs_[nx % 2], sqs[nx % 2]
                    b_y, b_z, b_sq = b_ys[nx % 2], b_zs[nx % 2], b_sqs[nx % 2]
                    nx += 1
                    for (s0, T, kind, sidx) in seqs:
                        TT = min(T, 512)
                        for tt in range(T // TT):
                            a = s0 + tt * TT
                            ps, b_ps = pA[pi[0] % 4], b_pA[pi[0] % 4]
                            pi[0] += 1
                            mm_acc(P, ps[:, 0:TT], b_ps,
                                   [(w[:, kc, hh * 128:(hh + 1) * 128], hT[:, kc, a:a + TT]) for kc in range(KC)],
                                   [b_w, b_h])
                            P.op("act", lambda e, ps=ps, a=a, TT=TT, y=y: e.copy(out=y[:, 1 + a:1 + a + TT], in_=ps[:, 0:TT]),
                                 reads=[b_ps], writes=[b_y])
                        zs = z[:, s0:s0 + T]
                        P.op("dve", lambda e, zs=zs, s0=s0, T=T, blk=blk, y=y: e.tensor_scalar(
                            out=zs, in0=y[:, 1 + s0:1 + s0 + T], scalar1=cw[:, blk, 1:2], scalar2=None, op0=ALU.mult),
                            reads=[b_y, b_c], writes=[b_z])
                        P.op("dve", lambda e, s0=s0, T=T, blk=blk, y=y, z=z: e.scalar_tensor_tensor(
                            out=z[:, s0 + 1:s0 + T], in0=y[:, 1 + s0:s0 + T], scalar=cw[:, blk, 0:1],
                            in1=z[:, s0 + 1:s0 + T], op0=ALU.mult, op1=ALU.add), reads=[b_y, b_c, b_z], writes=[b_z])
                        P.op("dve", lambda e, s0=s0, T=T, blk=blk, y=y, z=z: e.scalar_tensor_tensor(
                            out=z[:, s0:s0 + T - 1], in0=y[:, 2 + s0:1 + s0 + T], scalar=cw[:, blk, 2:3],
                            in1=z[:, s0:s0 + T - 1], op0=ALU.mult, op1=ALU.add), reads=[b_y, b_c, b_z], writes=[b_z])
                    P.op("act", lambda e, z=z: e.activation(out=z[:], in_=z[:], func=AF.Silu), reads=[b_z], writes=[b_z])
                    if X == 2:
                        P.op("act", lambda e, o2=o2, z=z: e.copy(out=o2[:], in_=z[:]), reads=[b_z], writes=[b_o2])
                    else:
                        P.op("act", lambda e, z=z, sq=sq: e.activation(out=sq[:], in_=z[:], func=AF.Square), reads=[b_z],
                             writes=[b_sq])
                        TT = min(NT, 512)
                        rin, b_rin = rins[nx % 2], b_rins[nx % 2]
                        for tt in range(NT // TT):
                            tsl = slice(tt * TT, (tt + 1) * TT)
                            pN_, b_pN_ = pNs[tt % 2], b_pNs[tt % 2]
                            P.op("pe", lambda e, tsl=tsl, TT=TT, sq=sq, pN_=pN_: e.matmul(
                                pN_[:, 0:TT], lhsT=ones[:], rhs=sq[:, tsl], start=True, stop=True),
                                reads=[b_sq, b_c], writes=[b_pN_])
                            P.op("act", lambda e, tsl=tsl, TT=TT, pN_=pN_, rin=rin: e.activation(
                                out=rin[:, tsl], in_=pN_[:, 0:TT], func=AF.Ln, bias=L2_EPS, scale=1.0), reads=[b_pN_],
                                writes=[b_rin])
                        P.op("act", lambda e, rin=rin: e.activation(out=rin[:], in_=rin[:], func=AF.Exp, scale=-0.5),
                             reads=[b_rin], writes=[b_rin])
                        sc = (128.0 ** -0.5) if X == 0 else 1.0
                        P.op("dve", lambda e, o2=o2, sc=sc, z=z, rin=rin: e.scalar_tensor_tensor(
                            out=o2[:], in0=z[:], scalar=sc, in1=rin[:], op0=ALU.mult, op1=ALU.mult),
                            reads=[b_z, b_rin], writes=[b_o2])
                    P.dma(STQ, lambda e, X=X, H=H, o2=o2: e.dma_start(out=G["dqkv"][X, H, :, tok0:tok0 + NT], in_=o2[:]),
                          reads=[b_o2])
        gz = [cx.sb([128, 512], BF16, "gz") for _ in range(2)]
        b_gz = [Buf(), Buf()]
        for gi in range(4):
            w, b_w = W.load(G["w_in"][:, 18432 + gi * 512:18432 + (gi + 1) * 512], 512)
            for t in range(NT // 128):
                sl = t % 2
                mm_acc(P, pZ[:, :], b_pZ, [(hT[:, kc, t * 128:(t + 1) * 128], w[:, kc, :]) for kc in range(KC)],
                       [b_w, b_h])
                P.op("act", lambda e, sl=sl: e.activation(out=gz[sl][:], in_=pZ[:, :], func=AF.Silu), reads=[b_pZ],
                     writes=[b_gz[sl]])
                P.dma(STQ, lambda e, sl=sl, t=t, gi=gi: e.dma_start(
                    out=G["dzg"][tok0 + t * 128:tok0 + (t + 1) * 128, gi * 512:(gi + 1) * 512], in_=gz[sl][:]),
                    reads=[b_gz[sl]])
        w, b_w = W.load(G["w_in"][:, 20480:20544], 64)
        NCH = NT // 64
        bg = cx.sb([64, NCH, 64], F32, "bg")
        b_bg = Buf()
        ab = cx.sb([64, 2, 32], F32, "ab")
        P.dma("sp", lambda e: e.dma_start(out=ab[:], in_=G["dn_ab"][:, :, :]), writes=[b_c])
        P.op("act", lambda e: e.activation(out=ab[:, 0, :], in_=ab[:, 0, :], func=AF.Exp), reads=[b_c], writes=[b_c])
        P.op("dve", lambda e: e.tensor_scalar_mul(out=ab[:, 0, :], in0=ab[:, 0, :], scalar1=-1.0), reads=[b_c],
             writes=[b_c])
        for cc in range(NCH):
            mm_acc(P, pZ[0:64, 0:64], b_pZ, [(hT[:, kc, cc * 64:(cc + 1) * 64], w[:, kc, 0:64]) for kc in range(KC)],
                   [b_w, b_h])
            P.op("act", lambda e, cc=cc: e.activation(out=bg[:, cc, 0:32], in_=pZ[0:64, 0:32], func=AF.Sigmoid),
                 reads=[b_pZ], writes=[b_bg])
            P.op("dve", lambda e, cc=cc: e.tensor_tensor(out=bg[:, cc, 32:64], in0=pZ[0:64, 32:64], in1=ab[:, 1, :],
                                                         op=ALU.add), reads=[b_pZ, b_c], writes=[b_bg])
        P.op("act", lambda e: e.activation(out=bg[:, :, 32:64], in_=bg[:, :, 32:64], func=AF.Exp), reads=[b_bg],
             writes=[b_bg])
        P.op("act", lambda e: e.activation(out=bg[:, :, 32:64], in_=bg[:, :, 32:64], func=AF.Ln, bias=1.0, scale=1.0),
             reads=[b_bg], writes=[b_bg])
        P.op("dve", lambda e: e.tensor_tensor(out=bg[:, :, 32:64], in0=bg[:, :, 32:64],
                                              in1=ab[:, 0, :].unsqueeze(1).broadcast_to([64, NCH, 32]), op=ALU.mult),
             reads=[b_bg, b_c], writes=[b_bg])
        P.dma(STQ, lambda e: e.dma_start(out=G["dbg"][:, tok0 // 64:tok0 // 64 + NCH, :], in_=bg[:]), reads=[b_bg])
        P.wait_all_dma()
        with nc.Block() as block:
            P.emit(block)


class _Ch:
    pass


DN_SEQ = False
DN_ALT = False


def stage_dnrec(nc, sync, G, NT, seqs, tok0, ngroups=4):
    with ExitStack() as es:
        cx = Ctx(nc, es)
        P = Prog(sync)
        NCH = NT // 64
        c0 = tok0 // 64
        cst = cx.sb([64, 7, 64], F32, "cst")
        id4 = cx.sb([64, 4, 64], F32, "id4")
        ones = cx.sb([64, 128], F32, "ones")
        identb = cx.sb([128, 128], BF16, "identb")
        dnw = cx.sb([64, 512], F32, "dnw")
        b_dnw = Buf()
        b_c = Buf()
        P.dma("sp", lambda e: e.dma_start(out=cst[:], in_=G["dn_cst"][:, :, :]), writes=[b_c])
        P.dma("sp", lambda e: e.dma_start(out=identb[:], in_=G["ident_bf"][:, :]), writes=[b_c])
        P.op("pool", lambda e: e.memset(ones[:], 1.0), writes=[b_c])
        for hh in range(4):
            P.op("pool", lambda e, hh=hh: e.tensor_copy(out=id4[:, hh, :], in_=cst[:, 0, :]), reads=[b_c], writes=[b_c])
        pb = [cx.ps([128, 512], F32, "pb") for _ in range(7)]
        pTr = cx.ps([128, 1024], BF16, "pTr")
        b_pb = [Buf() for _ in range(7)]
        b_pTr = Buf()
        bg = cx.sb([64, NCH, 64], F32, "bg")
        gcs = cx.sb([64, NCH, 32], F32, "gcs")
        egc = cx.sb([64, NCH, 32], F32, "egc")
        ekl = cx.sb([64, NCH, 32], F32, "ekl")
        bege = cx.sb([64, NCH, 32], F32, "bege")
        eglS = cx.sb([128, NCH, 32], F32, "eglS")
        b_sc = Buf()
        P.dma("sp", lambda e: e.dma_start(out=bg[:], in_=G["dbg"][:, c0:c0 + NCH, :]), writes=[b_sc])
        CB = 16
        for q0 in range(0, NCH, CB):
            nq = min(CB, NCH - q0)
            for d in range(2):
                P.op("pe", lambda e, d=d, q0=q0, nq=nq: e.matmul(
                    pb[6][0:64, 0:nq * 16].rearrange("p (c n) -> p c n", n=16), lhsT=cst[:, 1 + d, :],
                    rhs=bg[:, q0:q0 + nq, 32 + d * 16:48 + d * 16], start=True, stop=True),
                    reads=[b_sc, b_c], writes=[b_pb[6]])
                P.op("act", lambda e, d=d, q0=q0, nq=nq: e.copy(
                    out=gcs[:, q0:q0 + nq, d * 16:(d + 1) * 16],
                    in_=pb[6][0:64, 0:nq * 16].rearrange("p (c n) -> p c n", n=16)), reads=[b_pb[6]], writes=[b_sc])
            P.op("pe", lambda e, q0=q0, nq=nq: e.matmul(
                pb[0][0:64, 0:nq * 32].rearrange("p (c n) -> p c n", n=32), lhsT=ones[:, 0:64],
                rhs=bg[:, q0:q0 + nq, 32:64], start=True, stop=True), reads=[b_sc, b_c], writes=[b_pb[0]])
            P.op("dve", lambda e, q0=q0, nq=nq: e.tensor_tensor(
                out=ekl[:, q0:q0 + nq, :], in0=pb[0][0:64, 0:nq * 32].rearrange("p (c n) -> p c n", n=32),
                in1=gcs[:, q0:q0 + nq, :], op=ALU.subtract), reads=[b_pb[0], b_sc], writes=[b_sc])
            P.op("pe", lambda e, q0=q0, nq=nq: e.matmul(
                pb[1][:, 0:nq * 32].rearrange("p (c n) -> p c n", n=32), lhsT=ones[:, :],
                rhs=bg[:, q0:q0 + nq, 32:64], start=True, stop=True), reads=[b_sc, b_c], writes=[b_pb[1]])
            P.op("act", lambda e, q0=q0, nq=nq: e.activation(
                out=eglS[:, q0:q0 + nq, :], in_=pb[1][:, 0:nq * 32].rearrange("p (c n) -> p c n", n=32), func=AF.Exp),
                reads=[b_pb[1]], writes=[b_sc])
        P.op("act", lambda e: e.activation(out=ekl[:], in_=ekl[:], func=AF.Exp), reads=[b_sc], writes=[b_sc])
        P.op("act", lambda e: e.activation(out=egc[:], in_=gcs[:], func=AF.Exp), reads=[b_sc], writes=[b_sc])
        P.op("dve", lambda e: e.tensor_tensor(out=bege[:], in0=bg[:, :, 0:32], in1=egc[:], op=ALU.mult), reads=[b_sc],
             writes=[b_sc])
        qkv = cx.sb([128, 3, 4, NT], BF16, "qkv")
        b_qkv = Buf()
        of = cx.sb([64, NCH, 512], F32, "of")
        b_of = [Buf() for _ in range(NCH)]

        def bc(ap2, n):
            return ap2.unsqueeze(2).broadcast_to([ap2.shape[0], 4, n])

        def mk_chain(i):
            ch = _Ch()
            ch.Y = [pb[3 * i + k] for k in range(3)]
            ch.bY = [b_pb[3 * i + k] for k in range(3)]
            if DN_ALT:
                ch.Y = [pb[0], pb[1], pb[2]]
                ch.bY = [b_pb[0], b_pb[1], b_pb[2]]
                ch.YB, ch.bYB, ch.oB = pb[3], b_pb[3], 0
                ch.YQ, ch.bYQ, ch.oQ = pb[4], b_pb[4], 0
            else:
                ch.YB, ch.bYB, ch.oB = ch.Y[2], ch.bY[2], 256
                ch.YQ, ch.bYQ, ch.oQ = ch.Y[0], ch.bY[0], 0

            def t64(name, dt=F32, n=64):
                return cx.sb([64, 4, n], dt, name + str(i)), Buf()

            ch.S = cx.sb([128, 4, 128], F32, "S%d" % i)
            ch.Sb = cx.sb([128, 4, 128], BF16, "Sb%d" % i)
            ch.b_S, ch.b_Sb = Buf(), Buf()
            ch.dg, ch.b_dg = t64("dg")
            ch.ndg, ch.b_ndg = t64("ndg")
            ch.E1, ch.b_E1 = t64("E1")
            ch.E2, ch.b_E2 = t64("E2")
            ch.AB = [cx.sb([64, 8, 64], F32, "AB%d_%d" % (k, i)) for k in range(2)]
            ch.bAB = [Buf(), Buf()]
            ch.Lm = [(ch.AB[k][:, 0:4, :], ch.bAB[k]) for k in range(2)]
            ch.Bm = [(ch.AB[k][:, 4:8, :], ch.bAB[k]) for k in range(2)]
            ch.Q, ch.b_Q = t64("Q")
            ch.Qb, ch.b_Qb = t64("Qb", BF16)
            ch.at, ch.b_at = t64("at", BF16)
            ch.kbg, ch.b_kbg = t64("kbg", BF16, 128)
            ch.kg, ch.b_kg = t64("kg", BF16, 128)
            ch.vb, ch.b_vb = t64("vb", BF16, 128)
            ch.u, ch.b_u = t64("u", F32, 128)
            ch.vn, ch.b_vn = t64("vn", BF16, 128)
            ch.ot, ch.b_ot = t64("ot", F32, 128)
            ch.o2, ch.b_o2 = t64("o2", F32, 128)
            ch.wT = cx.sb([128, 4, 64], BF16, "wT%d" % i)
            ch.b_wT = Buf()
            ch.St = cx.sb([128, 4, 128], F32, "St%d" % i)
            ch.b_St = Buf()
            ch.ss = cx.sb([64, 4], F32, "ss%d" % i)
            ch.b_ss = Buf()
            ch.gzt = cx.sb([64, 512], BF16, "gzt%d" % i)
            ch.b_gzt = Buf()
            ch.ogd = cx.sb([64, 512], BF16, "ogd%d" % i)
            ch.b_ogd = Buf()
            ch.ogT = cx.sb([128, 4, 64], BF16, "ogT%d" % i)
            ch.b_ogT = Buf()
            return ch

        chains = [mk_chain(0), mk_chain(1)]

        def step(ch, gi, d, s0, NC, s):
            Y, bY = ch.Y, ch.bY
            cols = slice(d * 16 + gi * 4, d * 16 + gi * 4 + 4)
            c = s if d == 0 else NC - 1 - s
            finalize = (c >= NC // 2) if d == 0 else (c < NC // 2)
            t0 = s0 + c * 64
            cc = t0 // 64
            kc_ = qkv[:, 1, :, t0:t0 + 64]
            qc_ = qkv[:, 0, :, t0:t0 + 64]
            vc_ = qkv[:, 2, :, t0:t0 + 64]
            h64 = lambda ap: ap.rearrange("p (h n) -> p h n", n=64)
            h128 = lambda ap: ap.rearrange("p (h n) -> p h n", n=128)
            for hh in range(4):
                P.op("pe", lambda e, hh=hh: e.matmul(Y[0][0:64, hh * 128:hh * 128 + 64], lhsT=kc_[:, hh, :],
                                                     rhs=kc_[:, hh, :], start=True, stop=True),
                     reads=[b_qkv], writes=[bY[0]])
                P.op("pe", lambda e, hh=hh: e.matmul(Y[0][0:64, hh * 128 + 64:hh * 128 + 128], lhsT=kc_[:, hh, :],
                                                     rhs=qc_[:, hh, :], start=True, stop=True),
                     reads=[b_qkv], writes=[bY[0]])
            GA = h128(Y[0][0:64, :])
            P.op("dve", lambda e: e.tensor_tensor(out=ch.dg[:], in0=id4[:], in1=bc(gcs[:, cc, cols], 64), op=ALU.mult),
                 reads=[b_c, b_sc], writes=[ch.b_dg])
            for hh in range(4):
                P.op("pe", lambda e, hh=hh: e.matmul(Y[1][0:64, hh * 64:(hh + 1) * 64], lhsT=ones[:, 0:64],
                                                     rhs=ch.dg[:, hh, :], start=True, stop=True),
                     reads=[ch.b_dg, b_c], writes=[bY[1]])
            yield
            Dm = h64(Y[1][0:64, 0:256])
            P.op("dve", lambda e: e.tensor_tensor(
                out=ch.E1[:], in0=cst[:, 3 + d, :].unsqueeze(1).broadcast_to([64, 4, 64]), in1=Dm, op=ALU.subtract),
                reads=[bY[1], b_c], writes=[ch.b_E1])
            P.op("dve", lambda e: e.tensor_tensor(
                out=ch.E2[:], in0=Dm, in1=cst[:, 5 + d, :].unsqueeze(1).broadcast_to([64, 4, 64]), op=ALU.add),
                reads=[bY[1], b_c], writes=[ch.b_E2])
            P.op("dve", lambda e: e.tensor_tensor(out=ch.E1[:], in0=ch.E1[:], in1=bc(gcs[:, cc, cols], 64), op=ALU.add),
                 reads=[ch.b_E1, b_sc], writes=[ch.b_E1])
            P.op("dve", lambda e: e.tensor_tensor(out=ch.E2[:], in0=ch.E2[:], in1=bc(gcs[:, cc, cols], 64),
                                                  op=ALU.subtract), reads=[ch.b_E2, b_sc], writes=[ch.b_E2])
            P.op("act", lambda e: e.activation(out=ch.E1[:], in_=ch.E1[:], func=AF.Exp), reads=[ch.b_E1],
                 writes=[ch.b_E1])
            P.op("act", lambda e: e.activation(out=ch.E2[:], in_=ch.E2[:], func=AF.Exp), reads=[ch.b_E2],
                 writes=[ch.b_E2])
            yield
            A0, b_A0 = ch.Lm[0]
            P.op("dve", lambda e: e.tensor_tensor(out=A0, in0=GA[:, :, 0:64], in1=ch.E1[:], op=ALU.mult),
                 reads=[bY[0], ch.b_E1], writes=[b_A0])
            P.op("dve", lambda e: e.tensor_tensor(out=A0, in0=A0, in1=bc(bg[:, cc, cols], 64), op=ALU.mult),
                 reads=[b_sc, b_A0], writes=[b_A0])
            P.op("dve", lambda e: e.tensor_tensor(out=ch.at[:], in0=GA[:, :, 64:128], in1=ch.E2[:], op=ALU.mult),
                 reads=[bY[0], ch.b_E2], writes=[ch.b_at])
            B0, b_B0 = ch.Bm[0]
            for hh in range(4):
                P.op("pe", lambda e, hh=hh: e.transpose(out=Y[1][0:64, hh * 64:(hh + 1) * 64], in_=A0[:, hh, :],
                                                        identity=cst[:, 0, :]), reads=[b_A0, b_c], writes=[bY[1]])
            yield
            for hh in range(4):
                P.op("pe", lambda e, hh=hh: e.transpose(out=pTr[0:64, hh * 128:(hh + 1) * 128], in_=kc_[:, hh, :],
                                                        identity=identb[:]), reads=[b_qkv, b_c], writes=[b_pTr])
                P.op("pe", lambda e, hh=hh: e.transpose(out=pTr[0:64, 512 + hh * 128:512 + (hh + 1) * 128],
                                                        in_=vc_[:, hh, :], identity=identb[:]),
                     reads=[b_qkv, b_c], writes=[b_pTr])
            P.op("act", lambda e: e.copy(out=B0, in_=Dm), reads=[bY[1]], writes=[b_B0])
            P.op("dve", lambda e: e.tensor_tensor(out=ch.Q[:], in0=id4[:], in1=B0, op=ALU.subtract),
                 reads=[b_c, b_B0], writes=[ch.b_Q])
            ktr = h128(pTr[0:64, 0:512])
            vtr = h128(pTr[0:64, 512:1024])
            P.op("dve", lambda e: e.tensor_tensor(out=ch.kbg[:], in0=ktr, in1=bc(bege[:, cc, cols], 128), op=ALU.mult),
                 reads=[b_pTr, b_sc], writes=[ch.b_kbg])
            P.op("dve", lambda e: e.tensor_tensor(out=ch.kg[:], in0=ktr, in1=bc(ekl[:, cc, cols], 128), op=ALU.mult),
                 reads=[b_pTr, b_sc], writes=[ch.b_kg])
            P.op("dve", lambda e: e.tensor_tensor(out=ch.vb[:], in0=vtr, in1=bc(bg[:, cc, cols], 128), op=ALU.mult),
                 reads=[b_pTr, b_sc], writes=[ch.b_vb])
            yield

            def qupd(Ak, b_Ak):
                for hh in range(4):
                    P.op("pe", lambda e, hh=hh: e.matmul(ch.YQ[0:64, ch.oQ + hh * 64:ch.oQ + (hh + 1) * 64],
                                                         lhsT=Ak[:, hh, :], rhs=ch.Q[:, hh, :], start=True, stop=True),
                         reads=[b_Ak, ch.b_Q], writes=[ch.bYQ])

            def qadd():
                P.op("dve", lambda e: e.tensor_tensor(out=ch.Q[:], in0=ch.Q[:],
                                                      in1=h64(ch.YQ[0:64, ch.oQ:ch.oQ + 256]), op=ALU.add),
                     reads=[ch.bYQ, ch.b_Q], writes=[ch.b_Q])

            cur = 0
            for lvl in range(1, 6):
                A, b_A = ch.Lm[cur]
                B, b_B = ch.Bm[cur]
                An, b_An = ch.Lm[1 - cur]
                Bn, b_Bn = ch.Bm[1 - cur]
                for hh in range(4):
                    P.op("pe", lambda e, hh=hh, A=A, B=B: e.matmul(Y[2][0:64, hh * 64:(hh + 1) * 64], lhsT=B[:, hh, :],
                                                                   rhs=A[:, hh, :], start=True, stop=True),
                         reads=[b_A, b_B], writes=[bY[2]])
                if lvl < 5:
                    for hh in range(4):
                        P.op("pe", lambda e, hh=hh, A=A, B=B: e.matmul(
                            ch.YB[0:64, ch.oB + hh * 64:ch.oB + (hh + 1) * 64], lhsT=A[:, hh, :], rhs=B[:, hh, :],
                            start=True, stop=True), reads=[b_A, b_B], writes=[ch.bYB])
                if lvl >= 2:
                    qupd(A, b_A)
                yield
                if lvl < 5 and not DN_ALT:
                    ABn = ch.AB[1 - cur]
                    P.op("act", lambda e, ABn=ABn: e.copy(out=ABn[:], in_=h64(Y[2][0:64, 0:512])), reads=[bY[2]],
                         writes=[b_An])
                else:
                    P.op("act", lambda e, An=An: e.copy(out=An, in_=h64(Y[2][0:64, 0:256])), reads=[bY[2]],
                         writes=[b_An])
                    if lvl < 5:
                        P.op("dve", lambda e, Bn=Bn: e.tensor_copy(out=Bn, in_=h64(ch.YB[0:64, ch.oB:ch.oB + 256])),
                             reads=[ch.bYB], writes=[b_Bn])
                if lvl >= 2:
                    qadd()
                yield
                cur = 1 - cur
            A, b_A = ch.Lm[cur]
            qupd(A, b_A)
            yield
            qadd()
            P.op("act", lambda e: e.copy(out=ch.Qb[:], in_=ch.Q[:]), reads=[ch.b_Q], writes=[ch.b_Qb])
            yield
            for hh in range(4):
                P.op("pe", lambda e, hh=hh: e.matmul(Y[0][0:64, hh * 128:(hh + 1) * 128], lhsT=ch.Qb[:, hh, :],
                                                     rhs=ch.vb[:, hh, :], start=True, stop=True),
                     reads=[ch.b_Qb, ch.b_vb], writes=[bY[0]])
                P.op("pe", lambda e, hh=hh: e.matmul(Y[1][:, hh * 64:(hh + 1) * 64], lhsT=ch.kbg[:, hh, :],
                                                     rhs=ch.Qb[:, hh, :], start=True, stop=True),
                     reads=[ch.b_Qb, ch.b_kbg], writes=[bY[1]])
            yield
            P.op("act", lambda e: e.copy(out=ch.u[:], in_=h128(Y[0][0:64, :])), reads=[bY[0]], writes=[ch.b_u])
            P.op("act", lambda e: e.copy(out=ch.wT[:], in_=h64(Y[1][:, 0:256])), reads=[bY[1]], writes=[ch.b_wT])
            yield
            for hh in range(4):
                P.op("pe", lambda e, hh=hh: e.matmul(Y[0][0:64, hh * 128:(hh + 1) * 128], lhsT=ch.wT[:, hh, :],
                                                     rhs=ch.Sb[:, hh, :], start=True, stop=True),
                     reads=[ch.b_wT, ch.b_Sb], writes=[bY[0]])
            for hh in range(4):
                P.op("pe", lambda e, hh=hh: e.matmul(Y[2][0:64, hh * 128:(hh + 1) * 128], lhsT=qc_[:, hh, :],
                                                     rhs=ch.Sb[:, hh, :], start=True, stop=True),
                     reads=[b_qkv, ch.b_Sb], writes=[bY[2]])
            yield
            P.op("dve", lambda e: e.tensor_tensor(out=ch.vn[:], in0=ch.u[:], in1=h128(Y[0][0:64, :]), op=ALU.subtract),
                 reads=[ch.b_u, bY[0]], writes=[ch.b_vn])
            P.op("dve", lambda e: e.tensor_tensor(out=ch.ot[:], in0=h128(Y[2][0:64, :]), in1=bc(egc[:, cc, cols], 128),
                                                  op=ALU.mult), reads=[bY[2], b_sc], writes=[ch.b_ot])
            yield
            for hh in range(4):
                P.op("pe", lambda e, hh=hh: e.matmul(Y[1][0:64, hh * 128:(hh + 1) * 128], lhsT=ch.at[:, hh, :],
                                                     rhs=ch.vn[:, hh, :], start=True, stop=True),
                     reads=[ch.b_at, ch.b_vn], writes=[bY[1]])
                P.op("pe", lambda e, hh=hh: e.matmul(Y[2][:, hh * 128:(hh + 1) * 128], lhsT=ch.kg[:, hh, :],
                                                     rhs=ch.vn[:, hh, :], start=True, stop=True),
                     reads=[ch.b_kg, ch.b_vn], writes=[bY[2]])
            P.op("dve", lambda e: e.tensor_tensor(out=ch.St[:], in0=ch.S[:],
                                                  in1=eglS[:, cc, cols].unsqueeze(2).broadcast_to([128, 4, 128]),
                                                  op=ALU.mult), reads=[ch.b_S, b_sc], writes=[ch.b_St])
            yield
            P.op("dve", lambda e: e.tensor_tensor(out=ch.S[:], in0=ch.St[:], in1=h128(Y[2][:, :]), op=ALU.add),
                 reads=[ch.b_St, bY[2]], writes=[ch.b_S])
            P.op("act", lambda e: e.copy(out=ch.Sb[:], in_=ch.S[:]), reads=[ch.b_S], writes=[ch.b_Sb])
            ofc = h128(of[:, cc, :])
            if not finalize:
                P.op("dve", lambda e: e.tensor_tensor(out=ofc, in0=ch.ot[:], in1=h128(Y[1][0:64, :]), op=ALU.add),
                     reads=[ch.b_ot, bY[1]], writes=[b_of[cc]])
                yield
            else:
                P.dma("sp", lambda e: e.dma_start(out=ch.gzt[:],
                                                  in_=G["dzg"][tok0 + t0:tok0 + t0 + 64, gi * 512:(gi + 1) * 512]),
                      writes=[ch.b_gzt])
                P.op("dve", lambda e: e.tensor_tensor(out=ch.ot[:], in0=ch.ot[:], in1=h128(Y[1][0:64, :]), op=ALU.add),
                     reads=[ch.b_ot, bY[1]], writes=[ch.b_ot])
                P.op("dve", lambda e: e.tensor_tensor(out=ch.o2[:], in0=ch.ot[:], in1=ofc, op=ALU.add),
                     reads=[ch.b_ot, b_of[cc]], writes=[ch.b_o2])
                yield
                P.op("dve", lambda e: e.tensor_tensor(out=ch.ot[:], in0=ch.o2[:], in1=ch.o2[:], op=ALU.mult),
                     reads=[ch.b_o2], writes=[ch.b_ot])
                P.op("dve", lambda e: e.reduce_sum(out=ch.ss[:], in_=ch.ot[:], axis=AX.X), reads=[ch.b_ot],
                     writes=[ch.b_ss])
                P.op("act", lambda e: e.activation(out=ch.ss[:], in_=ch.ss[:], func=AF.Sqrt, bias=RMS_EPS,
                                                   scale=1.0 / 128.0), reads=[ch.b_ss], writes=[ch.b_ss])
                yield
                P.op("dve", lambda e: e.reciprocal(out=ch.ss[:], in_=ch.ss[:]), reads=[ch.b_ss], writes=[ch.b_ss])
                P.op("dve", lambda e: e.tensor_tensor(out=ch.o2[:], in0=ch.o2[:], in1=bc(ch.ss[:], 128), op=ALU.mult),
                     reads=[ch.b_ss, ch.b_o2], writes=[ch.b_o2])
                P.op("dve", lambda e: e.tensor_tensor(out=ch.o2[:], in0=ch.o2[:],
                                                      in1=h128(dnw[:, :]), op=ALU.mult),
                     reads=[b_dnw, ch.b_o2], writes=[ch.b_o2])
                P.op("dve", lambda e: e.tensor_tensor(out=h128(ch.ogd[:]), in0=ch.o2[:], in1=h128(ch.gzt[:]),
                                                      op=ALU.mult), reads=[ch.b_o2, ch.b_gzt], writes=[ch.b_ogd])
                yield
                for hh in range(4):
                    P.op("pe", lambda e, hh=hh: e.transpose(out=pTr[:, hh * 64:(hh + 1) * 64],
                                                            in_=ch.ogd[:, hh * 128:(hh + 1) * 128],
                                                            identity=identb[0:64, 0:64]),
                         reads=[ch.b_ogd, b_c], writes=[b_pTr])
                P.op("act", lambda e: e.copy(out=ch.ogT[:].rearrange("p h n -> p (h n)"), in_=pTr[:, 0:256]),
                     reads=[b_pTr], writes=[ch.b_ogT])
                P.dma(STQ, lambda e: e.dma_start(
                    out=G["og"][4096 + gi * 512:4096 + (gi + 1) * 512, tok0 + t0:tok0 + t0 + 64].rearrange(
                        "(h p) t -> p h t", p=128), in_=ch.ogT[:]), reads=[ch.b_ogT])
                yield

        for gi in range(ngroups):
            P.dma("sp", lambda e, gi=gi: e.dma_start(out=dnw[:], in_=G["dnw_bc"][:, gi * 512:(gi + 1) * 512]),
                  writes=[b_dnw])
            for X in range(3):
                P.dma("sp", lambda e, X=X, gi=gi: e.dma_start(
                    out=qkv[:, X, :, :], in_=G["dqkv"][X, gi * 4:(gi + 1) * 4, :, tok0:tok0 + NT].rearrange("h p t -> p h t")),
                    writes=[b_qkv])
            for (s0, T, kind, sidx) in seqs:
                NC = T // 64
                for d in range(2):
                    ch = chains[d]
                    if kind == "S":
                        P.dma("sp", lambda e, d=d, gi=gi, ch=ch: e.dma_start(
                            out=ch.S[:], in_=G["state_dn"][d, gi * 4:(gi + 1) * 4].rearrange("h k v -> k h v")),
                            writes=[ch.b_S])
                    else:
                        P.op("pool", lambda e, ch=ch: e.memset(ch.S[:], 0.0), writes=[ch.b_S])
                    P.op("act", lambda e, ch=ch: e.copy(out=ch.Sb[:], in_=ch.S[:]), reads=[ch.b_S], writes=[ch.b_Sb])
                for s in range(NC):
                    gens = [step(chains[0], gi, 0, s0, NC, s), step(chains[1], gi, 1, s0, NC, s)]
                    live = [True, True]
                    if DN_SEQ:
                        for g_ in gens:
                            for _ in g_:
                                pass
                        live = [False, False]
                    while any(live):
                        for k in range(2):
                            if live[k]:
                                try:
                                    next(gens[k])
                                except StopIteration:
                                    live[k] = False
                if kind == "P":
                    for d in range(2):
                        P.dma(STQ, lambda e, d=d, gi=gi, sidx=sidx: e.dma_start(
                            out=G["new_dn"][sidx, d, gi * 4:(gi + 1) * 4].rearrange("h k v -> k h v"), in_=chains[d].S[:]),
                            reads=[chains[d].b_S])
        P.wait_all_dma()
        with nc.Block() as block:
            P.emit(block)


ALPHA = 2.0 ** 0.25


def stage_modrow(nc, sync, G):
    with ExitStack() as es:
        cx = Ctx(nc, es)
        P = Prog(sync)
        scT = cx.sb([128, KC, 2], F32, "scT")
        brow = cx.sb([2, 4096], F32, "brow")
        mrow = cx.sb([2, 4096], F32, "mrow")
        wb = [cx.sb([128, KC, 512], F32, "wada") for _ in range(2)]
        ps = [cx.ps([128, 512], F32, "psm") for _ in range(2)]
        b_sc, b_br, b_mr = Buf(), Buf(), Buf()
        b_wb = [Buf(), Buf()]
        b_ps = [Buf(), Buf()]
        P.dma("sp", lambda e: e.dma_start(out=scT[:], in_=G["condT"][:, :, :]), writes=[b_sc])
        P.dma("sp", lambda e: e.dma_start(out=brow[:], in_=G["b_adarow"][:, :]), writes=[b_br])
        P.op("act", lambda e: e.activation(out=scT[:], in_=scT[:], func=AF.Silu), reads=[b_sc], writes=[b_sc])
        wv = G["w_ada"].rearrange("(kc p) n -> p kc n", p=128)
        for i, nb in enumerate(list(range(8, 12)) + list(range(20, 24))):
            sl = i % 2
            P.dma("sp", lambda e, nb=nb, sl=sl: e.dma_start(out=wb[sl][:], in_=wv[:, :, nb * 512:(nb + 1) * 512]),
                  writes=[b_wb[sl]])
            mm_acc(P, ps[sl][0:2, :], b_ps[sl], [(scT[:, kc, :], wb[sl][:, kc, :]) for kc in range(KC)],
                   [b_wb[sl], b_sc])
            P.op("dve", lambda e, i=i, sl=sl: e.tensor_tensor(out=mrow[:, i * 512:(i + 1) * 512], in0=ps[sl][0:2, :],
                                                               in1=brow[:, i * 512:(i + 1) * 512], op=ALU.add),
                 reads=[b_ps[sl], b_br], writes=[b_mr])
        P.dma(STQ, lambda e: e.dma_start(out=G["modrow"][:, :], in_=mrow[:]), reads=[b_mr])
        P.wait_all_dma()
        with nc.Block() as block:
            P.emit(block)


def stage_d1(nc, sync, G):
    with ExitStack() as es:
        cx = Ctx(nc, es)
        P = Prog(sync)
        W = WStream(P, cx, nbuf=4, nstg=4, engines=("act", "dve", "pool"))
        ogr = [cx.sb([128, 32, 512], BF16, "ogr") for _ in range(2)]
        ogd = cx.sb([128, 16, 512], BF16, "ogd")
        sgt = [cx.sb([128, 8, 512], BF16, "sgt") for _ in range(2)]
        b_ogr = [Buf(), Buf()]
        b_ogd = Buf()
        b_sg = [Buf(), Buf()]
        yo = [cx.sb([128, 4, 512], BF16, "yo") for _ in range(2)]
        b_yo = [Buf(), Buf()]
        t1 = cx.sb([128, 512], F32, "t1")
        t2 = cx.sb([128, 512], F32, "t2")
        b_t1, b_t2 = Buf(), Buf()
        pr = [cx.ps([128, 512], F32, "pr") for _ in range(4)]
        pd = [cx.ps([128, 512], F32, "pd") for _ in range(4)]
        b_pr = [Buf() for _ in range(4)]
        b_pd = [Buf() for _ in range(4)]
        it = 0
        for cb in range(4):
            cs_ = slice(cb * 512, (cb + 1) * 512)
            w0, b_w0 = W.load(G["w_br"][0:2048, cs_], 512)
            w1, b_w1 = W.load(G["w_br"][2048:4096, cs_], 512)
            w2, b_w2 = W.load(G["w_bd"][:, cs_], 512)
            for tt in range(NTOK // 512):
                tsl = slice(tt * 512, (tt + 1) * 512)
                k = it % 2
                it += 1
                orr, b_orr = ogr[k], b_ogr[k]
                sg_, b_sg_ = sgt[k], b_sg[k]
                P.dma("sp", lambda e, tsl=tsl, orr=orr: e.dma_start(
                    out=orr[:], in_=G["og"][0:4096, tsl].rearrange("(a p) t -> p a t", p=128)), writes=[b_orr])
                P.dma("sp", lambda e, tsl=tsl: e.dma_start(
                    out=ogd[:], in_=G["og"][4096:6144, tsl].rearrange("(a p) t -> p a t", p=128)), writes=[b_ogd])
                P.dma("sp", lambda e, tsl=tsl, sg_=sg_, cb=cb: e.dma_start(
                    out=sg_[:, 0:4, :], in_=G["sg"][cb * 512:(cb + 1) * 512, tsl].rearrange("(a p) t -> p a t", p=128)),
                    writes=[b_sg_])
                P.dma("sp", lambda e, tsl=tsl, sg_=sg_, cb=cb: e.dma_start(
                    out=sg_[:, 4:8, :],
                    in_=G["sg"][2048 + cb * 512:2048 + (cb + 1) * 512, tsl].rearrange("(a p) t -> p a t", p=128)),
                    writes=[b_sg_])
                o2, b_o2 = yo[k], b_yo[k]
                for sub in range(4):
                    ss_ = slice(sub * 128, (sub + 1) * 128)
                    pairs = [(w0[:, kc, ss_], orr[:, kc, :]) for kc in range(KC)]
                    pairs += [(w1[:, kc, ss_], orr[:, 16 + kc, :]) for kc in range(KC)]
                    mm_acc(P, pr[sub][:, :], b_pr[sub], pairs, [b_w0, b_w1, b_orr])
                for sub in range(4):
                    ss_ = slice(sub * 128, (sub + 1) * 128)
                    mm_acc(P, pd[sub][:, :], b_pd[sub], [(w2[:, kc, ss_], ogd[:, kc, :]) for kc in range(KC)],
                           [b_w2, b_ogd])
                for sub in range(4):
                    P.op("dve", lambda e, sub=sub, sg_=sg_: e.tensor_tensor(out=t1[:], in0=pr[sub][:, :], in1=sg_[:, sub, :],
                                                                            op=ALU.mult), reads=[b_pr[sub], b_sg_],
                         writes=[b_t1])
                    P.op("dve", lambda e, sub=sub, sg_=sg_: e.tensor_tensor(out=t2[:], in0=pd[sub][:, :],
                                                                            in1=sg_[:, 4 + sub, :], op=ALU.mult),
                         reads=[b_pd[sub], b_sg_], writes=[b_t2])
                    P.op("pool", lambda e, o2=o2, sub=sub: e.tensor_tensor(out=o2[:, sub, :], in0=t1[:], in1=t2[:],
                                                                             op=ALU.add), reads=[b_t1, b_t2], writes=[b_o2])
                P.dma(STQ, lambda e, o2=o2, cs_=cs_, tsl=tsl: e.dma_start(
                    out=G["yT"][cs_, tsl].rearrange("(a p) t -> p a t", p=128), in_=o2[:]), reads=[b_o2])
        P.wait_all_dma()
        with nc.Block() as block:
            P.emit(block)


def row_bcast(P, cx, G, c, lo, dst, b_dst, sel, b_sel, mr, b_mr, ps, b_ps):
    P.dma("sp", lambda e: e.dma_start(out=mr[:], in_=G["modrow"][:, lo:lo + 2048]), writes=[b_mr])
    for j in range(4):
        P.op("pe", lambda e, j=j: e.matmul(ps[:, :], lhsT=sel[:, c, :], rhs=mr[:, j * 512:(j + 1) * 512], start=True,
                                            stop=True), reads=[b_sel, b_mr], writes=[b_ps])
        P.op("act", lambda e, j=j: e.copy(out=dst[:, j * 512:(j + 1) * 512], in_=ps[:, :]), reads=[b_ps], writes=[b_dst])


def ln_affine_store(P, r_t, b_r, scr, gt, bt, b_gb, out_t, b_out):
    ln_rows(P, None, r_t, b_r, out_t, b_out, scr)
    P.op("pool", lambda e: e.tensor_tensor(out=out_t[:], in0=out_t[:], in1=gt[:], op=ALU.mult), reads=[b_gb, b_out],
         writes=[b_out])
    P.op("pool", lambda e: e.tensor_tensor(out=out_t[:], in0=out_t[:], in1=bt[:], op=ALU.add), reads=[b_gb, b_out],
         writes=[b_out])


def stage_d2(nc, sync, G):
    with ExitStack() as es:
        cx = Ctx(nc, es)
        P = Prog(sync)
        W = WStream(P, cx, nbuf=2, nstg=4, engines=("act", "act", "pool"))
        yts = [cx.sb([128, KC, 512], BF16, "yt") for _ in range(2)]
        b_yts = [Buf(), Buf()]
        rs = [cx.sb([128, 4, D], F32, "r") for _ in range(2)]
        b_rs = [[Buf() for _ in range(4)] for _ in range(2)]
        grow = cx.sb([128, D], F32, "grow")
        b_grow = Buf()
        lg_ = cx.sb([128, D], F32, "lng")
        lb_ = cx.sb([128, D], F32, "lnb")
        b_gb = Buf()
        x1 = cx.sb([128, D], F32, "x1")
        b_x1 = Buf()
        xn = cx.sb([128, D], BF16, "xn")
        b_xn = Buf()
        h2 = cx.sb([128, KC, 128], BF16, "h2")
        b_h2 = Buf()
        sel = cx.sb([2, 2, 128], F32, "sel")
        mr = cx.sb([2, 2048], F32, "mr")
        b_sel, b_mr = Buf(), Buf()
        ident = cx.sb([128, 128], BF16, "ident")
        scr = {"st": cx.sb([128, 4, 6], F32), "mv": cx.sb([128, 2], F32), "rstd": cx.sb([128, 1], F32), "b_st": Buf()}
        ps = [cx.ps([128, 512], F32, "pm") for _ in range(4)]
        b_ps = [Buf() for _ in range(4)]
        pT = [cx.ps([128, 1024], BF16, "pT") for _ in range(2)]
        b_pT = [Buf(), Buf()]
        modT = G["modT"]
        P.dma("sp", lambda e: e.dma_start(out=sel[:], in_=G["sel"][:, :, :]), writes=[b_sel])
        P.dma("sp", lambda e: e.dma_start(out=ident[:], in_=G["ident_bf"][:, :]), writes=[b_sel])
        P.dma("sp", lambda e: e.dma_start(out=lg_[:], in_=G["ln1g_bc"][:, :]), writes=[b_gb])
        P.dma("sp", lambda e: e.dma_start(out=lb_[:], in_=G["ln1b_bc"][:, :]), writes=[b_gb])
        mtmp = cx.sb([128, 512], F32, "mtmp")
        b_mtmp = Buf()
        state = {"last_c": None}

        def mm_phase(tt):
            c = 0 if tt == 0 else 1
            k = tt % 2
            tsl = slice(tt * 512, (tt + 1) * 512)
            rr, b_rr, yt_, b_yt_ = rs[k], b_rs[k], yts[k], b_yts[k]
            if c != state["last_c"]:
                row_bcast(P, cx, G, c, 0, grow, b_grow, sel, b_sel, mr, b_mr, ps[0], b_ps[0])
                state["last_c"] = c
            P.dma("sp", lambda e: e.dma_start(out=yt_[:], in_=G["yT"][:, tsl].rearrange("(a p) t -> p a t", p=128)),
                  writes=[b_yt_])
            for ts in range(4):
                P.dma("sp", lambda e, ts=ts: e.dma_start(
                    out=rr[:, ts, :], in_=G["x"][tt * 512 + ts * 128:tt * 512 + (ts + 1) * 128, :]), writes=[b_rr[ts]])
            for cb in range(4):
                cs_ = slice(cb * 512, (cb + 1) * 512)
                w, b_w = W.load(G["w_o"][:, cs_], 512)
                for ts in range(4):
                    mm_acc(P, ps[ts][:, :], b_ps[ts], [(yt_[:, kc, ts * 128:(ts + 1) * 128], w[:, kc, :]) for kc in range(KC)],
                           [b_w, b_yt_])
                    P.op("dve", lambda e, ts=ts, cs_=cs_: e.tensor_tensor(out=mtmp[:], in0=ps[ts][:, :], in1=grow[:, cs_],
                                                                           op=ALU.mult),
                         reads=[b_ps[ts], b_grow], writes=[b_mtmp])
                    P.op("dve", lambda e, ts=ts, cs_=cs_: e.scalar_tensor_tensor(
                        out=rr[:, ts, cs_], in0=rr[:, ts, cs_], scalar=ALPHA, in1=mtmp[:], op0=ALU.mult, op1=ALU.add),
                        reads=[b_mtmp, b_rr[ts]], writes=[b_rr[ts]])

        def ln_phase(tt):
            c = 0 if tt == 0 else 1
            k = tt % 2
            rr, b_rr = rs[k], b_rs[k]
            for ts in range(4):
                tok = tt * 512 + ts * 128
                ln_affine_store(P, rr[:, ts, :], b_rr[ts], scr, lg_, lb_, b_gb, x1, b_x1)
                P.dma(STQ, lambda e, tok=tok: e.dma_start(out=G["x1"][tok:tok + 128, :], in_=x1[:]), reads=[b_x1])
                ln_rows(P, None, x1, b_x1, xn, b_xn, scr)
                for g in range(2):
                    for j in range(8):
                        kc = g * 8 + j
                        P.op("pe", lambda e, g=g, j=j, kc=kc: e.transpose(
                            out=pT[g][:, j * 128:(j + 1) * 128], in_=xn[:, kc * 128:(kc + 1) * 128], identity=ident[:]),
                            reads=[b_xn, b_sel], writes=[b_pT[g]])
                    for j in range(8):
                        kc = g * 8 + j
                        P.op("act", lambda e, g=g, j=j, kc=kc, c=c: e.activation(
                            out=h2[:, kc, :], in_=pT[g][:, j * 128:(j + 1) * 128], func=AF.Identity,
                            scale=modT[:, 64 + kc, c:c + 1], bias=modT[:, 48 + kc, c:c + 1]), reads=[b_pT[g]],
                            writes=[b_h2])
                P.dma(STQ, lambda e, tok=tok: e.dma_start(
                    out=G["h2T"][:, tok:tok + 128].rearrange("(a p) t -> p a t", p=128), in_=h2[:]), reads=[b_h2])

        NTT = NTOK // 512
        for tt in range(NTT):
            mm_phase(tt)
            if tt > 0:
                ln_phase(tt - 1)
        ln_phase(NTT - 1)
        P.wait_all_dma()
        with nc.Block() as block:
            P.emit(block)


SEQS_ALL = [(0, 256), (256, 256), (512, 2048)]


def stage_e1(nc, sync, G):
    with ExitStack() as es:
        cx = Ctx(nc, es)
        P = Prog(sync)
        W = WStream(P, cx, nbuf=4, nstg=4, engines=("act", "act", "pool"))
        h2 = cx.sb([128, KC, NTOK], BF16, "h2")
        b_h2 = Buf()
        P.dma("sp", lambda e: e.dma_start(out=h2[:, :, 0:1280], in_=G["h2T"][:, 0:1280].rearrange("(a p) t -> p a t", p=128)),
              writes=[b_h2])
        P.dma("sp", lambda e: e.dma_start(out=h2[:, :, 1280:2560],
                                          in_=G["h2T"][:, 1280:2560].rearrange("(a p) t -> p a t", p=128)), writes=[b_h2])
        cw = cx.sb([128, 88, 4], F32, "cw")
        b_c = Buf()
        P.dma("sp", lambda e: e.dma_start(out=cw[:], in_=G["ffn_cwT"][:, :, :]), writes=[b_c])
        y = cx.sb([128, NTOK], F32, "y")
        za = cx.sb([128, NTOK], F32, "za")
        zv = cx.sb([128, NTOK], F32, "zv")
        go = [cx.sb([128, NTOK], BF16, "go") for _ in range(2)]
        b_y, b_za, b_zv = Buf(), Buf(), Buf()
        b_go = [Buf(), Buf()]
        ps = [cx.ps([128, 512], F32, "pe") for _ in range(5)]
        b_ps = [Buf() for _ in range(5)]

        def conv(zt, b_zt, blk):
            P.op("dve", lambda e: e.tensor_scalar(out=zt[:], in0=y[:], scalar1=cw[:, blk, 1:2], scalar2=cw[:, blk, 3:4],
                                                   op0=ALU.mult, op1=ALU.add), reads=[b_y, b_c], writes=[b_zt])
            for (s0, T) in SEQS_ALL:
                P.op("dve", lambda e, s0=s0, T=T: e.scalar_tensor_tensor(
                    out=zt[:, s0 + 1:s0 + T], in0=y[:, s0:s0 + T - 1], scalar=cw[:, blk, 0:1], in1=zt[:, s0 + 1:s0 + T],
                    op0=ALU.mult, op1=ALU.add), reads=[b_y, b_c, b_zt], writes=[b_zt])
                P.op("dve", lambda e, s0=s0, T=T: e.scalar_tensor_tensor(
                    out=zt[:, s0:s0 + T - 1], in0=y[:, s0 + 1:s0 + T], scalar=cw[:, blk, 2:3], in1=zt[:, s0:s0 + T - 1],
                    op0=ALU.mult, op1=ALU.add), reads=[b_y, b_c, b_zt], writes=[b_zt])

        for jb in range(11):
            wa, b_wa = W.load(G["w_up"][:, jb * 512:(jb + 1) * 512], 512)
            wv_, b_wv = W.load(G["w_up"][:, 5632 + jb * 512:5632 + (jb + 1) * 512], 512)
            for sub in range(4):
                fb = jb * 4 + sub
                ss_ = slice(sub * 128, (sub + 1) * 128)
                for (wt, b_w, zt, b_zt, blk) in ((wa, b_wa, za, b_za, fb), (wv_, b_wv, zv, b_zv, 44 + fb)):
                    for tt in range(5):
                        mm_acc(P, ps[tt][:, :], b_ps[tt],
                               [(wt[:, kc, ss_], h2[:, kc, tt * 512:(tt + 1) * 512]) for kc in range(KC)], [b_w, b_h2])
                        P.op("act", lambda e, tt=tt: e.copy(out=y[:, tt * 512:(tt + 1) * 512], in_=ps[tt][:, :]),
                             reads=[b_ps[tt]], writes=[b_y])
                    conv(zt, b_zt, blk)
                P.op("act", lambda e: e.activation(out=za[:], in_=za[:], func=AF.Silu), reads=[b_za], writes=[b_za])
                o2, b_o2 = go[fb % 2], b_go[fb % 2]
                P.op("dve", lambda e, o2=o2: e.tensor_tensor(out=o2[:], in0=za[:], in1=zv[:], op=ALU.mult),
                     reads=[b_za, b_zv], writes=[b_o2])
                P.dma(STQ, lambda e, o2=o2, fb=fb: e.dma_start(out=G["gT"][fb * 128:(fb + 1) * 128, :], in_=o2[:]),
                      reads=[b_o2])
        P.wait_all_dma()
        with nc.Block() as block:
            P.emit(block)


def stage_e2(nc, sync, G):
    with ExitStack() as es:
        cx = Ctx(nc, es)
        P = Prog(sync)
        W = WStream(P, cx, nbuf=4, nstg=4, engines=("act", "dve", "act", "pool"))
        gt = cx.sb([128, 44, 512], BF16, "gt")
        b_gt = Buf()
        r = cx.sb([128, 4, D], F32, "r")
        b_r = [Buf() for _ in range(4)]
        grow = cx.sb([128, D], F32, "grow")
        b_grow = Buf()
        lg_ = cx.sb([128, D], F32, "lng")
        lb_ = cx.sb([128, D], F32, "lnb")
        b_gb = Buf()
        tmp = cx.sb([128, 512], F32, "tmp")
        b_tmp = Buf()
        yo = cx.sb([128, D], F32, "yo")
        b_yo = Buf()
        sel = cx.sb([2, 2, 128], F32, "sel")
        mr = cx.sb([2, 2048], F32, "mr")
        b_sel, b_mr = Buf(), Buf()
        scr = {"st": cx.sb([128, 4, 6], F32), "mv": cx.sb([128, 2], F32), "rstd": cx.sb([128, 1], F32), "b_st": Buf()}
        ps = [cx.ps([128, 512], F32, "pm") for _ in range(8)]
        b_ps = [Buf() for _ in range(8)]
        P.dma("sp", lambda e: e.dma_start(out=sel[:], in_=G["sel"][:, :, :]), writes=[b_sel])
        P.dma("sp", lambda e: e.dma_start(out=lg_[:], in_=G["ln2g_bc"][:, :]), writes=[b_gb])
        P.dma("sp", lambda e: e.dma_start(out=lb_[:], in_=G["ln2b_bc"][:, :]), writes=[b_gb])
        last_c = None
        KG = [(0, 16), (16, 16), (32, 12)]
        for tt in range(NTOK // 512):
            c = 0 if tt == 0 else 1
            tsl = slice(tt * 512, (tt + 1) * 512)
            if c != last_c:
                row_bcast(P, cx, G, c, 2048, grow, b_grow, sel, b_sel, mr, b_mr, ps[0], b_ps[0])
                last_c = c
            P.dma("sp", lambda e, tsl=tsl: e.dma_start(out=gt[:], in_=G["gT"][:, tsl].rearrange("(a p) t -> p a t", p=128)),
                  writes=[b_gt])
            for ts in range(4):
                P.dma("sp", lambda e, ts=ts, tt=tt: e.dma_start(
                    out=r[:, ts, :], in_=G["x1"][tt * 512 + ts * 128:tt * 512 + (ts + 1) * 128, :]), writes=[b_r[ts]])
            for cb in range(4):
                cs_ = slice(cb * 512, (cb + 1) * 512)
                pb0 = (cb % 2) * 4
                for gi, (k0, kn) in enumerate(KG):
                    w, b_w = W.load(G["w_down"][k0 * 128:(k0 + kn) * 128, cs_], 512, kcn=kn)
                    for ts in range(4):
                        for kc in range(kn):
                            P.op("pe", lambda e, ts=ts, kc=kc, k0=k0, w=w, gi=gi, kn=kn, pb0=pb0: e.matmul(
                                ps[pb0 + ts][:, :], lhsT=gt[:, k0 + kc, ts * 128:(ts + 1) * 128], rhs=w[:, kc, :],
                                start=(gi == 0 and kc == 0), stop=(gi == 2 and kc == kn - 1)),
                                reads=[b_w, b_gt], writes=[b_ps[pb0 + ts]])
                for ts in range(4):
                    P.op("dve", lambda e, ts=ts, cs_=cs_, pb0=pb0: e.tensor_tensor(
                        out=tmp[:], in0=ps[pb0 + ts][:, :], in1=grow[:, cs_], op=ALU.mult), reads=[b_ps[pb0 + ts], b_grow],
                        writes=[b_tmp])
                    P.op("dve", lambda e, ts=ts, cs_=cs_: e.scalar_tensor_tensor(
                        out=r[:, ts, cs_], in0=r[:, ts, cs_], scalar=ALPHA, in1=tmp[:], op0=ALU.mult, op1=ALU.add),
                        reads=[b_tmp, b_r[ts]], writes=[b_r[ts]])
            for ts in range(4):
                tok = tt * 512 + ts * 128
                ln_affine_store(P, r[:, ts, :], b_r[ts], scr, lg_, lb_, b_gb, yo, b_yo)
                P.dma(STQ, lambda e, tok=tok: e.dma_start(out=G["y"][tok:tok + 128, :], in_=yo[:]), reads=[b_yo])
        P.wait_all_dma()
        with nc.Block() as block:
            P.emit(block)


def build(debug=None):
    nc = bass.Bass("TRN2", target_bir_lowering=False)
    G = {}

    def din(name, shape, dt=F32):
        G[name] = nc.dram_tensor(name, list(shape), dt, kind="ExternalInput").ap()

    def dout(name, shape, dt=F32):
        G[name] = nc.dram_tensor(name, list(shape), dt, kind="ExternalOutput").ap()

    def dscr(name, shape, dt=F32):
        G[name] = nc.dram_tensor(name, list(shape), dt, kind="ExternalOutput" if debug else "Internal").ap()

    din("x", [NTOK, D])
    din("condT", [128, KC, 2])
    din("w_ada", [D, 6 * D])
    din("b_adaT", [128, 96])
    din("b_adarow", [2, 6 * D])
    din("ident_bf", [128, 128], BF16)
    din("ones_bf", [128, 128], BF16)
    din("w_in", [D, 24640])
    din("lg_bc", [128, 16])
    din("ret_cst", [128, 6, 128])
    din("ret_pcol", [128, 3])
    din("rnw_bc", [128, 4096])
    din("rope_cs", [128, 2, 2048])
    din("state_ret", [2, 8, 256, 512])
    din("state_dn", [2, 16, 128, 128])
    din("dn_cwT", [128, 48, 3])
    din("dn_ab", [64, 2, 32])
    din("dn_cst", [64, 7, 64])
    din("dnw_bc", [64, 2048])
    din("sel", [2, 2, 128])
    din("w_br", [4096, D])
    din("w_bd", [D, D])
    din("w_o", [D, D])
    din("ln1g_bc", [128, D])
    din("ln1b_bc", [128, D])
    din("w_up", [D, 11264])
    din("ffn_cwT", [128, 88, 4])
    din("w_down", [5632, D])
    din("ln2g_bc", [128, D])
    din("ln2b_bc", [128, D])
    dout("y", [NTOK, D])
    dout("new_ret", [2, 2, 8, 256, 512])
    dout("new_dn", [2, 2, 16, 128, 128])
    dscr("og", [6144, NTOK], BF16)
    dscr("sg", [4096, NTOK], BF16)
    dscr("sb_scr", [16, 2, 128, 512], BF16)
    dscr("dqkv", [3, 16, 128, NTOK], BF16)
    dscr("dzg", [NTOK, 2048], BF16)
    dscr("dbg", [64, NTOK // 64, 64], F32)
    dscr("modrow", [2, 4096], F32)
    dscr("yT", [D, NTOK], BF16)
    dscr("x1", [NTOK, D], F32)
    dscr("h2T", [D, NTOK], BF16)
    dscr("gT", [5632, NTOK], BF16)
    upto = debug if isinstance(debug, str) else "all"
    order = ["ada", "ret", "gates", "dnproj", "dnrec", "d1", "d2", "e1", "all"]
    lim = order.index(upto)
    with ExitStack() as es:
        sync = Sync(nc, es)
        modT = es.enter_context(nc.sbuf_tensor("modT", [128, 96, 2], F32))
        G["modT"] = modT
        stage_ada(nc, sync, G)
        passes = [(512, 2048, 1, [(0, 2048, "S", None)], True),
                  (0, 512, 0, [(0, 256, "P", 0), (256, 256, "P", 1)], False)]
        if lim >= 1:
            for (tok0, NT, cond, seqs, rope) in passes:
                with ExitStack() as es2:
                    hT = es2.enter_context(nc.sbuf_tensor("hT%d" % tok0, [128, KC, 2048], BF16))
                    stage_ln1(nc, sync, G, hT, tok0, NT // 128, cond)
                    stage_ret(nc, sync, G, hT, None, NT, seqs, tok0, rope)
                    if lim >= 2:
                        stage_gates(nc, sync, G, hT, NT, tok0)
                    if lim >= 3:
                        stage_dnproj(nc, sync, G, hT, NT, seqs, tok0)
                if lim >= 4:
                    stage_dnrec(nc, sync, G, NT, seqs, tok0)
        if lim >= 5:
            stage_d1(nc, sync, G)
        if lim >= 6:
            stage_d2(nc, sync, G)
        if lim >= 7:
            stage_e1(nc, sync, G)
        if lim >= 8:
            stage_e2(nc, sync, G)
    return nc


_PERM = np.concatenate([np.arange(0, 256, 2), np.arange(1, 256, 2)])


def shared_inputs(inp):
    f = np.float32
    m = {}
    m["w_ada"] = np.ascontiguousarray(inp["w_ada"][0])
    b = inp["b_ada"][0]
    m["b_adaT"] = np.ascontiguousarray(b.reshape(96, 128).T)
    m["b_adarow"] = np.ascontiguousarray(np.stack([b, b], 0))
    m["ident_bf"] = np.eye(128, dtype=f).astype(ml_dtypes.bfloat16)
    m["ones_bf"] = np.ones((128, 128), dtype=f).astype(ml_dtypes.bfloat16)
    w_in = np.array(inp["w_in"][0])
    for h in range(8):
        for base in (0, 2048):
            blk = w_in[:, base + h * 256:base + (h + 1) * 256]
            w_in[:, base + h * 256:base + (h + 1) * 256] = blk[:, _PERM]
    m["w_in"] = np.ascontiguousarray(w_in)
    m["lg_bc"] = np.ascontiguousarray(np.tile(inp["ret_log_decay"][0].reshape(1, 16), (128, 1)))
    j = np.arange(128)[:, None].astype(f)
    i = np.arange(128)[None, :].astype(f)
    cst = np.zeros((128, 6, 128), f)
    cst[:, 0] = np.maximum(i - j, 0)
    cst[:, 1] = np.maximum(j - i, 0)
    cst[:, 2] = (i >= j) / 16.0
    cst[:, 3] = (j >= i) / 16.0
    cst[:, 4] = i + 1 + 0 * j
    cst[:, 5] = 128 - i + 0 * j
    m["ret_cst"] = cst
    p = np.arange(128).astype(f)
    m["ret_pcol"] = np.ascontiguousarray(np.stack([127 - p, p, 128 + 0 * p], 1))
    m["rnw_bc"] = np.ascontiguousarray(np.tile(inp["ret_norm_w"][0][None, :], (128, 1)))
    t = np.arange(2048)
    row = (t // 64).astype(f)
    col = (t % 64).astype(f)
    inv = (np.float32(10000.0) ** (-np.arange(64, dtype=f) / np.float32(64))).astype(f)
    ang = np.concatenate([row[:, None] * inv[None, :], col[:, None] * inv[None, :]], axis=-1).astype(f)
    m["rope_cs"] = np.ascontiguousarray(np.stack([np.cos(ang).T, np.sin(ang).T], 1).astype(f))
    cw = inp["dn_conv_w"][0]
    m["dn_cwT"] = np.ascontiguousarray(cw.reshape(3, 48, 128).transpose(2, 1, 0))
    ab = np.stack([inp["dn_A_log"][0].reshape(32), inp["dn_dt_bias"][0].reshape(32)], 0)
    m["dn_ab"] = np.ascontiguousarray(np.tile(ab[None], (64, 1, 1)))
    a = np.arange(64)[:, None]
    bb = np.arange(64)[None, :]
    NEG = -30000.0
    dc = np.zeros((64, 7, 64), f)
    dc[:, 0] = (a == bb)
    dc[:, 1] = (a <= bb)
    dc[:, 2] = (a >= bb)
    dc[:, 3] = np.where(a > bb, 0.0, NEG)
    dc[:, 4] = np.where(a < bb, 0.0, NEG)
    dc[:, 5] = np.where(bb >= a, 0.0, NEG)
    dc[:, 6] = np.where(bb <= a, 0.0, NEG)
    m["dn_cst"] = dc
    m["dnw_bc"] = np.ascontiguousarray(np.tile(inp["dn_norm_w"][0][None, :], (64, 1)))
    sel = np.zeros((2, 2, 128), f)
    sel[0, 0] = 1
    sel[1, 1] = 1
    m["sel"] = sel
    m["w_br"] = np.ascontiguousarray(inp["w_br"][0])
    m["w_bd"] = np.ascontiguousarray(inp["w_bd"][0])
    m["w_o"] = np.ascontiguousarray(inp["w_o"][0])
    m["w_up"] = np.ascontiguousarray(inp["w_up"][0])
    m["w_down"] = np.ascontiguousarray(inp["w_down"][0])
    for k in ("ln1_g", "ln1_b", "ln2_g", "ln2_b"):
        m[k.replace("_", "") + "_bc"] = np.ascontiguousarray(np.tile(inp[k][0][None, :], (128, 1)))
    fc = np.concatenate([inp["ffn_conv_w"][0], inp["ffn_conv_b"]], 0)
    m["ffn_cwT"] = np.ascontiguousarray(fc.reshape(4, 88, 128).transpose(2, 1, 0))
    return m


def core_inputs(core, inp, shared):
    m = dict(shared)
    xp = inp["x_prompt"][2 * core:2 * core + 2].reshape(512, D)
    m["x"] = np.ascontiguousarray(np.concatenate([xp, inp["x_sample"][core]], axis=0))
    cond = np.stack([inp["c_ctx"], inp["c"][core]], axis=0)
    m["condT"] = np.ascontiguousarray(cond.reshape(2, KC, 128).transpose(2, 1, 0))
    m["state_ret"] = np.ascontiguousarray(inp["state_ret"][core, 0][:, :, _PERM, :])
    m["state_dn"] = np.ascontiguousarray(inp["state_dn"][core, 0])
    return m


_NC_CACHE = {}


def kernel(**inp):
    inp = {k: np.asarray(v) for k, v in inp.items()}
    if "nc" not in _NC_CACHE:
        _NC_CACHE["nc"] = build()
    nc = _NC_CACHE["nc"]
    shared = shared_inputs(inp)
    in_maps = [core_inputs(c, inp, shared) for c in range(8)]
    res = run_bass_kernel_spmd(nc, in_maps, core_ids=list(range(8)))
    y_p = np.zeros((16, 256, D), np.float32)
    y_s = np.zeros((8, 2048, D), np.float32)
    n_ret = np.zeros((16, 1, 2, 8, 256, 512), np.float32)
    n_dn = np.zeros((16, 1, 2, 16, 128, 128), np.float32)
    for c in range(8):
        r = res.results[c]
        y = np.asarray(r["y"])
        y_p[2 * c:2 * c + 2] = y[0:512].reshape(2, 256, D)
        y_s[c] = y[512:]
        nr = np.asarray(r["new_ret"])
        n_ret[2 * c:2 * c + 2, 0][:, :, :, _PERM, :] = nr
        n_dn[2 * c:2 * c + 2, 0] = np.asarray(r["new_dn"])
    return (y_p, y_s, n_ret, n_dn)
```

```python
from contextlib import ExitStack
import numpy as np
import ml_dtypes
import concourse.bass as bass
import concourse.mybir as mybir
from concourse.bass_utils import run_bass_kernel_spmd

F32 = mybir.dt.float32
BF16 = mybir.dt.bfloat16
AF = mybir.ActivationFunctionType
ALU = mybir.AluOpType
AX = mybir.AxisListType

D = 2048
NTOK = 2560
KC = 16
LN_EPS = 1e-5

ENGS = ["pe", "act", "dve", "pool", "sp"]
NRING = 24
STQ = "act"


class Buf:
    __slots__ = ("name", "w", "r")

    def __init__(self, name=""):
        self.name = name
        self.w = None
        self.r = {}


class Sync:
    def __init__(self, nc, es):
        self.nc = nc
        self.esem = {e: es.enter_context(nc.semaphore("s_" + e)) for e in ENGS if e != "sp"}
        self.ring = [es.enter_context(nc.semaphore("r_%d" % i)) for i in range(NRING)]
        self.ebase = {e: 0 for e in self.esem}
        self.rcount = [0] * NRING
        self.ndma = 0


class Prog:
    def __init__(self, sync, selfsync=("act", "dve", "pool")):
        self.s = sync
        self.ops = {e: [] for e in ENGS}
        self.selfsync = set(selfsync)
        self.ring_last = {}

    def _deps(self, reads, writes):
        d = {}

        def add(t):
            if t is None:
                return
            k = (t[0], t[1])
            if k not in d or d[k][2] < t[2]:
                d[k] = t

        for b in reads:
            add(b.w)
        for b in writes:
            add(b.w)
            for t in b.r.values():
                add(t)
        return d

    def _finish(self, tok, reads, writes):
        k = (tok[0], tok[1])
        for b in reads:
            b.r[k] = tok
        for b in writes:
            b.w = tok
            b.r = {}

    def op(self, eng, fn, reads=(), writes=()):
        d = self._deps(reads, writes)
        if eng not in self.selfsync:
            d.pop(("e", eng), None)
        tok = ("e", eng, len(self.ops[eng]))
        self.ops[eng].append({"fn": fn, "deps": list(d.values()), "needed": False, "ring": None})
        self._finish(tok, reads, writes)
        return tok

    def dma(self, eng, fn, reads=(), writes=()):
        s = self.s
        ri = s.ndma % NRING
        s.ndma += 1
        s.rcount[ri] += 1
        tok = ("s", ri, 16 * s.rcount[ri])
        d = self._deps(reads, writes)
        if eng not in self.selfsync:
            d.pop(("e", eng), None)
        deps = list(d.values())
        prev = self.ring_last.get(ri)
        if prev is not None:
            deps.append(prev)
        self.ring_last[ri] = tok
        self.ops[eng].append({"fn": fn, "deps": deps, "needed": False, "ring": ri})
        self._finish(tok, reads, writes)
        return tok

    def wait_all_dma(self, eng="sp"):
        self.ops[eng].append({"fn": None, "deps": list(self.ring_last.values()), "needed": False, "ring": None})

    def emit(self, block):
        s = self.s
        for e in ENGS:
            for o in self.ops[e]:
                for t in o["deps"]:
                    if t[0] == "e":
                        self.ops[t[1]][t[2]]["needed"] = True
        for e in ENGS:
            if e == "sp":
                continue
            c = s.ebase[e]
            for o in self.ops[e]:
                if o["needed"]:
                    c += 1
                    o["val"] = c
            s.ebase[e] = c

        def resolve(t):
            if t[0] == "e":
                return s.esem[t[1]], self.ops[t[1]][t[2]]["val"]
            return s.ring[t[1]], t[2]

        def emit_engine(ename, eh):
            waited = {}
            for o in self.ops[ename]:
                ws = {}
                for t in o["deps"]:
                    sem, v = resolve(t)
                    k = id(sem)
                    if waited.get(k, 0) >= v:
                        continue
                    if k not in ws or ws[k][1] < v:
                        ws[k] = (sem, v)
                wl = list(ws.values())
                for sem, v in wl:
                    waited[id(sem)] = v
                if o["fn"] is None:
                    for sem, v in wl:
                        eh.wait_ge(sem, v)
                    continue
                embed = None
                if wl and ename != "pe":
                    embed = wl.pop()
                for sem, v in wl:
                    eh.wait_ge(sem, v)
                inst = o["fn"](eh)
                if embed is not None:
                    inst._wait_ge(embed[0], embed[1])
                if o["ring"] is not None:
                    inst.then_inc(s.ring[o["ring"]], 16)
                elif o["needed"]:
                    inst.then_inc(s.esem[ename], 1)

        @block.tensor
        def _(eh):
            emit_engine("pe", eh)

        @block.scalar
        def _(eh):
            emit_engine("act", eh)

        @block.vector
        def _(eh):
            emit_engine("dve", eh)

        @block.gpsimd
        def _(eh):
            emit_engine("pool", eh)

        @block.sync
        def _(eh):
            emit_engine("sp", eh)


class Ctx:
    CNT = [0]

    def __init__(self, nc, es):
        self.nc = nc
        self.es = es

    def sb(self, shape, dt, name=None):
        Ctx.CNT[0] += 1
        return self.es.enter_context(self.nc.sbuf_tensor("%s_%d" % (name or "t", Ctx.CNT[0]), list(shape), dt))

    def ps(self, shape, dt, name=None):
        Ctx.CNT[0] += 1
        return self.es.enter_context(self.nc.psum_tensor("%s_%d" % (name or "p", Ctx.CNT[0]), list(shape), dt))


def stage_ada(nc, sync, G):
    with ExitStack() as es:
        cx = Ctx(nc, es)
        P = Prog(sync)
        scT = cx.sb([128, KC, 2], F32, "scT")
        brow = cx.sb([2, 6 * D], F32, "brow")
        mrow = cx.sb([2, 6 * D], F32, "mrow")
        sel = cx.sb([2, 2, 128], F32, "sel")
        wb = [cx.sb([128, KC, 512], F32, "wada") for _ in range(3)]
        ps = [cx.ps([128, 512], F32, "psA") for _ in range(2)]
        pt = cx.ps([128, 512], F32, "psT")
        b_sc, b_br, b_mr, b_sel, b_pt = Buf(), Buf(), Buf(), Buf(), Buf()
        b_wb = [Buf() for _ in range(3)]
        b_ps = [Buf(), Buf()]
        modT = G["modT"]
        b_mod = Buf()
        P.dma("sp", lambda e: e.dma_start(out=scT[:], in_=G["condT"][:, :, :]), writes=[b_sc])
        P.dma("sp", lambda e: e.dma_start(out=brow[:], in_=G["b_adarow"][:, :]), writes=[b_br])
        P.dma("sp", lambda e: e.dma_start(out=sel[:], in_=G["sel"][:, :, :]), writes=[b_sel])
        P.op("act", lambda e: e.activation(out=scT[:], in_=scT[:], func=AF.Silu), reads=[b_sc], writes=[b_sc])
        wv = G["w_ada"].rearrange("(kc p) n -> p kc n", p=128)
        for nb in range(24):
            sl = nb % 3
            k = nb % 2
            P.dma("sp", lambda e, nb=nb, sl=sl: e.dma_start(out=wb[sl][:], in_=wv[:, :, nb * 512:(nb + 1) * 512]),
                  writes=[b_wb[sl]])
            mm_acc(P, ps[k][0:2, :], b_ps[k], [(scT[:, kc, :], wb[sl][:, kc, :]) for kc in range(KC)],
                   [b_wb[sl], b_sc])
            P.op("dve", lambda e, nb=nb, k=k: e.tensor_tensor(out=mrow[:, nb * 512:(nb + 1) * 512], in0=ps[k][0:2, :],
                                                               in1=brow[:, nb * 512:(nb + 1) * 512], op=ALU.add),
                 reads=[b_ps[k], b_br], writes=[b_mr])
        P.dma(STQ, lambda e: e.dma_start(out=G["modrow"][:, 0:2048], in_=mrow[:, 4096:6144]), reads=[b_mr])
        P.dma(STQ, lambda e: e.dma_start(out=G["modrow"][:, 2048:4096], in_=mrow[:, 10240:12288]), reads=[b_mr])
        for j in range(96):
            P.op("pe", lambda e, j=j: e.transpose(out=pt[:, 2 * j:2 * j + 2], in_=mrow[:, j * 128:(j + 1) * 128],
                                                  identity=sel[:, :, 0]), reads=[b_mr, b_sel], writes=[b_pt])
        P.op("dve", lambda e: e.tensor_copy(out=modT[:], in_=pt[:, 0:192].rearrange("p (j c) -> p j c", c=2)),
             reads=[b_pt], writes=[b_mod])
        for lo in (16, 64):
            P.op("dve", lambda e, lo=lo: e.tensor_scalar_add(out=modT[:, lo:lo + 16, :], in0=modT[:, lo:lo + 16, :],
                                                             scalar1=1.0), reads=[b_mod], writes=[b_mod])
        P.wait_all_dma()
        with nc.Block() as block:
            P.emit(block)


def ln_rows(P, cx, x_t, b_x, out_t, b_out, scratch):
    st, mv, rstd = scratch["st"], scratch["mv"], scratch["rstd"]
    b_st = scratch["b_st"]
    for c in range(4):
        P.op("dve", lambda e, c=c: e.bn_stats(out=st[:, c, :], in_=x_t[:, c * 512:(c + 1) * 512]),
             reads=[b_x], writes=[b_st])
    P.op("dve", lambda e: e.bn_aggr(out=mv[:], in_=st[:]), reads=[b_st], writes=[b_st])
    P.op("act", lambda e: e.activation(out=rstd[:], in_=mv[:, 1:2], func=AF.Sqrt, bias=LN_EPS, scale=1.0),
         reads=[b_st], writes=[b_st])
    P.op("dve", lambda e: e.reciprocal(out=rstd[:], in_=rstd[:]), reads=[b_st], writes=[b_st])
    P.op("dve", lambda e: e.tensor_scalar(out=out_t[:], in0=x_t[:], scalar1=mv[:, 0:1], scalar2=rstd[:, 0:1],
                                           op0=ALU.subtract, op1=ALU.mult), reads=[b_x, b_st], writes=[b_out])


def stage_ln1(nc, sync, G, hT, tok0, ntiles, cond, dbg=None):
    with ExitStack() as es:
        cx = Ctx(nc, es)
        P = Prog(sync)
        ident = cx.sb([128, 128], BF16, "ident")
        b_id = Buf()
        P.dma("sp", lambda e: e.dma_start(out=ident[:], in_=G["ident_bf"][:, :]), writes=[b_id])
        xt = [cx.sb([128, D], F32, "x") for _ in range(2)]
        b_xt = [Buf(), Buf()]
        xn = [cx.sb([128, D], BF16, "xn") for _ in range(2)]
        b_xn = [Buf(), Buf()]
        scr = {"st": cx.sb([128, 4, 6], F32), "mv": cx.sb([128, 2], F32), "rstd": cx.sb([128, 1], F32), "b_st": Buf()}
        pst = [cx.ps([128, 1024], BF16, "pT") for _ in range(2)]
        b_pst = [Buf(), Buf()]
        modT = G["modT"]
        b_h = Buf()
        for t in range(ntiles):
            sl = t % 2
            c = cond
            P.dma("sp", lambda e, t=t, sl=sl: e.dma_start(
                out=xt[sl][:], in_=G["x"][tok0 + t * 128:tok0 + (t + 1) * 128, :]), writes=[b_xt[sl]])
            ln_rows(P, cx, xt[sl], b_xt[sl], xn[sl], b_xn[sl], scr)
            for g in range(2):
                for j in range(8):
                    kc = g * 8 + j
                    P.op("pe", lambda e, g=g, j=j, kc=kc, sl=sl: e.transpose(
                        out=pst[g][:, j * 128:(j + 1) * 128], in_=xn[sl][:, kc * 128:(kc + 1) * 128],
                        identity=ident[:]), reads=[b_xn[sl], b_id], writes=[b_pst[g]])
                for j in range(8):
                    kc = g * 8 + j
                    P.op("act", lambda e, g=g, j=j, kc=kc, t=t, c=c: e.activation(
                        out=hT[:, kc, t * 128:(t + 1) * 128], in_=pst[g][:, j * 128:(j + 1) * 128],
                        func=AF.Identity, scale=modT[:, 16 + kc, c:c + 1], bias=modT[:, kc, c:c + 1]),
                        reads=[b_pst[g]], writes=[b_h])
        if dbg is not None:
            P.dma(STQ, lambda e: e.dma_start(
                out=dbg.rearrange("(kc p) t -> p kc t", p=128)[:, :, tok0:tok0 + ntiles * 128],
                in_=hT[:, :, 0:ntiles * 128]), reads=[b_h])
        P.wait_all_dma()
        with nc.Block() as block:
            P.emit(block)


class WStream:
    CH = 2

    def __init__(self, P, cx, nbuf=3, nstg=4, cols=512, engines=("pool",)):
        self.P = P
        self.engines = list(engines)
        self.stg = [cx.sb([128, self.CH, cols], F32, "stg") for _ in range(nstg)]
        self.b_stg = [Buf() for _ in range(nstg)]
        self.wb = [cx.sb([128, KC, cols], BF16, "wb") for _ in range(nbuf)]
        self.b_wb = [Buf() for _ in range(nbuf)]
        self.i = 0
        self.j = 0

    def load(self, w_ap, ncols, kcn=KC):
        P = self.P
        CH = self.CH
        sl = self.i % len(self.wb)
        self.i += 1
        wv = w_ap.rearrange("(kc p) n -> p kc n", p=128)
        wb, b_wb = self.wb[sl], self.b_wb[sl]
        for g in range(kcn // CH):
            st = self.j % len(self.stg)
            self.j += 1
            stg, b_stg = self.stg[st], self.b_stg[st]
            P.dma("sp", lambda e, stg=stg, g=g: e.dma_start(out=stg[:, :, 0:ncols], in_=wv[:, g * CH:(g + 1) * CH, :]),
                  writes=[b_stg])
            ce = self.engines[self.j % len(self.engines)]
            if ce == "act":
                P.op("act", lambda e, stg=stg, wb=wb, g=g: e.copy(out=wb[:, g * CH:(g + 1) * CH, 0:ncols],
                                                                  in_=stg[:, :, 0:ncols]), reads=[b_stg], writes=[b_wb])
            else:
                P.op(ce, lambda e, stg=stg, wb=wb, g=g: e.tensor_copy(out=wb[:, g * CH:(g + 1) * CH, 0:ncols],
                                                                      in_=stg[:, :, 0:ncols]),
                     reads=[b_stg], writes=[b_wb])
        return wb, b_wb


def mm_acc(P, ps, b_ps, pairs, reads):
    n = len(pairs)
    for i, (l, r) in enumerate(pairs):
        P.op("pe", lambda e, l=l, r=r, i=i: e.matmul(ps, lhsT=l, rhs=r, start=(i == 0), stop=(i == n - 1)),
             reads=reads, writes=[b_ps])


def stage_ret(nc, sync, G, hT, b_hT_unused, NT, seqs, tok0, rope):
    with ExitStack() as es:
        cx = Ctx(nc, es)
        P = Prog(sync)
        W = WStream(P, cx, nbuf=3, nstg=4, engines=("pool", "act"))
        b_h = Buf()
        NCH = NT // 128
        qT = cx.sb([128, 2, NT], BF16, "qT")
        kT = cx.sb([128, 2, NT], BF16, "kT")
        vt = cx.sb([128, NCH, 512], BF16, "v")
        b_q, b_k, b_v = Buf(), Buf(), Buf()
        Sst = [cx.sb([128, 2, 512], F32, "S") for _ in range(2)]
        Sbf = [cx.sb([128, 2, 512], BF16, "Sbf") for _ in range(2)]
        b_S = [Buf(), Buf()]
        b_Sbf = [Buf(), Buf()]
        sbb = [cx.sb([128, 2, 512], BF16, "sbb") for _ in range(2)]
        b_sbb = [Buf(), Buf()]
        tmp1 = cx.sb([128, 512], F32, "tmp1")
        tmp2 = cx.sb([128, 512], F32, "tmp2")
        b_t1, b_t2 = Buf(), Buf()
        cs = cx.sb([128, 2, 512], F32, "cs")
        b_cs = Buf()
        mask = cx.sb([128, 128], F32, "mask")
        mtmp = cx.sb([128, 128], F32, "mtmp")
        qdr = cx.sb([128, 2, 128], F32, "qdr")
        b_hc = Buf()
        rnw = cx.sb([128, 512], F32, "rnw")
        b_rnw = Buf()
        gate = cx.sb([128, 512], F32, "gate")
        b_gate = Buf()
        ogs = [cx.sb([128, 512], BF16, "og") for _ in range(2)]
        b_ogs = [Buf(), Buf()]
        ogT = [cx.sb([128, 4, 128], BF16, "ogT") for _ in range(2)]
        b_ogT = [Buf(), Buf()]
        ktok = cx.sb([128, 256], BF16, "ktok")
        b_ktok = Buf()
        sm = cx.sb([128, 128], BF16, "sm")
        b_sm = Buf()
        qfb = cx.sb([128, 2, 2, 128], BF16, "qfb")
        b_qfb = Buf()
        scr = {"st": cx.sb([128, 1, 6], F32), "mv": cx.sb([128, 2], F32), "rstd": cx.sb([128, 1], F32), "b_st": Buf()}
        ident = cx.sb([128, 128], BF16, "ident")
        lg = cx.sb([128, 16], F32, "lg")
        cst = cx.sb([128, 6, 128], F32, "cst")
        pcol = cx.sb([128, 3], F32, "pcol")
        dec = cx.sb([128, 8, 4], F32, "dec")
        b_c = Buf()
        P.dma("sp", lambda e: e.dma_start(out=ident[:], in_=G["ident_bf"][:, :]), writes=[b_c])
        P.dma("sp", lambda e: e.dma_start(out=lg[:], in_=G["lg_bc"][:, :]), writes=[b_c])
        P.dma("sp", lambda e: e.dma_start(out=cst[:], in_=G["ret_cst"][:, :, :]), writes=[b_c])
        P.dma("sp", lambda e: e.dma_start(out=pcol[:], in_=G["ret_pcol"][:, :]), writes=[b_c])
        for h in range(8):
            for d in range(2):
                l = lg[:, d * 8 + h:d * 8 + h + 1]
                P.op("act", lambda e, h=h, d=d, l=l: e.activation(out=dec[:, h, d:d + 1], in_=pcol[:, d:d + 1],
                                                                   func=AF.Exp, scale=l), reads=[b_c], writes=[b_c])
                P.op("act", lambda e, h=h, d=d, l=l: e.activation(out=dec[:, h, 2 + d:3 + d], in_=pcol[:, 2:3],
                                                                   func=AF.Exp, scale=l), reads=[b_c], writes=[b_c])
        P.op("dve", lambda e: e.tensor_scalar_mul(out=dec[:, :, 0:2], in0=dec[:, :, 0:2], scalar1=1.0 / 16.0),
             reads=[b_c], writes=[b_c])
        pA = cx.ps([128, 512], F32, "pA")
        pB = cx.ps([128, 512], F32, "pB")
        pO = cx.ps([128, 512], F32, "pO")
        pG = cx.ps([128, 512], F32, "pG")
        pU = [cx.ps([128, 512], F32, "pU") for _ in range(2)]
        pT = cx.ps([128, 1024], BF16, "pT")
        pS = cx.ps([128, 512], F32, "pS")
        b_pA, b_pB, b_pO, b_pG, b_pT, b_pS = Buf(), Buf(), Buf(), Buf(), Buf(), Buf()
        b_pU = [Buf(), Buf()]
        win = G["w_in"]
        sb_scr = G["sb_scr"]
        b_scr = [Buf() for _ in range(16)]

        def ktok_make(h, d, c0):
            for dkb in range(2):
                P.op("pe", lambda e, dkb=dkb: e.transpose(out=pT[:, dkb * 128:(dkb + 1) * 128],
                                                          in_=kT[:, dkb, c0:c0 + 128], identity=ident[:]),
                     reads=[b_k, b_c], writes=[b_pT])
            P.op("dve", lambda e: e.tensor_scalar(out=ktok[:], in0=pT[:, 0:256], scalar1=dec[:, h, d:d + 1],
                                                   scalar2=None, op0=ALU.mult), reads=[b_pT, b_c], writes=[b_ktok])

        def state_update(h, d, n_local):
            for dkb in range(2):
                P.op("pe", lambda e, dkb=dkb: e.matmul(pU[dkb][:, :], lhsT=ktok[:, dkb * 128:(dkb + 1) * 128],
                                                       rhs=vt[:, n_local, :], start=True, stop=True),
                     reads=[b_ktok, b_v], writes=[b_pU[dkb]])
            for dkb in range(2):
                P.op("dve", lambda e, dkb=dkb: e.scalar_tensor_tensor(
                    out=Sst[d][:, dkb, :], in0=Sst[d][:, dkb, :], scalar=dec[:, h, 2 + d:3 + d], in1=pU[dkb][:, :],
                    op0=ALU.mult, op1=ALU.add), reads=[b_pU[dkb], b_c, b_S[d]], writes=[b_S[d]])
            P.op("act", lambda e: e.copy(out=Sbf[d][:], in_=Sst[d][:]), reads=[b_S[d]], writes=[b_Sbf[d]])

        nxt = (W.load(win[:, 0:256], 256), W.load(win[:, 2048:2048 + 256], 256))
        for h in range(8):
            (wq, b_wq), (wk, b_wk) = nxt
            wv, b_wv = W.load(win[:, 4096 + h * 512:4096 + (h + 1) * 512], 512)
            lf = lg[:, h:h + 1]
            lb = lg[:, 8 + h:9 + h]
            P.op("act", lambda e, lf=lf: e.activation(out=mask[:], in_=cst[:, 0, :], func=AF.Exp, scale=lf),
                 reads=[b_c, b_sm], writes=[b_hc])
            P.op("act", lambda e, lb=lb: e.activation(out=mtmp[:], in_=cst[:, 1, :], func=AF.Exp, scale=lb),
                 reads=[b_c], writes=[b_hc])
            P.op("dve", lambda e: e.tensor_tensor(out=mask[:], in0=mask[:], in1=cst[:, 2, :], op=ALU.mult),
                 reads=[b_hc, b_c], writes=[b_hc])
            P.op("dve", lambda e: e.tensor_tensor(out=mtmp[:], in0=mtmp[:], in1=cst[:, 3, :], op=ALU.mult),
                 reads=[b_hc, b_c], writes=[b_hc])
            P.op("dve", lambda e: e.tensor_tensor(out=mask[:], in0=mask[:], in1=mtmp[:], op=ALU.add),
                 reads=[b_hc], writes=[b_hc])
            P.op("act", lambda e, lf=lf: e.activation(out=qdr[:, 0, :], in_=cst[:, 4, :], func=AF.Exp, scale=lf),
                 reads=[b_c, b_qfb], writes=[b_hc])
            P.op("act", lambda e, lb=lb: e.activation(out=qdr[:, 1, :], in_=cst[:, 5, :], func=AF.Exp, scale=lb),
                 reads=[b_c], writes=[b_hc])
            P.dma("sp", lambda e, h=h: e.dma_start(out=rnw[:], in_=G["rnw_bc"][:, h * 512:(h + 1) * 512]),
                  writes=[b_rnw])
            for tt in range(NT // 512):
                tsl = slice(tt * 512, (tt + 1) * 512)
                if rope:
                    P.dma("sp", lambda e, tsl=tsl: e.dma_start(out=cs[:], in_=G["rope_cs"][:, :, tsl]), writes=[b_cs])
                for (wt, b_w, dst, b_dst) in ((wq, b_wq, qT, b_q), (wk, b_wk, kT, b_k)):
                    for dkb, (ps, b_ps) in enumerate(((pA, b_pA), (pB, b_pB))):
                        mm_acc(P, ps[:, :], b_ps,
                               [(wt[:, kc, dkb * 128:(dkb + 1) * 128], hT[:, kc, tsl]) for kc in range(KC)],
                               [b_w, b_h])
                    if rope:
                        P.op("dve", lambda e: e.tensor_tensor(out=tmp1[:], in0=pA[:, :], in1=cs[:, 0, :], op=ALU.mult),
                             reads=[b_pA, b_cs], writes=[b_t1])
                        P.op("dve", lambda e: e.tensor_tensor(out=tmp2[:], in0=pB[:, :], in1=cs[:, 1, :], op=ALU.mult),
                             reads=[b_pB, b_cs], writes=[b_t2])
                        P.op("pool", lambda e, dst=dst, tsl=tsl: e.tensor_tensor(out=dst[:, 0, tsl], in0=tmp1[:],
                                                                                   in1=tmp2[:], op=ALU.subtract),
                             reads=[b_t1, b_t2], writes=[b_dst])
                        P.op("dve", lambda e: e.tensor_tensor(out=tmp1[:], in0=pA[:, :], in1=cs[:, 1, :], op=ALU.mult),
                             reads=[b_pA, b_cs], writes=[b_t1])
                        P.op("dve", lambda e: e.tensor_tensor(out=tmp2[:], in0=pB[:, :], in1=cs[:, 0, :], op=ALU.mult),
                             reads=[b_pB, b_cs], writes=[b_t2])
                        P.op("pool", lambda e, dst=dst, tsl=tsl: e.tensor_tensor(out=dst[:, 1, tsl], in0=tmp1[:],
                                                                                   in1=tmp2[:], op=ALU.add),
                             reads=[b_t1, b_t2], writes=[b_dst])
                    else:
                        P.op("act", lambda e, dst=dst, tsl=tsl: e.copy(out=dst[:, 0, tsl], in_=pA[:, :]),
                             reads=[b_pA], writes=[b_dst])
                        P.op("act", lambda e, dst=dst, tsl=tsl: e.copy(out=dst[:, 1, tsl], in_=pB[:, :]),
                             reads=[b_pB], writes=[b_dst])
            wg, b_wg = W.load(win[:, 8192 + h * 512:8192 + (h + 1) * 512], 512)
            for t in range(NCH):
                mm_acc(P, pO[:, :], b_pO, [(hT[:, kc, t * 128:(t + 1) * 128], wv[:, kc, :]) for kc in range(KC)],
                       [b_wv, b_h])
                P.op("act", lambda e, t=t: e.copy(out=vt[:, t, :], in_=pO[:, :]), reads=[b_pO], writes=[b_v])
            if h < 7:
                nxt = (W.load(win[:, (h + 1) * 256:(h + 2) * 256], 256),
                       W.load(win[:, 2048 + (h + 1) * 256:2048 + (h + 2) * 256], 256))
            for (s0, T, kind, sidx) in seqs:
                N = T // 128
                for d in range(2):
                    if kind == "S":
                        P.dma("sp", lambda e, d=d, h=h: e.dma_start(
                            out=Sst[d][:], in_=G["state_ret"][d, h].rearrange("(b p) v -> p b v", p=128)),
                            writes=[b_S[d]])
                    else:
                        P.op("pool", lambda e, d=d: e.memset(Sst[d][:], 0.0), writes=[b_S[d]])
                    P.op("act", lambda e, d=d: e.copy(out=Sbf[d][:], in_=Sst[d][:]), reads=[b_S[d]], writes=[b_Sbf[d]])
                for n in range(N - 1, -1, -1):
                    c0 = s0 + n * 128
                    P.dma(STQ, lambda e, n=n: e.dma_start(out=sb_scr[n].rearrange("b p v -> p b v"), in_=Sbf[1][:]),
                          reads=[b_Sbf[1]], writes=[b_scr[n]])
                    ktok_make(h, 1, c0)
                    state_update(h, 1, c0 // 128)
                if kind == "P":
                    P.dma(STQ, lambda e, h=h, sidx=sidx: e.dma_start(
                        out=G["new_ret"][sidx, 1, h].rearrange("(b p) v -> p b v", p=128), in_=Sst[1][:]),
                        reads=[b_S[1]])
                pOs, b_pOs = [pO, pA], [b_pO, b_pA]
                pGs, b_pGs = [pG, pB], [b_pG, b_pB]
                gates, b_gates = [gate[:], cs[:, 0, :]], [b_gate, b_cs]
                tmps, b_tmps = [tmp1, tmp2], [b_t1, b_t2]

                def fin_pe(n):
                    c0 = s0 + n * 128
                    sl = n % 2
                    og_, b_og_ = ogs[sl], b_ogs[sl]
                    for fb in range(4):
                        P.op("pe", lambda e, fb=fb: e.transpose(out=pT[:, 256 + fb * 128:256 + (fb + 1) * 128],
                                                                in_=og_[:, fb * 128:(fb + 1) * 128], identity=ident[:]),
                             reads=[b_og_, b_c], writes=[b_pT])
                    o2 = ogT[sl]
                    P.op("act", lambda e: e.copy(out=o2[:].rearrange("p a b -> p (a b)"), in_=pT[:, 256:768]),
                         reads=[b_pT], writes=[b_ogT[sl]])
                    P.dma(STQ, lambda e, h=h: e.dma_start(
                        out=G["og"][h * 512:(h + 1) * 512, tok0 + c0:tok0 + c0 + 128].rearrange("(a p) t -> p a t", p=128),
                        in_=o2[:]), reads=[b_ogT[sl]])

                def ln_chain(n):
                    sl = n % 2
                    pO_, b_pO_ = pOs[sl], b_pOs[sl]
                    gate_, b_gate_ = gates[sl], b_gates[sl]
                    tmp_, b_tmp_ = tmps[sl], b_tmps[sl]
                    og_, b_og_ = ogs[sl], b_ogs[sl]
                    st, mv, rstd, b_st = scr["st"], scr["mv"], scr["rstd"], scr["b_st"]
                    P.op("dve", lambda e, tmp_=tmp_: e.bn_stats(out=st[:, 0, :], in_=tmp_[:]), reads=[b_tmp_], writes=[b_st])
                    P.op("dve", lambda e: e.bn_aggr(out=mv[:], in_=st[:]), reads=[b_st], writes=[b_st])
                    P.op("act", lambda e: e.activation(out=rstd[:], in_=mv[:, 1:2], func=AF.Sqrt, bias=LN_EPS, scale=1.0),
                         reads=[b_st], writes=[b_st])
                    P.op("dve", lambda e: e.reciprocal(out=rstd[:], in_=rstd[:]), reads=[b_st], writes=[b_st])
                    P.op("dve", lambda e, tmp_=tmp_: e.tensor_scalar(
                        out=tmp_[:], in0=tmp_[:], scalar1=mv[:, 0:1], scalar2=rstd[:, 0:1], op0=ALU.subtract,
                        op1=ALU.mult), reads=[b_tmp_, b_st], writes=[b_tmp_])
                    P.op("dve", lambda e, tmp_=tmp_: e.tensor_tensor(out=tmp_[:], in0=tmp_[:], in1=rnw[:], op=ALU.mult),
                         reads=[b_tmp_, b_rnw], writes=[b_tmp_])
                    P.op("dve", lambda e, tmp_=tmp_, gate_=gate_, og_=og_: e.tensor_tensor(
                        out=og_[:], in0=tmp_[:], in1=gate_, op=ALU.mult), reads=[b_tmp_, b_gate_], writes=[b_og_])

                def pre(n):
                    c0 = s0 + n * 128
                    mm_acc(P, pS[:, 0:128], b_pS, [(kT[:, dkb, c0:c0 + 128], qT[:, dkb, c0:c0 + 128]) for dkb in range(2)],
                           [b_q, b_k])
                    P.op("dve", lambda e: e.tensor_tensor(out=sm[:], in0=pS[:, 0:128], in1=mask[:], op=ALU.mult),
                         reads=[b_pS, b_hc], writes=[b_sm])
                    for d in range(2):
                        P.op("dve", lambda e, d=d: e.tensor_tensor(
                            out=qfb[:, d, :, :], in0=qT[:, :, c0:c0 + 128],
                            in1=qdr[:, d, :].unsqueeze(1).broadcast_to([128, 2, 128]), op=ALU.mult),
                            reads=[b_q, b_hc], writes=[b_qfb])

                pre(0)
                for n in range(N):
                    c0 = s0 + n * 128
                    nl = c0 // 128
                    sl = n % 2
                    pO_, b_pO_ = pOs[sl], b_pOs[sl]
                    pG_, b_pG_ = pGs[sl], b_pGs[sl]
                    gate_, b_gate_ = gates[sl], b_gates[sl]
                    P.dma("sp", lambda e, n=n, sl=sl: e.dma_start(out=sbb[sl][:], in_=sb_scr[n].rearrange("b p v -> p b v")),
                          reads=[b_scr[n]], writes=[b_sbb[sl]])
                    pairs = [(sm[:], vt[:, nl, :])]
                    pairs += [(qfb[:, 0, dkb, :], Sbf[0][:, dkb, :]) for dkb in range(2)]
                    pairs += [(qfb[:, 1, dkb, :], sbb[sl][:, dkb, :]) for dkb in range(2)]
                    mm_acc(P, pO_[:, :], b_pO_, pairs, [b_sm, b_v, b_qfb, b_Sbf[0], b_sbb[sl]])
                    tmp_, b_tmp_ = tmps[sl], b_tmps[sl]
                    P.op("act", lambda e, tmp_=tmp_, pO_=pO_: e.copy(out=tmp_[:], in_=pO_[:, :]), reads=[b_pO_],
                         writes=[b_tmp_])
                    ktok_make(h, 0, c0)
                    if n >= 2:
                        fin_pe(n - 2)
                    mm_acc(P, pG_[:, :], b_pG_, [(hT[:, kc, c0:c0 + 128], wg[:, kc, :]) for kc in range(KC)], [b_wg, b_h])
                    P.op("act", lambda e, gate_=gate_, pG_=pG_: e.activation(out=gate_, in_=pG_[:, :], func=AF.Silu),
                         reads=[b_pG_], writes=[b_gate_])
                    state_update(h, 0, nl)
                    if n + 1 < N:
                        pre(n + 1)
                    if n > 0:
                        ln_chain(n - 1)
                ln_chain(N - 1)
                if N >= 2:
                    fin_pe(N - 2)
                fin_pe(N - 1)
                if kind == "P":
                    P.dma(STQ, lambda e, h=h, sidx=sidx: e.dma_start(
                        out=G["new_ret"][sidx, 0, h].rearrange("(b p) v -> p b v", p=128), in_=Sst[0][:]),
                        reads=[b_S[0]])
        P.wait_all_dma()
        with nc.Block() as block:
            P.emit(block)


L2_EPS = 1e-6
RMS_EPS = 1e-6


def stage_gates(nc, sync, G, hT, NT, tok0):
    with ExitStack() as es:
        cx = Ctx(nc, es)
        P = Prog(sync)
        W = WStream(P, cx, nbuf=3, nstg=4, engines=("dve", "dve", "pool"))
        b_h = Buf()
        ps = [cx.ps([128, 512], F32, "pg") for _ in range(4)]
        b_ps = [Buf() for _ in range(4)]
        TT = min(NT, 512)
        sgt = [cx.sb([128, 4, NT], BF16, "sgt") for _ in range(2)]
        b_sgt = [Buf(), Buf()]
        for blk in range(8):
            w, b_w = W.load(G["w_in"][:, 20544 + blk * 512:20544 + (blk + 1) * 512], 512)
            o2, b_o2 = sgt[blk % 2], b_sgt[blk % 2]
            for tt in range(NT // TT):
                tsl = slice(tt * TT, (tt + 1) * TT)
                for cb in range(4):
                    mm_acc(P, ps[cb][:, 0:TT], b_ps[cb],
                           [(w[:, kc, cb * 128:(cb + 1) * 128], hT[:, kc, tsl]) for kc in range(KC)], [b_w, b_h])
                    P.op("act", lambda e, cb=cb, o2=o2, tsl=tsl: e.activation(out=o2[:, cb, tsl], in_=ps[cb][:, 0:TT],
                                                                               func=AF.Sigmoid),
                         reads=[b_ps[cb]], writes=[b_o2])
            P.dma(STQ, lambda e, blk=blk, o2=o2: e.dma_start(
                out=G["sg"][blk * 512:(blk + 1) * 512, tok0:tok0 + NT].rearrange("(a p) t -> p a t", p=128), in_=o2[:]),
                reads=[b_o2])
        P.wait_all_dma()
        with nc.Block() as block:
            P.emit(block)


def stage_dnproj(nc, sync, G, hT, NT, seqs, tok0):
    with ExitStack() as es:
        cx = Ctx(nc, es)
        P = Prog(sync)
        W = WStream(P, cx, nbuf=3, nstg=4, engines=("pool", "dve", "act"))
        b_h = Buf()
        pA = [cx.ps([128, 512], F32, "pA") for _ in range(4)]
        b_pA = [Buf() for _ in range(4)]
        pNs = [cx.ps([128, 512], F32, "pN") for _ in range(2)]
        b_pNs = [Buf(), Buf()]
        pZ = cx.ps([128, 512], F32, "pZ")
        b_pZ = Buf()
        ys = [cx.sb([128, NT + 2], F32, "y") for _ in range(2)]
        zs_ = [cx.sb([128, NT], F32, "z") for _ in range(2)]
        sqs = [cx.sb([128, NT], BF16, "sq") for _ in range(2)]
        rins = [cx.sb([128, NT], F32, "rin") for _ in range(2)]
        b_rins = [Buf(), Buf()]
        xo = [cx.sb([128, NT], BF16, "xo") for _ in range(2)]
        b_ys, b_zs, b_sqs = [Buf(), Buf()], [Buf(), Buf()], [Buf(), Buf()]
        b_xo = [Buf(), Buf()]
        ones = cx.sb([128, 128], BF16, "ones")
        cw = cx.sb([128, 48, 3], F32, "cw")
        b_c = Buf()
        P.dma("sp", lambda e: e.dma_start(out=ones[:], in_=G["ones_bf"][:, :]), writes=[b_c])
        P.dma("sp", lambda e: e.dma_start(out=cw[:], in_=G["dn_cwT"][:, :, :]), writes=[b_c])
        nx = 0
        pi = [0]
        for X in range(3):
            for gi in range(4):
                w, b_w = W.load(G["w_in"][:, 12288 + X * 2048 + gi * 512:12288 + X * 2048 + (gi + 1) * 512], 512)
                for hh in range(4):
                    H = gi * 4 + hh
                    blk = X * 16 + H
                    o2, b_o2 = xo[nx % 2], b_xo[nx % 2]
                    y, z, sq = ys[nx % 2], zs_[nx % 2], sqs[nx % 2]
                    b_y, b_z, b_sq = b_ys[nx % 2], b_zs[nx % 2], b_sqs[nx % 2]
                    nx += 1
                    for (s0, T, kind, sidx) in seqs:
                        TT = min(T, 512)
                        for tt in range(T // TT):
                            a = s0 + tt * TT
                            ps, b_ps = pA[pi[0] % 4], b_pA[pi[0] % 4]
                            pi[0] += 1
                            mm_acc(P, ps[:, 0:TT], b_ps,
                                   [(w[:, kc, hh * 128:(hh + 1) * 128], hT[:, kc, a:a + TT]) for kc in range(KC)],
                                   [b_w, b_h])
                            P.op("act", lambda e, ps=ps, a=a, TT=TT, y=y: e.copy(out=y[:, 1 + a:1 + a + TT], in_=ps[:, 0:TT]),
                                 reads=[b_ps], writes=[b_y])
                        zs = z[:, s0:s0 + T]
                        P.op("dve", lambda e, zs=zs, s0=s0, T=T, blk=blk, y=y: e.tensor_scalar(
                            out=zs, in0=y[:, 1 + s0:1 + s0 + T], scalar1=cw[:, blk, 1:2], scalar2=None, op0=ALU.mult),
                            reads=[b_y, b_c], writes=[b_z])
                        P.op("dve", lambda e, s0=s0, T=T, blk=blk, y=y, z=z: e.scalar_tensor_tensor(
                            out=z[:, s0 + 1:s0 + T], in0=y[:, 1 + s0:s0 + T], scalar=cw[:, blk, 0:1],
                            in1=z[:, s0 + 1:s0 + T], op0=ALU.mult, op1=ALU.add), reads=[b_y, b_c, b_z], writes=[b_z])
                        P.op("dve", lambda e, s0=s0, T=T, blk=blk, y=y, z=z: e.scalar_tensor_tensor(
                            out=z[:, s0:s0 + T - 1], in0=y[:, 2 + s0:1 + s0 + T], scalar=cw[:, blk, 2:3],
                            in1=z[:, s0:s0 + T - 1], op0=ALU.mult, op1=ALU.add), reads=[b_y, b_c, b_z], writes=[b_z])
                    P.op("act", lambda e, z=z: e.activation(out=z[:], in_=z[:], func=AF.Silu), reads=[b_z], writes=[b_z])
                    if X == 2:
                        P.op("act", lambda e, o2=o2, z=z: e.copy(out=o2[:], in_=z[:]), reads=[b_z], writes=[b_o2])
                    else:
                        P.op("act", lambda e, z=z, sq=sq: e.activation(out=sq[:], in_=z[:], func=AF.Square), reads=[b_z],
                             writes=[b_sq])
                        TT = min(NT, 512)
                        rin, b_rin = rins[nx % 2], b_rins[nx % 2]
                        for tt in range(NT // TT):
                            tsl = slice(tt * TT, (tt + 1) * TT)
                            pN_, b_pN_ = pNs[tt % 2], b_pNs[tt % 2]
                            P.op("pe", lambda e, tsl=tsl, TT=TT, sq=sq, pN_=pN_: e.matmul(
                                pN_[:, 0:TT], lhsT=ones[:], rhs=sq[:, tsl], start=True, stop=True),
                                reads=[b_sq, b_c], writes=[b_pN_])
                            P.op("act", lambda e, tsl=tsl, TT=TT, pN_=pN_, rin=rin: e.activation(
                                out=rin[:, tsl], in_=pN_[:, 0:TT], func=AF.Ln, bias=L2_EPS, scale=1.0), reads=[b_pN_],
                                writes=[b_rin])
                        P.op("act", lambda e, rin=rin: e.activation(out=rin[:], in_=rin[:], func=AF.Exp, scale=-0.5),
                             reads=[b_rin], writes=[b_rin])
                        sc = (128.0 ** -0.5) if X == 0 else 1.0
                        P.op("dve", lambda e, o2=o2, sc=sc, z=z, rin=rin: e.scalar_tensor_tensor(
                            out=o2[:], in0=z[:], scalar=sc, in1=rin[:], op0=ALU.mult, op1=ALU.mult),
                            reads=[b_z, b_rin], writes=[b_o2])
                    P.dma(STQ, lambda e, X=X, H=H, o2=o2: e.dma_start(out=G["dqkv"][X, H, :, tok0:tok0 + NT], in_=o2[:]),
                          reads=[b_o2])
        gz = [cx.sb([128, 512], BF16, "gz") for _ in range(2)]
        b_gz = [Buf(), Buf()]
        for gi in range(4):
            w, b_w = W.load(G["w_in"][:, 18432 + gi * 512:18432 + (gi + 1) * 512], 512)
            for t in range(NT // 128):
                sl = t % 2
                mm_acc(P, pZ[:, :], b_pZ, [(hT[:, kc, t * 128:(t + 1) * 128], w[:, kc, :]) for kc in range(KC)],
                       [b_w, b_h])
                P.op("act", lambda e, sl=sl: e.activation(out=gz[sl][:], in_=pZ[:, :], func=AF.Silu), reads=[b_pZ],
                     writes=[b_gz[sl]])
                P.dma(STQ, lambda e, sl=sl, t=t, gi=gi: e.dma_start(
                    out=G["dzg"][tok0 + t * 128:tok0 + (t + 1) * 128, gi * 512:(gi + 1) * 512], in_=gz[sl][:]),
                    reads=[b_gz[sl]])
        w, b_w = W.load(G["w_in"][:, 20480:20544], 64)
        NCH = NT // 64
        bg = cx.sb([64, NCH, 64], F32, "bg")
        b_bg = Buf()
        ab = cx.sb([64, 2, 32], F32, "ab")
        P.dma("sp", lambda e: e.dma_start(out=ab[:], in_=G["dn_ab"][:, :, :]), writes=[b_c])
        P.op("act", lambda e: e.activation(out=ab[:, 0, :], in_=ab[:, 0, :], func=AF.Exp), reads=[b_c], writes=[b_c])
        P.op("dve", lambda e: e.tensor_scalar_mul(out=ab[:, 0, :], in0=ab[:, 0, :], scalar1=-1.0), reads=[b_c],
             writes=[b_c])
        for cc in range(NCH):
            mm_acc(P, pZ[0:64, 0:64], b_pZ, [(hT[:, kc, cc * 64:(cc + 1) * 64], w[:, kc, 0:64]) for kc in range(KC)],
                   [b_w, b_h])
            P.op("act", lambda e, cc=cc: e.activation(out=bg[:, cc, 0:32], in_=pZ[0:64, 0:32], func=AF.Sigmoid),
                 reads=[b_pZ], writes=[b_bg])
            P.op("dve", lambda e, cc=cc: e.tensor_tensor(out=bg[:, cc, 32:64], in0=pZ[0:64, 32:64], in1=ab[:, 1, :],
                                                         op=ALU.add), reads=[b_pZ, b_c], writes=[b_bg])
        P.op("act", lambda e: e.activation(out=bg[:, :, 32:64], in_=bg[:, :, 32:64], func=AF.Exp), reads=[b_bg],
             writes=[b_bg])
        P.op("act", lambda e: e.activation(out=bg[:, :, 32:64], in_=bg[:, :, 32:64], func=AF.Ln, bias=1.0, scale=1.0),
             reads=[b_bg], writes=[b_bg])
        P.op("dve", lambda e: e.tensor_tensor(out=bg[:, :, 32:64], in0=bg[:, :, 32:64],
                                              in1=ab[:, 0, :].unsqueeze(1).broadcast_to([64, NCH, 32]), op=ALU.mult),
             reads=[b_bg, b_c], writes=[b_bg])
        P.dma(STQ, lambda e: e.dma_start(out=G["dbg"][:, tok0 // 64:tok0 // 64 + NCH, :], in_=bg[:]), reads=[b_bg])
        P.wait_all_dma()
        with nc.Block() as block:
            P.emit(block)


class _Ch:
    pass


DN_SEQ = False
DN_ALT = False


def stage_dnrec(nc, sync, G, NT, seqs, tok0, ngroups=4):
    with ExitStack() as es:
        cx = Ctx(nc, es)
        P = Prog(sync)
        NCH = NT // 64
        c0 = tok0 // 64
        cst = cx.sb([64, 7, 64], F32, "cst")
        id4 = cx.sb([64, 4, 64], F32, "id4")
        ones = cx.sb([64, 128], F32, "ones")
        identb = cx.sb([128, 128], BF16, "identb")
        dnw = cx.sb([64, 512], F32, "dnw")
        b_dnw = Buf()
        b_c = Buf()
        P.dma("sp", lambda e: e.dma_start(out=cst[:], in_=G["dn_cst"][:, :, :]), writes=[b_c])
        P.dma("sp", lambda e: e.dma_start(out=identb[:], in_=G["ident_bf"][:, :]), writes=[b_c])
        P.op("pool", lambda e: e.memset(ones[:], 1.0), writes=[b_c])
        for hh in range(4):
            P.op("pool", lambda e, hh=hh: e.tensor_copy(out=id4[:, hh, :], in_=cst[:, 0, :]), reads=[b_c], writes=[b_c])
        pb = [cx.ps([128, 512], F32, "pb") for _ in range(7)]
        pTr = cx.ps([128, 1024], BF16, "pTr")
        b_pb = [Buf() for _ in range(7)]
        b_pTr = Buf()
        bg = cx.sb([64, NCH, 64], F32, "bg")
        gcs = cx.sb([64, NCH, 32], F32, "gcs")
        egc = cx.sb([64, NCH, 32], F32, "egc")
        ekl = cx.sb([64, NCH, 32], F32, "ekl")
        bege = cx.sb([64, NCH, 32], F32, "bege")
        eglS = cx.sb([128, NCH, 32], F32, "eglS")
        b_sc = Buf()
        P.dma("sp", lambda e: e.dma_start(out=bg[:], in_=G["dbg"][:, c0:c0 + NCH, :]), writes=[b_sc])
        CB = 16
        for q0 in range(0, NCH, CB):
            nq = min(CB, NCH - q0)
            for d in range(2):
                P.op("pe", lambda e, d=d, q0=q0, nq=nq: e.matmul(
                    pb[6][0:64, 0:nq * 16].rearrange("p (c n) -> p c n", n=16), lhsT=cst[:, 1 + d, :],
                    rhs=bg[:, q0:q0 + nq, 32 + d * 16:48 + d * 16], start=True, stop=True),
                    reads=[b_sc, b_c], writes=[b_pb[6]])
                P.op("act", lambda e, d=d, q0=q0, nq=nq: e.copy(
                    out=gcs[:, q0:q0 + nq, d * 16:(d + 1) * 16],
                    in_=pb[6][0:64, 0:nq * 16].rearrange("p (c n) -> p c n", n=16)), reads=[b_pb[6]], writes=[b_sc])
            P.op("pe", lambda e, q0=q0, nq=nq: e.matmul(
                pb[0][0:64, 0:nq * 32].rearrange("p (c n) -> p c n", n=32), lhsT=ones[:, 0:64],
                rhs=bg[:, q0:q0 + nq, 32:64], start=True, stop=True), reads=[b_sc, b_c], writes=[b_pb[0]])
            P.op("dve", lambda e, q0=q0, nq=nq: e.tensor_tensor(
                out=ekl[:, q0:q0 + nq, :], in0=pb[0][0:64, 0:nq * 32].rearrange("p (c n) -> p c n", n=32),
                in1=gcs[:, q0:q0 + nq, :], op=ALU.subtract), reads=[b_pb[0], b_sc], writes=[b_sc])
            P.op("pe", lambda e, q0=q0, nq=nq: e.matmul(
                pb[1][:, 0:nq * 32].rearrange("p (c n) -> p c n", n=32), lhsT=ones[:, :],
                rhs=bg[:, q0:q0 + nq, 32:64], start=True, stop=True), reads=[b_sc, b_c], writes=[b_pb[1]])
            P.op("act", lambda e, q0=q0, nq=nq: e.activation(
                out=eglS[:, q0:q0 + nq, :], in_=pb[1][:, 0:nq * 32].rearrange("p (c n) -> p c n", n=32), func=AF.Exp),
                reads=[b_pb[1]], writes=[b_sc])
        P.op("act", lambda e: e.activation(out=ekl[:], in_=ekl[:], func=AF.Exp), reads=[b_sc], writes=[b_sc])
        P.op("act", lambda e: e.activation(out=egc[:], in_=gcs[:], func=AF.Exp), reads=[b_sc], writes=[b_sc])
        P.op("dve", lambda e: e.tensor_tensor(out=bege[:], in0=bg[:, :, 0:32], in1=egc[:], op=ALU.mult), reads=[b_sc],
             writes=[b_sc])
        qkv = cx.sb([128, 3, 4, NT], BF16, "qkv")
        b_qkv = Buf()
        of = cx.sb([64, NCH, 512], F32, "of")
        b_of = [Buf() for _ in range(NCH)]

        def bc(ap2, n):
            return ap2.unsqueeze(2).broadcast_to([ap2.shape[0], 4, n])

        def mk_chain(i):
            ch = _Ch()
            ch.Y = [pb[3 * i + k] for k in range(3)]
            ch.bY = [b_pb[3 * i + k] for k in range(3)]
            if DN_ALT:
                ch.Y = [pb[0], pb[1], pb[2]]
                ch.bY = [b_pb[0], b_pb[1], b_pb[2]]
                ch.YB, ch.bYB, ch.oB = pb[3], b_pb[3], 0
                ch.YQ, ch.bYQ, ch.oQ = pb[4], b_pb[4], 0
            else:
                ch.YB, ch.bYB, ch.oB = ch.Y[2], ch.bY[2], 256
                ch.YQ, ch.bYQ, ch.oQ = ch.Y[0], ch.bY[0], 0

            def t64(name, dt=F32, n=64):
                return cx.sb([64, 4, n], dt, name + str(i)), Buf()

            ch.S = cx.sb([128, 4, 128], F32, "S%d" % i)
            ch.Sb = cx.sb([128, 4, 128], BF16, "Sb%d" % i)
            ch.b_S, ch.b_Sb = Buf(), Buf()
            ch.dg, ch.b_dg = t64("dg")
            ch.ndg, ch.b_ndg = t64("ndg")
            ch.E1, ch.b_E1 = t64("E1")
            ch.E2, ch.b_E2 = t64("E2")
            ch.AB = [cx.sb([64, 8, 64], F32, "AB%d_%d" % (k, i)) for k in range(2)]
            ch.bAB = [Buf(), Buf()]
            ch.Lm = [(ch.AB[k][:, 0:4, :], ch.bAB[k]) for k in range(2)]
            ch.Bm = [(ch.AB[k][:, 4:8, :], ch.bAB[k]) for k in range(2)]
            ch.Q, ch.b_Q = t64("Q")
            ch.Qb, ch.b_Qb = t64("Qb", BF16)
            ch.at, ch.b_at = t64("at", BF16)
            ch.kbg, ch.b_kbg = t64("kbg", BF16, 128)
            ch.kg, ch.b_kg = t64("kg", BF16, 128)
            ch.vb, ch.b_vb = t64("vb", BF16, 128)
            ch.u, ch.b_u = t64("u", F32, 128)
            ch.vn, ch.b_vn = t64("vn", BF16, 128)
            ch.ot, ch.b_ot = t64("ot", F32, 128)
            ch.o2, ch.b_o2 = t64("o2", F32, 128)
            ch.wT = cx.sb([128, 4, 64], BF16, "wT%d" % i)
            ch.b_wT = Buf()
            ch.St = cx.sb([128, 4, 128], F32, "St%d" % i)
            ch.b_St = Buf()
            ch.ss = cx.sb([64, 4], F32, "ss%d" % i)
            ch.b_ss = Buf()
            ch.gzt = cx.sb([64, 512], BF16, "gzt%d" % i)
            ch.b_gzt = Buf()
            ch.ogd = cx.sb([64, 512], BF16, "ogd%d" % i)
            ch.b_ogd = Buf()
            ch.ogT = cx.sb([128, 4, 64], BF16, "ogT%d" % i)
            ch.b_ogT = Buf()
            return ch

        chains = [mk_chain(0), mk_chain(1)]

        def step(ch, gi, d, s0, NC, s):
            Y, bY = ch.Y, ch.bY
            cols = slice(d * 16 + gi * 4, d * 16 + gi * 4 + 4)
            c = s if d == 0 else NC - 1 - s
            finalize = (c >= NC // 2) if d == 0 else (c < NC // 2)
            t0 = s0 + c * 64
            cc = t0 // 64
            kc_ = qkv[:, 1, :, t0:t0 + 64]
            qc_ = qkv[:, 0, :, t0:t0 + 64]
            vc_ = qkv[:, 2, :, t0:t0 + 64]
            h64 = lambda ap: ap.rearrange("p (h n) -> p h n", n=64)
            h128 = lambda ap: ap.rearrange("p (h n) -> p h n", n=128)
            for hh in range(4):
                P.op("pe", lambda e, hh=hh: e.matmul(Y[0][0:64, hh * 128:hh * 128 + 64], lhsT=kc_[:, hh, :],
                                                     rhs=kc_[:, hh, :], start=True, stop=True),
                     reads=[b_qkv], writes=[bY[0]])
                P.op("pe", lambda e, hh=hh: e.matmul(Y[0][0:64, hh * 128 + 64:hh * 128 + 128], lhsT=kc_[:, hh, :],
                                                     rhs=qc_[:, hh, :], start=True, stop=True),
                     reads=[b_qkv], writes=[bY[0]])
            GA = h128(Y[0][0:64, :])
            P.op("dve", lambda e: e.tensor_tensor(out=ch.dg[:], in0=id4[:], in1=bc(gcs[:, cc, cols], 64), op=ALU.mult),
                 reads=[b_c, b_sc], writes=[ch.b_dg])
            for hh in range(4):
                P.op("pe", lambda e, hh=hh: e.matmul(Y[1][0:64, hh * 64:(hh + 1) * 64], lhsT=ones[:, 0:64],
                                                     rhs=ch.dg[:, hh, :], start=True, stop=True),
                     reads=[ch.b_dg, b_c], writes=[bY[1]])
            yield
            Dm = h64(Y[1][0:64, 0:256])
            P.op("dve", lambda e: e.tensor_tensor(
                out=ch.E1[:], in0=cst[:, 3 + d, :].unsqueeze(1).broadcast_to([64, 4, 64]), in1=Dm, op=ALU.subtract),
                reads=[bY[1], b_c], writes=[ch.b_E1])
            P.op("dve", lambda e: e.tensor_tensor(
                out=ch.E2[:], in0=Dm, in1=cst[:, 5 + d, :].unsqueeze(1).broadcast_to([64, 4, 64]), op=ALU.add),
                reads=[bY[1], b_c], writes=[ch.b_E2])
            P.op("dve", lambda e: e.tensor_tensor(out=ch.E1[:], in0=ch.E1[:], in1=bc(gcs[:, cc, cols], 64), op=ALU.add),
                 reads=[ch.b_E1, b_sc], writes=[ch.b_E1])
            P.op("dve", lambda e: e.tensor_tensor(out=ch.E2[:], in0=ch.E2[:], in1=bc(gcs[:, cc, cols], 64),
                                                  op=ALU.subtract), reads=[ch.b_E2, b_sc], writes=[ch.b_E2])
            P.op("act", lambda e: e.activation(out=ch.E1[:], in_=ch.E1[:], func=AF.Exp), reads=[ch.b_E1],
                 writes=[ch.b_E1])
            P.op("act", lambda e: e.activation(out=ch.E2[:], in_=ch.E2[:], func=AF.Exp), reads=[ch.b_E2],
                 writes=[ch.b_E2])
            yield
            A0, b_A0 = ch.Lm[0]
            P.op("dve", lambda e: e.tensor_tensor(out=A0, in0=GA[:, :, 0:64], in1=ch.E1[:], op=ALU.mult),
                 reads=[bY[0], ch.b_E1], writes=[b_A0])
            P.op("dve", lambda e: e.tensor_tensor(out=A0, in0=A0, in1=bc(bg[:, cc, cols], 64), op=ALU.mult),
                 reads=[b_sc, b_A0], writes=[b_A0])
            P.op("dve", lambda e: e.tensor_tensor(out=ch.at[:], in0=GA[:, :, 64:128], in1=ch.E2[:], op=ALU.mult),
                 reads=[bY[0], ch.b_E2], writes=[ch.b_at])
            B0, b_B0 = ch.Bm[0]
            for hh in range(4):
                P.op("pe", lambda e, hh=hh: e.transpose(out=Y[1][0:64, hh * 64:(hh + 1) * 64], in_=A0[:, hh, :],
                                                        identity=cst[:, 0, :]), reads=[b_A0, b_c], writes=[bY[1]])
            yield
            for hh in range(4):
                P.op("pe", lambda e, hh=hh: e.transpose(out=pTr[0:64, hh * 128:(hh + 1) * 128], in_=kc_[:, hh, :],
                                                        identity=identb[:]), reads=[b_qkv, b_c], writes=[b_pTr])
                P.op("pe", lambda e, hh=hh: e.transpose(out=pTr[0:64, 512 + hh * 128:512 + (hh + 1) * 128],
                                                        in_=vc_[:, hh, :], identity=identb[:]),
                     reads=[b_qkv, b_c], writes=[b_pTr])
            P.op("act", lambda e: e.copy(out=B0, in_=Dm), reads=[bY[1]], writes=[b_B0])
            P.op("dve", lambda e: e.tensor_tensor(out=ch.Q[:], in0=id4[:], in1=B0, op=ALU.subtract),
                 reads=[b_c, b_B0], writes=[ch.b_Q])
            ktr = h128(pTr[0:64, 0:512])
            vtr = h128(pTr[0:64, 512:1024])
            P.op("dve", lambda e: e.tensor_tensor(out=ch.kbg[:], in0=ktr, in1=bc(bege[:, cc, cols], 128), op=ALU.mult),
                 reads=[b_pTr, b_sc], writes=[ch.b_kbg])
            P.op("dve", lambda e: e.tensor_tensor(out=ch.kg[:], in0=ktr, in1=bc(ekl[:, cc, cols], 128), op=ALU.mult),
                 reads=[b_pTr, b_sc], writes=[ch.b_kg])
            P.op("dve", lambda e: e.tensor_tensor(out=ch.vb[:], in0=vtr, in1=bc(bg[:, cc, cols], 128), op=ALU.mult),
                 reads=[b_pTr, b_sc], writes=[ch.b_vb])
            yield

            def qupd(Ak, b_Ak):
                for hh in range(4):
                    P.op("pe", lambda e, hh=hh: e.matmul(ch.YQ[0:64, ch.oQ + hh * 64:ch.oQ + (hh + 1) * 64],
                                                         lhsT=Ak[:, hh, :], rhs=ch.Q[:, hh, :], start=True, stop=True),
                         reads=[b_Ak, ch.b_Q], writes=[ch.bYQ])

            def qadd():
                P.op("dve", lambda e: e.tensor_tensor(out=ch.Q[:], in0=ch.Q[:],
                                                      in1=h64(ch.YQ[0:64, ch.oQ:ch.oQ + 256]), op=ALU.add),
                     reads=[ch.bYQ, ch.b_Q], writes=[ch.b_Q])

            cur = 0
            for lvl in range(1, 6):
                A, b_A = ch.Lm[cur]
                B, b_B = ch.Bm[cur]
                An, b_An = ch.Lm[1 - cur]
                Bn, b_Bn = ch.Bm[1 - cur]
                for hh in range(4):
                    P.op("pe", lambda e, hh=hh, A=A, B=B: e.matmul(Y[2][0:64, hh * 64:(hh + 1) * 64], lhsT=B[:, hh, :],
                                                                   rhs=A[:, hh, :], start=True, stop=True),
                         reads=[b_A, b_B], writes=[bY[2]])
                if lvl < 5:
                    for hh in range(4):
                        P.op("pe", lambda e, hh=hh, A=A, B=B: e.matmul(
                            ch.YB[0:64, ch.oB + hh * 64:ch.oB + (hh + 1) * 64], lhsT=A[:, hh, :], rhs=B[:, hh, :],
                            start=True, stop=True), reads=[b_A, b_B], writes=[ch.bYB])
                if lvl >= 2:
                    qupd(A, b_A)
                yield
                if lvl < 5 and not DN_ALT:
                    ABn = ch.AB[1 - cur]
                    P.op("act", lambda e, ABn=ABn: e.copy(out=ABn[:], in_=h64(Y[2][0:64, 0:512])), reads=[bY[2]],
                         writes=[b_An])
                else:
                    P.op("act", lambda e, An=An: e.copy(out=An, in_=h64(Y[2][0:64, 0:256])), reads=[bY[2]],
                         writes=[b_An])
                    if lvl < 5:
                        P.op("dve", lambda e, Bn=Bn: e.tensor_copy(out=Bn, in_=h64(ch.YB[0:64, ch.oB:ch.oB + 256])),
                             reads=[ch.bYB], writes=[b_Bn])
                if lvl >= 2:
                    qadd()
                yield
                cur = 1 - cur
            A, b_A = ch.Lm[cur]
            qupd(A, b_A)
            yield
            qadd()
            P.op("act", lambda e: e.copy(out=ch.Qb[:], in_=ch.Q[:]), reads=[ch.b_Q], writes=[ch.b_Qb])
            yield
            for hh in range(4):
                P.op("pe", lambda e, hh=hh: e.matmul(Y[0][0:64, hh * 128:(hh + 1) * 128], lhsT=ch.Qb[:, hh, :],
                                                     rhs=ch.vb[:, hh, :], start=True, stop=True),
                     reads=[ch.b_Qb, ch.b_vb], writes=[bY[0]])
                P.op("pe", lambda e, hh=hh: e.matmul(Y[1][:, hh * 64:(hh + 1) * 64], lhsT=ch.kbg[:, hh, :],
                                                     rhs=ch.Qb[:, hh, :], start=True, stop=True),
                     reads=[ch.b_Qb, ch.b_kbg], writes=[bY[1]])
            yield
            P.op("act", lambda e: e.copy(out=ch.u[:], in_=h128(Y[0][0:64, :])), reads=[bY[0]], writes=[ch.b_u])
            P.op("act", lambda e: e.copy(out=ch.wT[:], in_=h64(Y[1][:, 0:256])), reads=[bY[1]], writes=[ch.b_wT])
            yield
            for hh in range(4):
                P.op("pe", lambda e, hh=hh: e.matmul(Y[0][0:64, hh * 128:(hh + 1) * 128], lhsT=ch.wT[:, hh, :],
                                                     rhs=ch.Sb[:, hh, :], start=True, stop=True),
                     reads=[ch.b_wT, ch.b_Sb], writes=[bY[0]])
            for hh in range(4):
                P.op("pe", lambda e, hh=hh: e.matmul(Y[2][0:64, hh * 128:(hh + 1) * 128], lhsT=qc_[:, hh, :],
                                                     rhs=ch.Sb[:, hh, :], start=True, stop=True),
                     reads=[b_qkv, ch.b_Sb], writes=[bY[2]])
            yield
            P.op("dve", lambda e: e.tensor_tensor(out=ch.vn[:], in0=ch.u[:], in1=h128(Y[0][0:64, :]), op=ALU.subtract),
                 reads=[ch.b_u, bY[0]], writes=[ch.b_vn])
            P.op("dve", lambda e: e.tensor_tensor(out=ch.ot[:], in0=h128(Y[2][0:64, :]), in1=bc(egc[:, cc, cols], 128),
                                                  op=ALU.mult), reads=[bY[2], b_sc], writes=[ch.b_ot])
            yield
            for hh in range(4):
                P.op("pe", lambda e, hh=hh: e.matmul(Y[1][0:64, hh * 128:(hh + 1) * 128], lhsT=ch.at[:, hh, :],
                                                     rhs=ch.vn[:, hh, :], start=True, stop=True),
                     reads=[ch.b_at, ch.b_vn], writes=[bY[1]])
                P.op("pe", lambda e, hh=hh: e.matmul(Y[2][:, hh * 128:(hh + 1) * 128], lhsT=ch.kg[:, hh, :],
                                                     rhs=ch.vn[:, hh, :], start=True, stop=True),
                     reads=[ch.b_kg, ch.b_vn], writes=[bY[2]])
            P.op("dve", lambda e: e.tensor_tensor(out=ch.St[:], in0=ch.S[:],
                                                  in1=eglS[:, cc, cols].unsqueeze(2).broadcast_to([128, 4, 128]),
                                                  op=ALU.mult), reads=[ch.b_S, b_sc], writes=[ch.b_St])
            yield
            P.op("dve", lambda e: e.tensor_tensor(out=ch.S[:], in0=ch.St[:], in1=h128(Y[2][:, :]), op=ALU.add),
                 reads=[ch.b_St, bY[2]], writes=[ch.b_S])
            P.op("act", lambda e: e.copy(out=ch.Sb[:], in_=ch.S[:]), reads=[ch.b_S], writes=[ch.b_Sb])
            ofc = h128(of[:, cc, :])
            if not finalize:
                P.op("dve", lambda e: e.tensor_tensor(out=ofc, in0=ch.ot[:], in1=h128(Y[1][0:64, :]), op=ALU.add),
                     reads=[ch.b_ot, bY[1]], writes=[b_of[cc]])
                yield
            else:
                P.dma("sp", lambda e: e.dma_start(out=ch.gzt[:],
                                                  in_=G["dzg"][tok0 + t0:tok0 + t0 + 64, gi * 512:(gi + 1) * 512]),
                      writes=[ch.b_gzt])
                P.op("dve", lambda e: e.tensor_tensor(out=ch.ot[:], in0=ch.ot[:], in1=h128(Y[1][0:64, :]), op=ALU.add),
                     reads=[ch.b_ot, bY[1]], writes=[ch.b_ot])
                P.op("dve", lambda e: e.tensor_tensor(out=ch.o2[:], in0=ch.ot[:], in1=ofc, op=ALU.add),
                     reads=[ch.b_ot, b_of[cc]], writes=[ch.b_o2])
                yield
                P.op("dve", lambda e: e.tensor_tensor(out=ch.ot[:], in0=ch.o2[:], in1=ch.o2[:], op=ALU.mult),
                     reads=[ch.b_o2], writes=[ch.b_ot])
                P.op("dve", lambda e: e.reduce_sum(out=ch.ss[:], in_=ch.ot[:], axis=AX.X), reads=[ch.b_ot],
                     writes=[ch.b_ss])
                P.op("act", lambda e: e.activation(out=ch.ss[:], in_=ch.ss[:], func=AF.Sqrt, bias=RMS_EPS,
                                                   scale=1.0 / 128.0), reads=[ch.b_ss], writes=[ch.b_ss])
                yield
                P.op("dve", lambda e: e.reciprocal(out=ch.ss[:], in_=ch.ss[:]), reads=[ch.b_ss], writes=[ch.b_ss])
                P.op("dve", lambda e: e.tensor_tensor(out=ch.o2[:], in0=ch.o2[:], in1=bc(ch.ss[:], 128), op=ALU.mult),
                     reads=[ch.b_ss, ch.b_o2], writes=[ch.b_o2])
                P.op("dve", lambda e: e.tensor_tensor(out=ch.o2[:], in0=ch.o2[:],
                                                      in1=h128(dnw[:, :]), op=ALU.mult),
                     reads=[b_dnw, ch.b_o2], writes=[ch.b_o2])
                P.op("dve", lambda e: e.tensor_tensor(out=h128(ch.ogd[:]), in0=ch.o2[:], in1=h128(ch.gzt[:]),
                                                      op=ALU.mult), reads=[ch.b_o2, ch.b_gzt], writes=[ch.b_ogd])
                yield
                for hh in range(4):
                    P.op("pe", lambda e, hh=hh: e.transpose(out=pTr[:, hh * 64:(hh + 1) * 64],
                                                            in_=ch.ogd[:, hh * 128:(hh + 1) * 128],
                                                            identity=identb[0:64, 0:64]),
                         reads=[ch.b_ogd, b_c], writes=[b_pTr])
                P.op("act", lambda e: e.copy(out=ch.ogT[:].rearrange("p h n -> p (h n)"), in_=pTr[:, 0:256]),
                     reads=[b_pTr], writes=[ch.b_ogT])
                P.dma(STQ, lambda e: e.dma_start(
                    out=G["og"][4096 + gi * 512:4096 + (gi + 1) * 512, tok0 + t0:tok0 + t0 + 64].rearrange(
                        "(h p) t -> p h t", p=128), in_=ch.ogT[:]), reads=[ch.b_ogT])
                yield

        for gi in range(ngroups):
            P.dma("sp", lambda e, gi=gi: e.dma_start(out=dnw[:], in_=G["dnw_bc"][:, gi * 512:(gi + 1) * 512]),
                  writes=[b_dnw])
            for X in range(3):
                P.dma("sp", lambda e, X=X, gi=gi: e.dma_start(
                    out=qkv[:, X, :, :], in_=G["dqkv"][X, gi * 4:(gi + 1) * 4, :, tok0:tok0 + NT].rearrange("h p t -> p h t")),
                    writes=[b_qkv])
            for (s0, T, kind, sidx) in seqs:
                NC = T // 64
                for d in range(2):
                    ch = chains[d]
                    if kind == "S":
                        P.dma("sp", lambda e, d=d, gi=gi, ch=ch: e.dma_start(
                            out=ch.S[:], in_=G["state_dn"][d, gi * 4:(gi + 1) * 4].rearrange("h k v -> k h v")),
                            writes=[ch.b_S])
                    else:
                        P.op("pool", lambda e, ch=ch: e.memset(ch.S[:], 0.0), writes=[ch.b_S])
                    P.op("act", lambda e, ch=ch: e.copy(out=ch.Sb[:], in_=ch.S[:]), reads=[ch.b_S], writes=[ch.b_Sb])
                for s in range(NC):
                    gens = [step(chains[0], gi, 0, s0, NC, s), step(chains[1], gi, 1, s0, NC, s)]
                    live = [True, True]
                    if DN_SEQ:
                        for g_ in gens:
                            for _ in g_:
                                pass
                        live = [False, False]
                    while any(live):
                        for k in range(2):
                            if live[k]:
                                try:
                                    next(gens[k])
                                except StopIteration:
                                    live[k] = False
                if kind == "P":
                    for d in range(2):
                        P.dma(STQ, lambda e, d=d, gi=gi, sidx=sidx: e.dma_start(
                            out=G["new_dn"][sidx, d, gi * 4:(gi + 1) * 4].rearrange("h k v -> k h v"), in_=chains[d].S[:]),
                            reads=[chains[d].b_S])
        P.wait_all_dma()
        with nc.Block() as block:
            P.emit(block)


ALPHA = 2.0 ** 0.25


def stage_modrow(nc, sync, G):
    with ExitStack() as es:
        cx = Ctx(nc, es)
        P = Prog(sync)
        scT = cx.sb([128, KC, 2], F32, "scT")
        brow = cx.sb([2, 4096], F32, "brow")
        mrow = cx.sb([2, 4096], F32, "mrow")
        wb = [cx.sb([128, KC, 512], F32, "wada") for _ in range(2)]
        ps = [cx.ps([128, 512], F32, "psm") for _ in range(2)]
        b_sc, b_br, b_mr = Buf(), Buf(), Buf()
        b_wb = [Buf(), Buf()]
        b_ps = [Buf(), Buf()]
        P.dma("sp", lambda e: e.dma_start(out=scT[:], in_=G["condT"][:, :, :]), writes=[b_sc])
        P.dma("sp", lambda e: e.dma_start(out=brow[:], in_=G["b_adarow"][:, :]), writes=[b_br])
        P.op("act", lambda e: e.activation(out=scT[:], in_=scT[:], func=AF.Silu), reads=[b_sc], writes=[b_sc])
        wv = G["w_ada"].rearrange("(kc p) n -> p kc n", p=128)
        for i, nb in enumerate(list(range(8, 12)) + list(range(20, 24))):
            sl = i % 2
            P.dma("sp", lambda e, nb=nb, sl=sl: e.dma_start(out=wb[sl][:], in_=wv[:, :, nb * 512:(nb + 1) * 512]),
                  writes=[b_wb[sl]])
            mm_acc(P, ps[sl][0:2, :], b_ps[sl], [(scT[:, kc, :], wb[sl][:, kc, :]) for kc in range(KC)],
                   [b_wb[sl], b_sc])
            P.op("dve", lambda e, i=i, sl=sl: e.tensor_tensor(out=mrow[:, i * 512:(i + 1) * 512], in0=ps[sl][0:2, :],
                                                               in1=brow[:, i * 512:(i + 1) * 512], op=ALU.add),
                 reads=[b_ps[sl], b_br], writes=[b_mr])
        P.dma(STQ, lambda e: e.dma_start(out=G["modrow"][:, :], in_=mrow[:]), reads=[b_mr])
        P.wait_all_dma()
        with nc.Block() as block:
            P.emit(block)


def stage_d1(nc, sync, G):
    with ExitStack() as es:
        cx = Ctx(nc, es)
        P = Prog(sync)
        W = WStream(P, cx, nbuf=4, nstg=4, engines=("act", "dve", "pool"))
        ogr = [cx.sb([128, 32, 512], BF16, "ogr") for _ in range(2)]
        ogd = cx.sb([128, 16, 512], BF16, "ogd")
        sgt = [cx.sb([128, 8, 512], BF16, "sgt") for _ in range(2)]
        b_ogr = [Buf(), Buf()]
        b_ogd = Buf()
        b_sg = [Buf(), Buf()]
        yo = [cx.sb([128, 4, 512], BF16, "yo") for _ in range(2)]
        b_yo = [Buf(), Buf()]
        t1 = cx.sb([128, 512], F32, "t1")
        t2 = cx.sb([128, 512], F32, "t2")
        b_t1, b_t2 = Buf(), Buf()
        pr = [cx.ps([128, 512], F32, "pr") for _ in range(4)]
        pd = [cx.ps([128, 512], F32, "pd") for _ in range(4)]
        b_pr = [Buf() for _ in range(4)]
        b_pd = [Buf() for _ in range(4)]
        it = 0
        for cb in range(4):
            cs_ = slice(cb * 512, (cb + 1) * 512)
            w0, b_w0 = W.load(G["w_br"][0:2048, cs_], 512)
            w1, b_w1 = W.load(G["w_br"][2048:4096, cs_], 512)
            w2, b_w2 = W.load(G["w_bd"][:, cs_], 512)
            for tt in range(NTOK // 512):
                tsl = slice(tt * 512, (tt + 1) * 512)
                k = it % 2
                it += 1
                orr, b_orr = ogr[k], b_ogr[k]
                sg_, b_sg_ = sgt[k], b_sg[k]
                P.dma("sp", lambda e, tsl=tsl, orr=orr: e.dma_start(
                    out=orr[:], in_=G["og"][0:4096, tsl].rearrange("(a p) t -> p a t", p=128)), writes=[b_orr])
                P.dma("sp", lambda e, tsl=tsl: e.dma_start(
                    out=ogd[:], in_=G["og"][4096:6144, tsl].rearrange("(a p) t -> p a t", p=128)), writes=[b_ogd])
                P.dma("sp", lambda e, tsl=tsl, sg_=sg_, cb=cb: e.dma_start(
                    out=sg_[:, 0:4, :], in_=G["sg"][cb * 512:(cb + 1) * 512, tsl].rearrange("(a p) t -> p a t", p=128)),
                    writes=[b_sg_])
                P.dma("sp", lambda e, tsl=tsl, sg_=sg_, cb=cb: e.dma_start(
                    out=sg_[:, 4:8, :],
                    in_=G["sg"][2048 + cb * 512:2048 + (cb + 1) * 512, tsl].rearrange("(a p) t -> p a t", p=128)),
                    writes=[b_sg_])
                o2, b_o2 = yo[k], b_yo[k]
                for sub in range(4):
                    ss_ = slice(sub * 128, (sub + 1) * 128)
                    pairs = [(w0[:, kc, ss_], orr[:, kc, :]) for kc in range(KC)]
                    pairs += [(w1[:, kc, ss_], orr[:, 16 + kc, :]) for kc in range(KC)]
                    mm_acc(P, pr[sub][:, :], b_pr[sub], pairs, [b_w0, b_w1, b_orr])
                for sub in range(4):
                    ss_ = slice(sub * 128, (sub + 1) * 128)
                    mm_acc(P, pd[sub][:, :], b_pd[sub], [(w2[:, kc, ss_], ogd[:, kc, :]) for kc in range(KC)],
                           [b_w2, b_ogd])
                for sub in range(4):
                    P.op("dve", lambda e, sub=sub, sg_=sg_: e.tensor_tensor(out=t1[:], in0=pr[sub][:, :], in1=sg_[:, sub, :],
                                                                            op=ALU.mult), reads=[b_pr[sub], b_sg_],
                         writes=[b_t1])
                    P.op("dve", lambda e, sub=sub, sg_=sg_: e.tensor_tensor(out=t2[:], in0=pd[sub][:, :],
                                                                            in1=sg_[:, 4 + sub, :], op=ALU.mult),
                         reads=[b_pd[sub], b_sg_], writes=[b_t2])
                    P.op("pool", lambda e, o2=o2, sub=sub: e.tensor_tensor(out=o2[:, sub, :], in0=t1[:], in1=t2[:],
                                                                             op=ALU.add), reads=[b_t1, b_t2], writes=[b_o2])
                P.dma(STQ, lambda e, o2=o2, cs_=cs_, tsl=tsl: e.dma_start(
                    out=G["yT"][cs_, tsl].rearrange("(a p) t -> p a t", p=128), in_=o2[:]), reads=[b_o2])
        P.wait_all_dma()
        with nc.Block() as block:
            P.emit(block)


def row_bcast(P, cx, G, c, lo, dst, b_dst, sel, b_sel, mr, b_mr, ps, b_ps):
    P.dma("sp", lambda e: e.dma_start(out=mr[:], in_=G["modrow"][:, lo:lo + 2048]), writes=[b_mr])
    for j in range(4):
        P.op("pe", lambda e, j=j: e.matmul(ps[:, :], lhsT=sel[:, c, :], rhs=mr[:, j * 512:(j + 1) * 512], start=True,
                                            stop=True), reads=[b_sel, b_mr], writes=[b_ps])
        P.op("act", lambda e, j=j: e.copy(out=dst[:, j * 512:(j + 1) * 512], in_=ps[:, :]), reads=[b_ps], writes=[b_dst])


def ln_affine_store(P, r_t, b_r, scr, gt, bt, b_gb, out_t, b_out):
    ln_rows(P, None, r_t, b_r, out_t, b_out, scr)
    P.op("pool", lambda e: e.tensor_tensor(out=out_t[:], in0=out_t[:], in1=gt[:], op=ALU.mult), reads=[b_gb, b_out],
         writes=[b_out])
    P.op("pool", lambda e: e.tensor_tensor(out=out_t[:], in0=out_t[:], in1=bt[:], op=ALU.add), reads=[b_gb, b_out],
         writes=[b_out])


def stage_d2(nc, sync, G):
    with ExitStack() as es:
        cx = Ctx(nc, es)
        P = Prog(sync)
        W = WStream(P, cx, nbuf=2, nstg=4, engines=("act", "act", "pool"))
        yts = [cx.sb([128, KC, 512], BF16, "yt") for _ in range(2)]
        b_yts = [Buf(), Buf()]
        rs = [cx.sb([128, 4, D], F32, "r") for _ in range(2)]
        b_rs = [[Buf() for _ in range(4)] for _ in range(2)]
        grow = cx.sb([128, D], F32, "grow")
        b_grow = Buf()
        lg_ = cx.sb([128, D], F32, "lng")
        lb_ = cx.sb([128, D], F32, "lnb")
        b_gb = Buf()
        x1 = cx.sb([128, D], F32, "x1")
        b_x1 = Buf()
        xn = cx.sb([128, D], BF16, "xn")
        b_xn = Buf()
        h2 = cx.sb([128, KC, 128], BF16, "h2")
        b_h2 = Buf()
        sel = cx.sb([2, 2, 128], F32, "sel")
        mr = cx.sb([2, 2048], F32, "mr")
        b_sel, b_mr = Buf(), Buf()
        ident = cx.sb([128, 128], BF16, "ident")
        scr = {"st": cx.sb([128, 4, 6], F32), "mv": cx.sb([128, 2], F32), "rstd": cx.sb([128, 1], F32), "b_st": Buf()}
        ps = [cx.ps([128, 512], F32, "pm") for _ in range(4)]
        b_ps = [Buf() for _ in range(4)]
        pT = [cx.ps([128, 1024], BF16, "pT") for _ in range(2)]
        b_pT = [Buf(), Buf()]
        modT = G["modT"]
        P.dma("sp", lambda e: e.dma_start(out=sel[:], in_=G["sel"][:, :, :]), writes=[b_sel])
        P.dma("sp", lambda e: e.dma_start(out=ident[:], in_=G["ident_bf"][:, :]), writes=[b_sel])
        P.dma("sp", lambda e: e.dma_start(out=lg_[:], in_=G["ln1g_bc"][:, :]), writes=[b_gb])
        P.dma("sp", lambda e: e.dma_start(out=lb_[:], in_=G["ln1b_bc"][:, :]), writes=[b_gb])
        mtmp = cx.sb([128, 512], F32, "mtmp")
        b_mtmp = Buf()
        state = {"last_c": None}

        def mm_phase(tt):
            c = 0 if tt == 0 else 1
            k = tt % 2
            tsl = slice(tt * 512, (tt + 1) * 512)
            rr, b_rr, yt_, b_yt_ = rs[k], b_rs[k], yts[k], b_yts[k]
            if c != state["last_c"]:
                row_bcast(P, cx, G, c, 0, grow, b_grow, sel, b_sel, mr, b_mr, ps[0], b_ps[0])
                state["last_c"] = c
            P.dma("sp", lambda e: e.dma_start(out=yt_[:], in_=G["yT"][:, tsl].rearrange("(a p) t -> p a t", p=128)),
                  writes=[b_yt_])
            for ts in range(4):
                P.dma("sp", lambda e, ts=ts: e.dma_start(
                    out=rr[:, ts, :], in_=G["x"][tt * 512 + ts * 128:tt * 512 + (ts + 1) * 128, :]), writes=[b_rr[ts]])
            for cb in range(4):
                cs_ = slice(cb * 512, (cb + 1) * 512)
                w, b_w = W.load(G["w_o"][:, cs_], 512)
                for ts in range(4):
                    mm_acc(P, ps[ts][:, :], b_ps[ts], [(yt_[:, kc, ts * 128:(ts + 1) * 128], w[:, kc, :]) for kc in range(KC)],
                           [b_w, b_yt_])
                    P.op("dve", lambda e, ts=ts, cs_=cs_: e.tensor_tensor(out=mtmp[:], in0=ps[ts][:, :], in1=grow[:, cs_],
                                                                           op=ALU.mult),
                         reads=[b_ps[ts], b_grow], writes=[b_mtmp])
                    P.op("dve", lambda e, ts=ts, cs_=cs_: e.scalar_tensor_tensor(
                        out=rr[:, ts, cs_], in0=rr[:, ts, cs_], scalar=ALPHA, in1=mtmp[:], op0=ALU.mult, op1=ALU.add),
                        reads=[b_mtmp, b_rr[ts]], writes=[b_rr[ts]])

        def ln_phase(tt):
            c = 0 if tt == 0 else 1
            k = tt % 2
            rr, b_rr = rs[k], b_rs[k]
            for ts in range(4):
                tok = tt * 512 + ts * 128
                ln_affine_store(P, rr[:, ts, :], b_rr[ts], scr, lg_, lb_, b_gb, x1, b_x1)
                P.dma(STQ, lambda e, tok=tok: e.dma_start(out=G["x1"][tok:tok + 128, :], in_=x1[:]), reads=[b_x1])
                ln_rows(P, None, x1, b_x1, xn, b_xn, scr)
                for g in range(2):
                    for j in range(8):
                        kc = g * 8 + j
                        P.op("pe", lambda e, g=g, j=j, kc=kc: e.transpose(
                            out=pT[g][:, j * 128:(j + 1) * 128], in_=xn[:, kc * 128:(kc + 1) * 128], identity=ident[:]),
                            reads=[b_xn, b_sel], writes=[b_pT[g]])
                    for j in range(8):
                        kc = g * 8 + j
                        P.op("act", lambda e, g=g, j=j, kc=kc, c=c: e.activation(
                            out=h2[:, kc, :], in_=pT[g][:, j * 128:(j + 1) * 128], func=AF.Identity,
                            scale=modT[:, 64 + kc, c:c + 1], bias=modT[:, 48 + kc, c:c + 1]), reads=[b_pT[g]],
                            writes=[b_h2])
                P.dma(STQ, lambda e, tok=tok: e.dma_start(
                    out=G["h2T"][:, tok:tok + 128].rearrange("(a p) t -> p a t", p=128), in_=h2[:]), reads=[b_h2])

        NTT = NTOK // 512
        for tt in range(NTT):
            mm_phase(tt)
            if tt > 0:
                ln_phase(tt - 1)
        ln_phase(NTT - 1)
        P.wait_all_dma()
        with nc.Block() as block:
            P.emit(block)


SEQS_ALL = [(0, 256), (256, 256), (512, 2048)]


def stage_e1(nc, sync, G):
    with ExitStack() as es:
        cx = Ctx(nc, es)
        P = Prog(sync)
        W = WStream(P, cx, nbuf=4, nstg=4, engines=("act", "act", "pool"))
        h2 = cx.sb([128, KC, NTOK], BF16, "h2")
        b_h2 = Buf()
        P.dma("sp", lambda e: e.dma_start(out=h2[:, :, 0:1280], in_=G["h2T"][:, 0:1280].rearrange("(a p) t -> p a t", p=128)),
              writes=[b_h2])
        P.dma("sp", lambda e: e.dma_start(out=h2[:, :, 1280:2560],
                                          in_=G["h2T"][:, 1280:2560].rearrange("(a p) t -> p a t", p=128)), writes=[b_h2])
        cw = cx.sb([128, 88, 4], F32, "cw")
        b_c = Buf()
        P.dma("sp", lambda e: e.dma_start(out=cw[:], in_=G["ffn_cwT"][:, :, :]), writes=[b_c])
        y = cx.sb([128, NTOK], F32, "y")
        za = cx.sb([128, NTOK], F32, "za")
        zv = cx.sb([128, NTOK], F32, "zv")
        go = [cx.sb([128, NTOK], BF16, "go") for _ in range(2)]
        b_y, b_za, b_zv = Buf(), Buf(), Buf()
        b_go = [Buf(), Buf()]
        ps = [cx.ps([128, 512], F32, "pe") for _ in range(5)]
        b_ps = [Buf() for _ in range(5)]

        def conv(zt, b_zt, blk):
            P.op("dve", lambda e: e.tensor_scalar(out=zt[:], in0=y[:], scalar1=cw[:, blk, 1:2], scalar2=cw[:, blk, 3:4],
                                                   op0=ALU.mult, op1=ALU.add), reads=[b_y, b_c], writes=[b_zt])
            for (s0, T) in SEQS_ALL:
                P.op("dve", lambda e, s0=s0, T=T: e.scalar_tensor_tensor(
                    out=zt[:, s0 + 1:s0 + T], in0=y[:, s0:s0 + T - 1], scalar=cw[:, blk, 0:1], in1=zt[:, s0 + 1:s0 + T],
                    op0=ALU.mult, op1=ALU.add), reads=[b_y, b_c, b_zt], writes=[b_zt])
                P.op("dve", lambda e, s0=s0, T=T: e.scalar_tensor_tensor(
                    out=zt[:, s0:s0 + T - 1], in0=y[:, s0 + 1:s0 + T], scalar=cw[:, blk, 2:3], in1=zt[:, s0:s0 + T - 1],
                    op0=ALU.mult, op1=ALU.add), reads=[b_y, b_c, b_zt], writes=[b_zt])

        for jb in range(11):
            wa, b_wa = W.load(G["w_up"][:, jb * 512:(jb + 1) * 512], 512)
            wv_, b_wv = W.load(G["w_up"][:, 5632 + jb * 512:5632 + (jb + 1) * 512], 512)
            for sub in range(4):
                fb = jb * 4 + sub
                ss_ = slice(sub * 128, (sub + 1) * 128)
                for (wt, b_w, zt, b_zt, blk) in ((wa, b_wa, za, b_za, fb), (wv_, b_wv, zv, b_zv, 44 + fb)):
                    for tt in range(5):
                        mm_acc(P, ps[tt][:, :], b_ps[tt],
                               [(wt[:, kc, ss_], h2[:, kc, tt * 512:(tt + 1) * 512]) for kc in range(KC)], [b_w, b_h2])
                        P.op("act", lambda e, tt=tt: e.copy(out=y[:, tt * 512:(tt + 1) * 512], in_=ps[tt][:, :]),
                             reads=[b_ps[tt]], writes=[b_y])
                    conv(zt, b_zt, blk)
                P.op("act", lambda e: e.activation(out=za[:], in_=za[:], func=AF.Silu), reads=[b_za], writes=[b_za])
                o2, b_o2 = go[fb % 2], b_go[fb % 2]
                P.op("dve", lambda e, o2=o2: e.tensor_tensor(out=o2[:], in0=za[:], in1=zv[:], op=ALU.mult),
                     reads=[b_za, b_zv], writes=[b_o2])
                P.dma(STQ, lambda e, o2=o2, fb=fb: e.dma_start(out=G["gT"][fb * 128:(fb + 1) * 128, :], in_=o2[:]),
                      reads=[b_o2])
        P.wait_all_dma()
        with nc.Block() as block:
            P.emit(block)


def stage_e2(nc, sync, G):
    with ExitStack() as es:
        cx = Ctx(nc, es)
        P = Prog(sync)
        W = WStream(P, cx, nbuf=4, nstg=4, engines=("act", "dve", "act", "pool"))
        gt = cx.sb([128, 44, 512], BF16, "gt")
        b_gt = Buf()
        r = cx.sb([128, 4, D], F32, "r")
        b_r = [Buf() for _ in range(4)]
        grow = cx.sb([128, D], F32, "grow")
        b_grow = Buf()
        lg_ = cx.sb([128, D], F32, "lng")
        lb_ = cx.sb([128, D], F32, "lnb")
        b_gb = Buf()
        tmp = cx.sb([128, 512], F32, "tmp")
        b_tmp = Buf()
        yo = cx.sb([128, D], F32, "yo")
        b_yo = Buf()
        sel = cx.sb([2, 2, 128], F32, "sel")
        mr = cx.sb([2, 2048], F32, "mr")
        b_sel, b_mr = Buf(), Buf()
        scr = {"st": cx.sb([128, 4, 6], F32), "mv": cx.sb([128, 2], F32), "rstd": cx.sb([128, 1], F32), "b_st": Buf()}
        ps = [cx.ps([128, 512], F32, "pm") for _ in range(8)]
        b_ps = [Buf() for _ in range(8)]
        P.dma("sp", lambda e: e.dma_start(out=sel[:], in_=G["sel"][:, :, :]), writes=[b_sel])
        P.dma("sp", lambda e: e.dma_start(out=lg_[:], in_=G["ln2g_bc"][:, :]), writes=[b_gb])
        P.dma("sp", lambda e: e.dma_start(out=lb_[:], in_=G["ln2b_bc"][:, :]), writes=[b_gb])
        last_c = None
        KG = [(0, 16), (16, 16), (32, 12)]
        for tt in range(NTOK // 512):
            c = 0 if tt == 0 else 1
            tsl = slice(tt * 512, (tt + 1) * 512)
            if c != last_c:
                row_bcast(P, cx, G, c, 2048, grow, b_grow, sel, b_sel, mr, b_mr, ps[0], b_ps[0])
                last_c = c
            P.dma("sp", lambda e, tsl=tsl: e.dma_start(out=gt[:], in_=G["gT"][:, tsl].rearrange("(a p) t -> p a t", p=128)),
                  writes=[b_gt])
            for ts in range(4):
                P.dma("sp", lambda e, ts=ts, tt=tt: e.dma_start(
                    out=r[:, ts, :], in_=G["x1"][tt * 512 + ts * 128:tt * 512 + (ts + 1) * 128, :]), writes=[b_r[ts]])
            for cb in range(4):
                cs_ = slice(cb * 512, (cb + 1) * 512)
                pb0 = (cb % 2) * 4
                for gi, (k0, kn) in enumerate(KG):
                    w, b_w = W.load(G["w_down"][k0 * 128:(k0 + kn) * 128, cs_], 512, kcn=kn)
                    for ts in range(4):
                        for kc in range(kn):
                            P.op("pe", lambda e, ts=ts, kc=kc, k0=k0, w=w, gi=gi, kn=kn, pb0=pb0: e.matmul(
                                ps[pb0 + ts][:, :], lhsT=gt[:, k0 + kc, ts * 128:(ts + 1) * 128], rhs=w[:, kc, :],
                                start=(gi == 0 and kc == 0), stop=(gi == 2 and kc == kn - 1)),
                                reads=[b_w, b_gt], writes=[b_ps[pb0 + ts]])
                for ts in range(4):
                    P.op("dve", lambda e, ts=ts, cs_=cs_, pb0=pb0: e.tensor_tensor(
                        out=tmp[:], in0=ps[pb0 + ts][:, :], in1=grow[:, cs_], op=ALU.mult), reads=[b_ps[pb0 + ts], b_grow],
                        writes=[b_tmp])
                    P.op("dve", lambda e, ts=ts, cs_=cs_: e.scalar_tensor_tensor(
                        out=r[:, ts, cs_], in0=r[:, ts, cs_], scalar=ALPHA, in1=tmp[:], op0=ALU.mult, op1=ALU.add),
                        reads=[b_tmp, b_r[ts]], writes=[b_r[ts]])
            for ts in range(4):
                tok = tt * 512 + ts * 128
                ln_affine_store(P, r[:, ts, :], b_r[ts], scr, lg_, lb_, b_gb, yo, b_yo)
                P.dma(STQ, lambda e, tok=tok: e.dma_start(out=G["y"][tok:tok + 128, :], in_=yo[:]), reads=[b_yo])
        P.wait_all_dma()
        with nc.Block() as block:
            P.emit(block)


def build(debug=None):
    nc = bass.Bass("TRN2", target_bir_lowering=False)
    G = {}

    def din(name, shape, dt=F32):
        G[name] = nc.dram_tensor(name, list(shape), dt, kind="ExternalInput").ap()

    def dout(name, shape, dt=F32):
        G[name] = nc.dram_tensor(name, list(shape), dt, kind="ExternalOutput").ap()

    def dscr(name, shape, dt=F32):
        G[name] = nc.dram_tensor(name, list(shape), dt, kind="ExternalOutput" if debug else "Internal").ap()

    din("x", [NTOK, D])
    din("condT", [128, KC, 2])
    din("w_ada", [D, 6 * D])
    din("b_adaT", [128, 96])
    din("b_adarow", [2, 6 * D])
    din("ident_bf", [128, 128], BF16)
    din("ones_bf", [128, 128], BF16)
    din("w_in", [D, 24640])
    din("lg_bc", [128, 16])
    din("ret_cst", [128, 6, 128])
    din("ret_pcol", [128, 3])
    din("rnw_bc", [128, 4096])
    din("rope_cs", [128, 2, 2048])
    din("state_ret", [2, 8, 256, 512])
    din("state_dn", [2, 16, 128, 128])
    din("dn_cwT", [128, 48, 3])
    din("dn_ab", [64, 2, 32])
    din("dn_cst", [64, 7, 64])
    din("dnw_bc", [64, 2048])
    din("sel", [2, 2, 128])
    din("w_br", [4096, D])
    din("w_bd", [D, D])
    din("w_o", [D, D])
    din("ln1g_bc", [128, D])
    din("ln1b_bc", [128, D])
    din("w_up", [D, 11264])
    din("ffn_cwT", [128, 88, 4])
    din("w_down", [5632, D])
    din("ln2g_bc", [128, D])
    din("ln2b_bc", [128, D])
    dout("y", [NTOK, D])
    dout("new_ret", [2, 2, 8, 256, 512])
    dout("new_dn", [2, 2, 16, 128, 128])
    dscr("og", [6144, NTOK], BF16)
    dscr("sg", [4096, NTOK], BF16)
    dscr("sb_scr", [16, 2, 128, 512], BF16)
    dscr("dqkv", [3, 16, 128, NTOK], BF16)
    dscr("dzg", [NTOK, 2048], BF16)
    dscr("dbg", [64, NTOK // 64, 64], F32)
    dscr("modrow", [2, 4096], F32)
    dscr("yT", [D, NTOK], BF16)
    dscr("x1", [NTOK, D], F32)
    dscr("h2T", [D, NTOK], BF16)
    dscr("gT", [5632, NTOK], BF16)
    upto = debug if isinstance(debug, str) else "all"
    order = ["ada", "ret", "gates", "dnproj", "dnrec", "d1", "d2", "e1", "all"]
    lim = order.index(upto)
    with ExitStack() as es:
        sync = Sync(nc, es)
        modT = es.enter_context(nc.sbuf_tensor("modT", [128, 96, 2], F32))
        G["modT"] = modT
        stage_ada(nc, sync, G)
        passes = [(512, 2048, 1, [(0, 2048, "S", None)], True),
                  (0, 512, 0, [(0, 256, "P", 0), (256, 256, "P", 1)], False)]
        if lim >= 1:
            for (tok0, NT, cond, seqs, rope) in passes:
                with ExitStack() as es2:
                    hT = es2.enter_context(nc.sbuf_tensor("hT%d" % tok0, [128, KC, 2048], BF16))
                    stage_ln1(nc, sync, G, hT, tok0, NT // 128, cond)
                    stage_ret(nc, sync, G, hT, None, NT, seqs, tok0, rope)
                    if lim >= 2:
                        stage_gates(nc, sync, G, hT, NT, tok0)
                    if lim >= 3:
                        stage_dnproj(nc, sync, G, hT, NT, seqs, tok0)
                if lim >= 4:
                    stage_dnrec(nc, sync, G, NT, seqs, tok0)
        if lim >= 5:
            stage_d1(nc, sync, G)
        if lim >= 6:
            stage_d2(nc, sync, G)
        if lim >= 7:
            stage_e1(nc, sync, G)
        if lim >= 8:
            stage_e2(nc, sync, G)
    return nc


_PERM = np.concatenate([np.arange(0, 256, 2), np.arange(1, 256, 2)])


def shared_inputs(inp):
    f = np.float32
    m = {}
    m["w_ada"] = np.ascontiguousarray(inp["w_ada"][0])
    b = inp["b_ada"][0]
    m["b_adaT"] = np.ascontiguousarray(b.reshape(96, 128).T)
    m["b_adarow"] = np.ascontiguousarray(np.stack([b, b], 0))
    m["ident_bf"] = np.eye(128, dtype=f).astype(ml_dtypes.bfloat16)
    m["ones_bf"] = np.ones((128, 128), dtype=f).astype(ml_dtypes.bfloat16)
    w_in = np.array(inp["w_in"][0])
    for h in range(8):
        for base in (0, 2048):
            blk = w_in[:, base + h * 256:base + (h + 1) * 256]
            w_in[:, base + h * 256:base + (h + 1) * 256] = blk[:, _PERM]
    m["w_in"] = np.ascontiguousarray(w_in)
    m["lg_bc"] = np.ascontiguousarray(np.tile(inp["ret_log_decay"][0].reshape(1, 16), (128, 1)))
    j = np.arange(128)[:, None].astype(f)
    i = np.arange(128)[None, :].astype(f)
    cst = np.zeros((128, 6, 128), f)
    cst[:, 0] = np.maximum(i - j, 0)
    cst[:, 1] = np.maximum(j - i, 0)
    cst[:, 2] = (i >= j) / 16.0
    cst[:, 3] = (j >= i) / 16.0
    cst[:, 4] = i + 1 + 0 * j
    cst[:, 5] = 128 - i + 0 * j
    m["ret_cst"] = cst
    p = np.arange(128).astype(f)
    m["ret_pcol"] = np.ascontiguousarray(np.stack([127 - p, p, 128 + 0 * p], 1))
    m["rnw_bc"] = np.ascontiguousarray(np.tile(inp["ret_norm_w"][0][None, :], (128, 1)))
    t = np.arange(2048)
    row = (t // 64).astype(f)
    col = (t % 64).astype(f)
    inv = (np.float32(10000.0) ** (-np.arange(64, dtype=f) / np.float32(64))).astype(f)
    ang = np.concatenate([row[:, None] * inv[None, :], col[:, None] * inv[None, :]], axis=-1).astype(f)
    m["rope_cs"] = np.ascontiguousarray(np.stack([np.cos(ang).T, np.sin(ang).T], 1).astype(f))
    cw = inp["dn_conv_w"][0]
    m["dn_cwT"] = np.ascontiguousarray(cw.reshape(3, 48, 128).transpose(2, 1, 0))
    ab = np.stack([inp["dn_A_log"][0].reshape(32), inp["dn_dt_bias"][0].reshape(32)], 0)
    m["dn_ab"] = np.ascontiguousarray(np.tile(ab[None], (64, 1, 1)))
    a = np.arange(64)[:, None]
    bb = np.arange(64)[None, :]
    NEG = -30000.0
    dc = np.zeros((64, 7, 64), f)
    dc[:, 0] = (a == bb)
    dc[:, 1] = (a <= bb)
    dc[:, 2] = (a >= bb)
    dc[:, 3] = np.where(a > bb, 0.0, NEG)
    dc[:, 4] = np.where(a < bb, 0.0, NEG)
    dc[:, 5] = np.where(bb >= a, 0.0, NEG)
    dc[:, 6] = np.where(bb <= a, 0.0, NEG)
    m["dn_cst"] = dc
    m["dnw_bc"] = np.ascontiguousarray(np.tile(inp["dn_norm_w"][0][None, :], (64, 1)))
    sel = np.zeros((2, 2, 128), f)
    sel[0, 0] = 1
    sel[1, 1] = 1
    m["sel"] = sel
    m["w_br"] = np.ascontiguousarray(inp["w_br"][0])
    m["w_bd"] = np.ascontiguousarray(inp["w_bd"][0])
    m["w_o"] = np.ascontiguousarray(inp["w_o"][0])
    m["w_up"] = np.ascontiguousarray(inp["w_up"][0])
    m["w_down"] = np.ascontiguousarray(inp["w_down"][0])
    for k in ("ln1_g", "ln1_b", "ln2_g", "ln2_b"):
        m[k.replace("_", "") + "_bc"] = np.ascontiguousarray(np.tile(inp[k][0][None, :], (128, 1)))
    fc = np.concatenate([inp["ffn_conv_w"][0], inp["ffn_conv_b"]], 0)
    m["ffn_cwT"] = np.ascontiguousarray(fc.reshape(4, 88, 128).transpose(2, 1, 0))
    return m


def core_inputs(core, inp, shared):
    m = dict(shared)
    xp = inp["x_prompt"][2 * core:2 * core + 2].reshape(512, D)
    m["x"] = np.ascontiguousarray(np.concatenate([xp, inp["x_sample"][core]], axis=0))
    cond = np.stack([inp["c_ctx"], inp["c"][core]], axis=0)
    m["condT"] = np.ascontiguousarray(cond.reshape(2, KC, 128).transpose(2, 1, 0))
    m["state_ret"] = np.ascontiguousarray(inp["state_ret"][core, 0][:, :, _PERM, :])
    m["state_dn"] = np.ascontiguousarray(inp["state_dn"][core, 0])
    return m


_NC_CACHE = {}


def kernel(**inp):
    inp = {k: np.asarray(v) for k, v in inp.items()}
    if "nc" not in _NC_CACHE:
        _NC_CACHE["nc"] = build()
    nc = _NC_CACHE["nc"]
    shared = shared_inputs(inp)
    in_maps = [core_inputs(c, inp, shared) for c in range(8)]
    res = run_bass_kernel_spmd(nc, in_maps, core_ids=list(range(8)))
    y_p = np.zeros((16, 256, D), np.float32)
    y_s = np.zeros((8, 2048, D), np.float32)
    n_ret = np.zeros((16, 1, 2, 8, 256, 512), np.float32)
    n_dn = np.zeros((16, 1, 2, 16, 128, 128), np.float32)
    for c in range(8):
        r = res.results[c]
        y = np.asarray(r["y"])
        y_p[2 * c:2 * c + 2] = y[0:512].reshape(2, 256, D)
        y_s[c] = y[512:]
        nr = np.asarray(r["new_ret"])
        n_ret[2 * c:2 * c + 2, 0][:, :, :, _PERM, :] = nr
        n_dn[2 * c:2 * c + 2, 0] = np.asarray(r["new_dn"])
    return (y_p, y_s, n_ret, n_dn)
```

```python
from contextlib import ExitStack
import numpy as np
import ml_dtypes
import concourse.bass as bass
import concourse.mybir as mybir
from concourse.bass_utils import run_bass_kernel_spmd

F32 = mybir.dt.float32
BF16 = mybir.dt.bfloat16
AF = mybir.ActivationFunctionType
ALU = mybir.AluOpType
AX = mybir.AxisListType

D = 2048
NTOK = 2560
KC = 16
LN_EPS = 1e-5

ENGS = ["pe", "act", "dve", "pool", "sp"]
NRING = 24
STQ = "act"


class Buf:
    __slots__ = ("name", "w", "r")

    def __init__(self, name=""):
        self.name = name
        self.w = None
        self.r = {}


class Sync:
    def __init__(self, nc, es):
        self.nc = nc
        self.esem = {e: es.enter_context(nc.semaphore("s_" + e)) for e in ENGS if e != "sp"}
        self.ring = [es.enter_context(nc.semaphore("r_%d" % i)) for i in range(NRING)]
        self.ebase = {e: 0 for e in self.esem}
        self.rcount = [0] * NRING
        self.ndma = 0


class Prog:
    def __init__(self, sync, selfsync=("act", "dve", "pool")):
        self.s = sync
        self.ops = {e: [] for e in ENGS}
        self.selfsync = set(selfsync)
        self.ring_last = {}

    def _deps(self, reads, writes):
        d = {}

        def add(t):
            if t is None:
                return
            k = (t[0], t[1])
            if k not in d or d[k][2] < t[2]:
                d[k] = t

        for b in reads:
            add(b.w)
        for b in writes:
            add(b.w)
            for t in b.r.values():
                add(t)
        return d

    def _finish(self, tok, reads, writes):
        k = (tok[0], tok[1])
        for b in reads:
            b.r[k] = tok
        for b in writes:
            b.w = tok
            b.r = {}

    def op(self, eng, fn, reads=(), writes=()):
        d = self._deps(reads, writes)
        if eng not in self.selfsync:
            d.pop(("e", eng), None)
        tok = ("e", eng, len(self.ops[eng]))
        self.ops[eng].append({"fn": fn, "deps": list(d.values()), "needed": False, "ring": None})
        self._finish(tok, reads, writes)
        return tok

    def dma(self, eng, fn, reads=(), writes=()):
        s = self.s
        ri = s.ndma % NRING
        s.ndma += 1
        s.rcount[ri] += 1
        tok = ("s", ri, 16 * s.rcount[ri])
        d = self._deps(reads, writes)
        if eng not in self.selfsync:
            d.pop(("e", eng), None)
        deps = list(d.values())
        prev = self.ring_last.get(ri)
        if prev is not None:
            deps.append(prev)
        self.ring_last[ri] = tok
        self.ops[eng].append({"fn": fn, "deps": deps, "needed": False, "ring": ri})
        self._finish(tok, reads, writes)
        return tok

    def wait_all_dma(self, eng="sp"):
        self.ops[eng].append({"fn": None, "deps": list(self.ring_last.values()), "needed": False, "ring": None})

    def emit(self, block):
        s = self.s
        for e in ENGS:
            for o in self.ops[e]:
                for t in o["deps"]:
                    if t[0] == "e":
                        self.ops[t[1]][t[2]]["needed"] = True
        for e in ENGS:
            if e == "sp":
                continue
            c = s.ebase[e]
            for o in self.ops[e]:
                if o["needed"]:
                    c += 1
                    o["val"] = c
            s.ebase[e] = c

        def resolve(t):
            if t[0] == "e":
                return s.esem[t[1]], self.ops[t[1]][t[2]]["val"]
            return s.ring[t[1]], t[2]

        def emit_engine(ename, eh):
            waited = {}
            for o in self.ops[ename]:
                ws = {}
                for t in o["deps"]:
                    sem, v = resolve(t)
                    k = id(sem)
                    if waited.get(k, 0) >= v:
                        continue
                    if k not in ws or ws[k][1] < v:
                        ws[k] = (sem, v)
                wl = list(ws.values())
                for sem, v in wl:
                    waited[id(sem)] = v
                if o["fn"] is None:
                    for sem, v in wl:
                        eh.wait_ge(sem, v)
                    continue
                embed = None
                if wl and ename != "pe":
                    embed = wl.pop()
                for sem, v in wl:
                    eh.wait_ge(sem, v)
                inst = o["fn"](eh)
                if embed is not None:
                    inst._wait_ge(embed[0], embed[1])
                if o["ring"] is not None:
                    inst.then_inc(s.ring[o["ring"]], 16)
                elif o["needed"]:
                    inst.then_inc(s.esem[ename], 1)

        @block.tensor
        def _(eh):
            emit_engine("pe", eh)

        @block.scalar
        def _(eh):
            emit_engine("act", eh)

        @block.vector
        def _(eh):
            emit_engine("dve", eh)

        @block.gpsimd
        def _(eh):
            emit_engine("pool", eh)

        @block.sync
        def _(eh):
            emit_engine("sp", eh)


class Ctx:
    CNT = [0]

    def __init__(self, nc, es):
        self.nc = nc
        self.es = es

    def sb(self, shape, dt, name=None):
        Ctx.CNT[0] += 1
        return self.es.enter_context(self.nc.sbuf_tensor("%s_%d" % (name or "t", Ctx.CNT[0]), list(shape), dt))

    def ps(self, shape, dt, name=None):
        Ctx.CNT[0] += 1
        return self.es.enter_context(self.nc.psum_tensor("%s_%d" % (name or "p", Ctx.CNT[0]), list(shape), dt))


def stage_ada(nc, sync, G):
    with ExitStack() as es:
        cx = Ctx(nc, es)
        P = Prog(sync)
        scT = cx.sb([128, KC, 2], F32, "scT")
        brow = cx.sb([2, 6 * D], F32, "brow")
        mrow = cx.sb([2, 6 * D], F32, "mrow")
        sel = cx.sb([2, 2, 128], F32, "sel")
        wb = [cx.sb([128, KC, 512], F32, "wada") for _ in range(3)]
        ps = [cx.ps([128, 512], F32, "psA") for _ in range(2)]
        pt = cx.ps([128, 512], F32, "psT")
        b_sc, b_br, b_mr, b_sel, b_pt = Buf(), Buf(), Buf(), Buf(), Buf()
        b_wb = [Buf() for _ in range(3)]
        b_ps = [Buf(), Buf()]
        modT = G["modT"]
        b_mod = Buf()
        P.dma("sp", lambda e: e.dma_start(out=scT[:], in_=G["condT"][:, :, :]), writes=[b_sc])
        P.dma("sp", lambda e: e.dma_start(out=brow[:], in_=G["b_adarow"][:, :]), writes=[b_br])
        P.dma("sp", lambda e: e.dma_start(out=sel[:], in_=G["sel"][:, :, :]), writes=[b_sel])
        P.op("act", lambda e: e.activation(out=scT[:], in_=scT[:], func=AF.Silu), reads=[b_sc], writes=[b_sc])
        wv = G["w_ada"].rearrange("(kc p) n -> p kc n", p=128)
        for nb in range(24):
            sl = nb % 3
            k = nb % 2
            P.dma("sp", lambda e, nb=nb, sl=sl: e.dma_start(out=wb[sl][:], in_=wv[:, :, nb * 512:(nb + 1) * 512]),
                  writes=[b_wb[sl]])
            mm_acc(P, ps[k][0:2, :], b_ps[k], [(scT[:, kc, :], wb[sl][:, kc, :]) for kc in range(KC)],
                   [b_wb[sl], b_sc])
            P.op("dve", lambda e, nb=nb, k=k: e.tensor_tensor(out=mrow[:, nb * 512:(nb + 1) * 512], in0=ps[k][0:2, :],
                                                               in1=brow[:, nb * 512:(nb + 1) * 512], op=ALU.add),
                 reads=[b_ps[k], b_br], writes=[b_mr])
        P.dma(STQ, lambda e: e.dma_start(out=G["modrow"][:, 0:2048], in_=mrow[:, 4096:6144]), reads=[b_mr])
        P.dma(STQ, lambda e: e.dma_start(out=G["modrow"][:, 2048:4096], in_=mrow[:, 10240:12288]), reads=[b_mr])
        for j in range(96):
            P.op("pe", lambda e, j=j: e.transpose(out=pt[:, 2 * j:2 * j + 2], in_=mrow[:, j * 128:(j + 1) * 128],
                                                  identity=sel[:, :, 0]), reads=[b_mr, b_sel], writes=[b_pt])
        P.op("dve", lambda e: e.tensor_copy(out=modT[:], in_=pt[:, 0:192].rearrange("p (j c) -> p j c", c=2)),
             reads=[b_pt], writes=[b_mod])
        for lo in (16, 64):
            P.op("dve", lambda e, lo=lo: e.tensor_scalar_add(out=modT[:, lo:lo + 16, :], in0=modT[:, lo:lo + 16, :],
                                                             scalar1=1.0), reads=[b_mod], writes=[b_mod])
        P.wait_all_dma()
        with nc.Block() as block:
            P.emit(block)


def ln_rows(P, cx, x_t, b_x, out_t, b_out, scratch):
    st, mv, rstd = scratch["st"], scratch["mv"], scratch["rstd"]
    b_st = scratch["b_st"]
    for c in range(4):
        P.op("dve", lambda e, c=c: e.bn_stats(out=st[:, c, :], in_=x_t[:, c * 512:(c + 1) * 512]),
             reads=[b_x], writes=[b_st])
    P.op("dve", lambda e: e.bn_aggr(out=mv[:], in_=st[:]), reads=[b_st], writes=[b_st])
    P.op("act", lambda e: e.activation(out=rstd[:], in_=mv[:, 1:2], func=AF.Sqrt, bias=LN_EPS, scale=1.0),
         reads=[b_st], writes=[b_st])
    P.op("dve", lambda e: e.reciprocal(out=rstd[:], in_=rstd[:]), reads=[b_st], writes=[b_st])
    P.op("dve", lambda e: e.tensor_scalar(out=out_t[:], in0=x_t[:], scalar1=mv[:, 0:1], scalar2=rstd[:, 0:1],
                                           op0=ALU.subtract, op1=ALU.mult), reads=[b_x, b_st], writes=[b_out])


def stage_ln1(nc, sync, G, hT, tok0, ntiles, cond, dbg=None):
    with ExitStack() as es:
        cx = Ctx(nc, es)
        P = Prog(sync)
        ident = cx.sb([128, 128], BF16, "ident")
        b_id = Buf()
        P.dma("sp", lambda e: e.dma_start(out=ident[:], in_=G["ident_bf"][:, :]), writes=[b_id])
        xt = [cx.sb([128, D], F32, "x") for _ in range(2)]
        b_xt = [Buf(), Buf()]
        xn = [cx.sb([128, D], BF16, "xn") for _ in range(2)]
        b_xn = [Buf(), Buf()]
        scr = {"st": cx.sb([128, 4, 6], F32), "mv": cx.sb([128, 2], F32), "rstd": cx.sb([128, 1], F32), "b_st": Buf()}
        pst = [cx.ps([128, 1024], BF16, "pT") for _ in range(2)]
        b_pst = [Buf(), Buf()]
        modT = G["modT"]
        b_h = Buf()
        for t in range(ntiles):
            sl = t % 2
            c = cond
            P.dma("sp", lambda e, t=t, sl=sl: e.dma_start(
                out=xt[sl][:], in_=G["x"][tok0 + t * 128:tok0 + (t + 1) * 128, :]), writes=[b_xt[sl]])
            ln_rows(P, cx, xt[sl], b_xt[sl], xn[sl], b_xn[sl], scr)
            for g in range(2):
                for j in range(8):
                    kc = g * 8 + j
                    P.op("pe", lambda e, g=g, j=j, kc=kc, sl=sl: e.transpose(
                        out=pst[g][:, j * 128:(j + 1) * 128], in_=xn[sl][:, kc * 128:(kc + 1) * 128],
                        identity=ident[:]), reads=[b_xn[sl], b_id], writes=[b_pst[g]])
                for j in range(8):
                    kc = g * 8 + j
                    P.op("act", lambda e, g=g, j=j, kc=kc, t=t, c=c: e.activation(
                        out=hT[:, kc, t * 128:(t + 1) * 128], in_=pst[g][:, j * 128:(j + 1) * 128],
                        func=AF.Identity, scale=modT[:, 16 + kc, c:c + 1], bias=modT[:, kc, c:c + 1]),
                        reads=[b_pst[g]], writes=[b_h])
        if dbg is not None:
            P.dma(STQ, lambda e: e.dma_start(
                out=dbg.rearrange("(kc p) t -> p kc t", p=128)[:, :, tok0:tok0 + ntiles * 128],
                in_=hT[:, :, 0:ntiles * 128]), reads=[b_h])
        P.wait_all_dma()
        with nc.Block() as block:
            P.emit(block)


class WStream:
    CH = 2

    def __init__(self, P, cx, nbuf=3, nstg=4, cols=512, engines=("pool",)):
        self.P = P
        self.engines = list(engines)
        self.stg = [cx.sb([128, self.CH, cols], F32, "stg") for _ in range(nstg)]
        self.b_stg = [Buf() for _ in range(nstg)]
        self.wb = [cx.sb([128, KC, cols], BF16, "wb") for _ in range(nbuf)]
        self.b_wb = [Buf() for _ in range(nbuf)]
        self.i = 0
        self.j = 0

    def load(self, w_ap, ncols, kcn=KC):
        P = self.P
        CH = self.CH
        sl = self.i % len(self.wb)
        self.i += 1
        wv = w_ap.rearrange("(kc p) n -> p kc n", p=128)
        wb, b_wb = self.wb[sl], self.b_wb[sl]
        for g in range(kcn // CH):
            st = self.j % len(self.stg)
            self.j += 1
            stg, b_stg = self.stg[st], self.b_stg[st]
            P.dma("sp", lambda e, stg=stg, g=g: e.dma_start(out=stg[:, :, 0:ncols], in_=wv[:, g * CH:(g + 1) * CH, :]),
                  writes=[b_stg])
            ce = self.engines[self.j % len(self.engines)]
            if ce == "act":
                P.op("act", lambda e, stg=stg, wb=wb, g=g: e.copy(out=wb[:, g * CH:(g + 1) * CH, 0:ncols],
                                                                  in_=stg[:, :, 0:ncols]), reads=[b_stg], writes=[b_wb])
            else:
                P.op(ce, lambda e, stg=stg, wb=wb, g=g: e.tensor_copy(out=wb[:, g * CH:(g + 1) * CH, 0:ncols],
                                                                      in_=stg[:, :, 0:ncols]),
                     reads=[b_stg], writes=[b_wb])
        return wb, b_wb


def mm_acc(P, ps, b_ps, pairs, reads):
    n = len(pairs)
    for i, (l, r) in enumerate(pairs):
        P.op("pe", lambda e, l=l, r=r, i=i: e.matmul(ps, lhsT=l, rhs=r, start=(i == 0), stop=(i == n - 1)),
             reads=reads, writes=[b_ps])


def stage_ret(nc, sync, G, hT, b_hT_unused, NT, seqs, tok0, rope):
    with ExitStack() as es:
        cx = Ctx(nc, es)
        P = Prog(sync)
        W = WStream(P, cx, nbuf=3, nstg=4, engines=("pool", "act"))
        b_h = Buf()
        NCH = NT // 128
        qT = cx.sb([128, 2, NT], BF16, "qT")
        kT = cx.sb([128, 2, NT], BF16, "kT")
        vt = cx.sb([128, NCH, 512], BF16, "v")
        b_q, b_k, b_v = Buf(), Buf(), Buf()
        Sst = [cx.sb([128, 2, 512], F32, "S") for _ in range(2)]
        Sbf = [cx.sb([128, 2, 512], BF16, "Sbf") for _ in range(2)]
        b_S = [Buf(), Buf()]
        b_Sbf = [Buf(), Buf()]
        sbb = [cx.sb([128, 2, 512], BF16, "sbb") for _ in range(2)]
        b_sbb = [Buf(), Buf()]
        tmp1 = cx.sb([128, 512], F32, "tmp1")
        tmp2 = cx.sb([128, 512], F32, "tmp2")
        b_t1, b_t2 = Buf(), Buf()
        cs = cx.sb([128, 2, 512], F32, "cs")
        b_cs = Buf()
        mask = cx.sb([128, 128], F32, "mask")
        mtmp = cx.sb([128, 128], F32, "mtmp")
        qdr = cx.sb([128, 2, 128], F32, "qdr")
        b_hc = Buf()
        rnw = cx.sb([128, 512], F32, "rnw")
        b_rnw = Buf()
        gate = cx.sb([128, 512], F32, "gate")
        b_gate = Buf()
        ogs = [cx.sb([128, 512], BF16, "og") for _ in range(2)]
        b_ogs = [Buf(), Buf()]
        ogT = [cx.sb([128, 4, 128], BF16, "ogT") for _ in range(2)]
        b_ogT = [Buf(), Buf()]
        ktok = cx.sb([128, 256], BF16, "ktok")
        b_ktok = Buf()
        sm = cx.sb([128, 128], BF16, "sm")
        b_sm = Buf()
        qfb = cx.sb([128, 2, 2, 128], BF16, "qfb")
        b_qfb = Buf()
        scr = {"st": cx.sb([128, 1, 6], F32), "mv": cx.sb([128, 2], F32), "rstd": cx.sb([128, 1], F32), "b_st": Buf()}
        ident = cx.sb([128, 128], BF16, "ident")
        lg = cx.sb([128, 16], F32, "lg")
        cst = cx.sb([128, 6, 128], F32, "cst")
        pcol = cx.sb([128, 3], F32, "pcol")
        dec = cx.sb([128, 8, 4], F32, "dec")
        b_c = Buf()
        P.dma("sp", lambda e: e.dma_start(out=ident[:], in_=G["ident_bf"][:, :]), writes=[b_c])
        P.dma("sp", lambda e: e.dma_start(out=lg[:], in_=G["lg_bc"][:, :]), writes=[b_c])
        P.dma("sp", lambda e: e.dma_start(out=cst[:], in_=G["ret_cst"][:, :, :]), writes=[b_c])
        P.dma("sp", lambda e: e.dma_start(out=pcol[:], in_=G["ret_pcol"][:, :]), writes=[b_c])
        for h in range(8):
            for d in range(2):
                l = lg[:, d * 8 + h:d * 8 + h + 1]
                P.op("act", lambda e, h=h, d=d, l=l: e.activation(out=dec[:, h, d:d + 1], in_=pcol[:, d:d + 1],
                                                                   func=AF.Exp, scale=l), reads=[b_c], writes=[b_c])
                P.op("act", lambda e, h=h, d=d, l=l: e.activation(out=dec[:, h, 2 + d:3 + d], in_=pcol[:, 2:3],
                                                                   func=AF.Exp, scale=l), reads=[b_c], writes=[b_c])
        P.op("dve", lambda e: e.tensor_scalar_mul(out=dec[:, :, 0:2], in0=dec[:, :, 0:2], scalar1=1.0 / 16.0),
             reads=[b_c], writes=[b_c])
        pA = cx.ps([128, 512], F32, "pA")
        pB = cx.ps([128, 512], F32, "pB")
        pO = cx.ps([128, 512], F32, "pO")
        pG = cx.ps([128, 512], F32, "pG")
        pU = [cx.ps([128, 512], F32, "pU") for _ in range(2)]
        pT = cx.ps([128, 1024], BF16, "pT")
        pS = cx.ps([128, 512], F32, "pS")
        b_pA, b_pB, b_pO, b_pG, b_pT, b_pS = Buf(), Buf(), Buf(), Buf(), Buf(), Buf()
        b_pU = [Buf(), Buf()]
        win = G["w_in"]
        sb_scr = G["sb_scr"]
        b_scr = [Buf() for _ in range(16)]

        def ktok_make(h, d, c0):
            for dkb in range(2):
                P.op("pe", lambda e, dkb=dkb: e.transpose(out=pT[:, dkb * 128:(dkb + 1) * 128],
                                                          in_=kT[:, dkb, c0:c0 + 128], identity=ident[:]),
                     reads=[b_k, b_c], writes=[b_pT])
            P.op("dve", lambda e: e.tensor_scalar(out=ktok[:], in0=pT[:, 0:256], scalar1=dec[:, h, d:d + 1],
                                                   scalar2=None, op0=ALU.mult), reads=[b_pT, b_c], writes=[b_ktok])

        def state_update(h, d, n_local):
            for dkb in range(2):
                P.op("pe", lambda e, dkb=dkb: e.matmul(pU[dkb][:, :], lhsT=ktok[:, dkb * 128:(dkb + 1) * 128],
                                                       rhs=vt[:, n_local, :], start=True, stop=True),
                     reads=[b_ktok, b_v], writes=[b_pU[dkb]])
            for dkb in range(2):
                P.op("dve", lambda e, dkb=dkb: e.scalar_tensor_tensor(
                    out=Sst[d][:, dkb, :], in0=Sst[d][:, dkb, :], scalar=dec[:, h, 2 + d:3 + d], in1=pU[dkb][:, :],
                    op0=ALU.mult, op1=ALU.add), reads=[b_pU[dkb], b_c, b_S[d]], writes=[b_S[d]])
            P.op("act", lambda e: e.copy(out=Sbf[d][:], in_=Sst[d][:]), reads=[b_S[d]], writes=[b_Sbf[d]])

        nxt = (W.load(win[:, 0:256], 256), W.load(win[:, 2048:2048 + 256], 256))
        for h in range(8):
            (wq, b_wq), (wk, b_wk) = nxt
            wv, b_wv = W.load(win[:, 4096 + h * 512:4096 + (h + 1) * 512], 512)
            lf = lg[:, h:h + 1]
            lb = lg[:, 8 + h:9 + h]
            P.op("act", lambda e, lf=lf: e.activation(out=mask[:], in_=cst[:, 0, :], func=AF.Exp, scale=lf),
                 reads=[b_c, b_sm], writes=[b_hc])
            P.op("act", lambda e, lb=lb: e.activation(out=mtmp[:], in_=cst[:, 1, :], func=AF.Exp, scale=lb),
                 reads=[b_c], writes=[b_hc])
            P.op("dve", lambda e: e.tensor_tensor(out=mask[:], in0=mask[:], in1=cst[:, 2, :], op=ALU.mult),
                 reads=[b_hc, b_c], writes=[b_hc])
            P.op("dve", lambda e: e.tensor_tensor(out=mtmp[:], in0=mtmp[:], in1=cst[:, 3, :], op=ALU.mult),
                 reads=[b_hc, b_c], writes=[b_hc])
            P.op("dve", lambda e: e.tensor_tensor(out=mask[:], in0=mask[:], in1=mtmp[:], op=ALU.add),
                 reads=[b_hc], writes=[b_hc])
            P.op("act", lambda e, lf=lf: e.activation(out=qdr[:, 0, :], in_=cst[:, 4, :], func=AF.Exp, scale=lf),
                 reads=[b_c, b_qfb], writes=[b_hc])
            P.op("act", lambda e, lb=lb: e.activation(out=qdr[:, 1, :], in_=cst[:, 5, :], func=AF.Exp, scale=lb),
                 reads=[b_c], writes=[b_hc])
            P.dma("sp", lambda e, h=h: e.dma_start(out=rnw[:], in_=G["rnw_bc"][:, h * 512:(h + 1) * 512]),
                  writes=[b_rnw])
            for tt in range(NT // 512):
                tsl = slice(tt * 512, (tt + 1) * 512)
                if rope:
                    P.dma("sp", lambda e, tsl=tsl: e.dma_start(out=cs[:], in_=G["rope_cs"][:, :, tsl]), writes=[b_cs])
                for (wt, b_w, dst, b_dst) in ((wq, b_wq, qT, b_q), (wk, b_wk, kT, b_k)):
                    for dkb, (ps, b_ps) in enumerate(((pA, b_pA), (pB, b_pB))):
                        mm_acc(P, ps[:, :], b_ps,
                               [(wt[:, kc, dkb * 128:(dkb + 1) * 128], hT[:, kc, tsl]) for kc in range(KC)],
                               [b_w, b_h])
                    if rope:
                        P.op("dve", lambda e: e.tensor_tensor(out=tmp1[:], in0=pA[:, :], in1=cs[:, 0, :], op=ALU.mult),
                             reads=[b_pA, b_cs], writes=[b_t1])
                        P.op("dve", lambda e: e.tensor_tensor(out=tmp2[:], in0=pB[:, :], in1=cs[:, 1, :], op=ALU.mult),
                             reads=[b_pB, b_cs], writes=[b_t2])
                        P.op("pool", lambda e, dst=dst, tsl=tsl: e.tensor_tensor(out=dst[:, 0, tsl], in0=tmp1[:],
                                                                                   in1=tmp2[:], op=ALU.subtract),
                             reads=[b_t1, b_t2], writes=[b_dst])
                        P.op("dve", lambda e: e.tensor_tensor(out=tmp1[:], in0=pA[:, :], in1=cs[:, 1, :], op=ALU.mult),
                             reads=[b_pA, b_cs], writes=[b_t1])
                        P.op("dve", lambda e: e.tensor_tensor(out=tmp2[:], in0=pB[:, :], in1=cs[:, 0, :], op=ALU.mult),
                             reads=[b_pB, b_cs], writes=[b_t2])
                        P.op("pool", lambda e, dst=dst, tsl=tsl: e.tensor_tensor(out=dst[:, 1, tsl], in0=tmp1[:],
                                                                                   in1=tmp2[:], op=ALU.add),
                             reads=[b_t1, b_t2], writes=[b_dst])
                    else:
                        P.op("act", lambda e, dst=dst, tsl=tsl: e.copy(out=dst[:, 0, tsl], in_=pA[:, :]),
                             reads=[b_pA], writes=[b_dst])
                        P.op("act", lambda e, dst=dst, tsl=tsl: e.copy(out=dst[:, 1, tsl], in_=pB[:, :]),
                             reads=[b_pB], writes=[b_dst])
            wg, b_wg = W.load(win[:, 8192 + h * 512:8192 + (h + 1) * 512], 512)
            for t in range(NCH):
                mm_acc(P, pO[:, :], b_pO, [(hT[:, kc, t * 128:(t + 1) * 128], wv[:, kc, :]) for kc in range(KC)],
                       [b_wv, b_h])
                P.op("act", lambda e, t=t: e.copy(out=vt[:, t, :], in_=pO[:, :]), reads=[b_pO], writes=[b_v])
            if h < 7:
                nxt = (W.load(win[:, (h + 1) * 256:(h + 2) * 256], 256),
                       W.load(win[:, 2048 + (h + 1) * 256:2048 + (h + 2) * 256], 256))
            for (s0, T, kind, sidx) in seqs:
                N = T // 128
                for d in range(2):
                    if kind == "S":
                        P.dma("sp", lambda e, d=d, h=h: e.dma_start(
                            out=Sst[d][:], in_=G["state_ret"][d, h].rearrange("(b p) v -> p b v", p=128)),
                            writes=[b_S[d]])
                    else:
                        P.op("pool", lambda e, d=d: e.memset(Sst[d][:], 0.0), writes=[b_S[d]])
                    P.op("act", lambda e, d=d: e.copy(out=Sbf[d][:], in_=Sst[d][:]), reads=[b_S[d]], writes=[b_Sbf[d]])
                for n in range(N - 1, -1, -1):
                    c0 = s0 + n * 128
                    P.dma(STQ, lambda e, n=n: e.dma_start(out=sb_scr[n].rearrange("b p v -> p b v"), in_=Sbf[1][:]),
                          reads=[b_Sbf[1]], writes=[b_scr[n]])
                    ktok_make(h, 1, c0)
                    state_update(h, 1, c0 // 128)
                if kind == "P":
                    P.dma(STQ, lambda e, h=h, sidx=sidx: e.dma_start(
                        out=G["new_ret"][sidx, 1, h].rearrange("(b p) v -> p b v", p=128), in_=Sst[1][:]),
                        reads=[b_S[1]])
                pOs, b_pOs = [pO, pA], [b_pO, b_pA]
                pGs, b_pGs = [pG, pB], [b_pG, b_pB]
                gates, b_gates = [gate[:], cs[:, 0, :]], [b_gate, b_cs]
                tmps, b_tmps = [tmp1, tmp2], [b_t1, b_t2]

                def fin_pe(n):
                    c0 = s0 + n * 128
                    sl = n % 2
                    og_, b_og_ = ogs[sl], b_ogs[sl]
                    for fb in range(4):
                        P.op("pe", lambda e, fb=fb: e.transpose(out=pT[:, 256 + fb * 128:256 + (fb + 1) * 128],
                                                                in_=og_[:, fb * 128:(fb + 1) * 128], identity=ident[:]),
                             reads=[b_og_, b_c], writes=[b_pT])
                    o2 = ogT[sl]
                    P.op("act", lambda e: e.copy(out=o2[:].rearrange("p a b -> p (a b)"), in_=pT[:, 256:768]),
                         reads=[b_pT], writes=[b_ogT[sl]])
                    P.dma(STQ, lambda e, h=h: e.dma_start(
                        out=G["og"][h * 512:(h + 1) * 512, tok0 + c0:tok0 + c0 + 128].rearrange("(a p) t -> p a t", p=128),
                        in_=o2[:]), reads=[b_ogT[sl]])

                def ln_chain(n):
                    sl = n % 2
                    pO_, b_pO_ = pOs[sl], b_pOs[sl]
                    gate_, b_gate_ = gates[sl], b_gates[sl]
                    tmp_, b_tmp_ = tmps[sl], b_tmps[sl]
                    og_, b_og_ = ogs[sl], b_ogs[sl]
                    st, mv, rstd, b_st = scr["st"], scr["mv"], scr["rstd"], scr["b_st"]
                    P.op("dve", lambda e, tmp_=tmp_: e.bn_stats(out=st[:, 0, :], in_=tmp_[:]), reads=[b_tmp_], writes=[b_st])
                    P.op("dve", lambda e: e.bn_aggr(out=mv[:], in_=st[:]), reads=[b_st], writes=[b_st])
                    P.op("act", lambda e: e.activation(out=rstd[:], in_=mv[:, 1:2], func=AF.Sqrt, bias=LN_EPS, scale=1.0),
                         reads=[b_st], writes=[b_st])
                    P.op("dve", lambda e: e.reciprocal(out=rstd[:], in_=rstd[:]), reads=[b_st], writes=[b_st])
                    P.op("dve", lambda e, tmp_=tmp_: e.tensor_scalar(
                        out=tmp_[:], in0=tmp_[:], scalar1=mv[:, 0:1], scalar2=rstd[:, 0:1], op0=ALU.subtract,
                        op1=ALU.mult), reads=[b_tmp_, b_st], writes=[b_tmp_])
                    P.op("dve", lambda e, tmp_=tmp_: e.tensor_tensor(out=tmp_[:], in0=tmp_[:], in1=rnw[:], op=ALU.mult),
                         reads=[b_tmp_, b_rnw], writes=[b_tmp_])
                    P.op("dve", lambda e, tmp_=tmp_, gate_=gate_, og_=og_: e.tensor_tensor(
                        out=og_[:], in0=tmp_[:], in1=gate_, op=ALU.mult), reads=[b_tmp_, b_gate_], writes=[b_og_])

                def pre(n):
                    c0 = s0 + n * 128
                    mm_acc(P, pS[:, 0:128], b_pS, [(kT[:, dkb, c0:c0 + 128], qT[:, dkb, c0:c0 + 128]) for dkb in range(2)],
                           [b_q, b_k])
                    P.op("dve", lambda e: e.tensor_tensor(out=sm[:], in0=pS[:, 0:128], in1=mask[:], op=ALU.mult),
                         reads=[b_pS, b_hc], writes=[b_sm])
                    for d in range(2):
                        P.op("dve", lambda e, d=d: e.tensor_tensor(
                            out=qfb[:, d, :, :], in0=qT[:, :, c0:c0 + 128],
                            in1=qdr[:, d, :].unsqueeze(1).broadcast_to([128, 2, 128]), op=ALU.mult),
                            reads=[b_q, b_hc], writes=[b_qfb])

                pre(0)
                for n in range(N):
                    c0 = s0 + n * 128
                    nl = c0 // 128
                    sl = n % 2
                    pO_, b_pO_ = pOs[sl], b_pOs[sl]
                    pG_, b_pG_ = pGs[sl], b_pGs[sl]
                    gate_, b_gate_ = gates[sl], b_gates[sl]
                    P.dma("sp", lambda e, n=n, sl=sl: e.dma_start(out=sbb[sl][:], in_=sb_scr[n].rearrange("b p v -> p b v")),
                          reads=[b_scr[n]], writes=[b_sbb[sl]])
                    pairs = [(sm[:], vt[:, nl, :])]
                    pairs += [(qfb[:, 0, dkb, :], Sbf[0][:, dkb, :]) for dkb in range(2)]
                    pairs += [(qfb[:, 1, dkb, :], sbb[sl][:, dkb, :]) for dkb in range(2)]
                    mm_acc(P, pO_[:, :], b_pO_, pairs, [b_sm, b_v, b_qfb, b_Sbf[0], b_sbb[sl]])
                    tmp_, b_tmp_ = tmps[sl], b_tmps[sl]
                    P.op("act", lambda e, tmp_=tmp_, pO_=pO_: e.copy(out=tmp_[:], in_=pO_[:, :]), reads=[b_pO_],
                         writes=[b_tmp_])
                    ktok_make(h, 0, c0)
                    if n >= 2:
                        fin_pe(n - 2)
                    mm_acc(P, pG_[:, :], b_pG_, [(hT[:, kc, c0:c0 + 128], wg[:, kc, :]) for kc in range(KC)], [b_wg, b_h])
                    P.op("act", lambda e, gate_=gate_, pG_=pG_: e.activation(out=gate_, in_=pG_[:, :], func=AF.Silu),
                         reads=[b_pG_], writes=[b_gate_])
                    state_update(h, 0, nl)
                    if n + 1 < N:
                        pre(n + 1)
                    if n > 0:
                        ln_chain(n - 1)
                ln_chain(N - 1)
                if N >= 2:
                    fin_pe(N - 2)
                fin_pe(N - 1)
                if kind == "P":
                    P.dma(STQ, lambda e, h=h, sidx=sidx: e.dma_start(
                        out=G["new_ret"][sidx, 0, h].rearrange("(b p) v -> p b v", p=128), in_=Sst[0][:]),
                        reads=[b_S[0]])
        P.wait_all_dma()
        with nc.Block() as block:
            P.emit(block)


L2_EPS = 1e-6
RMS_EPS = 1e-6


def stage_gates(nc, sync, G, hT, NT, tok0):
    with ExitStack() as es:
        cx = Ctx(nc, es)
        P = Prog(sync)
        W = WStream(P, cx, nbuf=3, nstg=4, engines=("dve", "dve", "pool"))
        b_h = Buf()
        ps = [cx.ps([128, 512], F32, "pg") for _ in range(4)]
        b_ps = [Buf() for _ in range(4)]
        TT = min(NT, 512)
        sgt = [cx.sb([128, 4, NT], BF16, "sgt") for _ in range(2)]
        b_sgt = [Buf(), Buf()]
        for blk in range(8):
            w, b_w = W.load(G["w_in"][:, 20544 + blk * 512:20544 + (blk + 1) * 512], 512)
            o2, b_o2 = sgt[blk % 2], b_sgt[blk % 2]
            for tt in range(NT // TT):
                tsl = slice(tt * TT, (tt + 1) * TT)
                for cb in range(4):
                    mm_acc(P, ps[cb][:, 0:TT], b_ps[cb],
                           [(w[:, kc, cb * 128:(cb + 1) * 128], hT[:, kc, tsl]) for kc in range(KC)], [b_w, b_h])
                    P.op("act", lambda e, cb=cb, o2=o2, tsl=tsl: e.activation(out=o2[:, cb, tsl], in_=ps[cb][:, 0:TT],
                                                                               func=AF.Sigmoid),
                         reads=[b_ps[cb]], writes=[b_o2])
            P.dma(STQ, lambda e, blk=blk, o2=o2: e.dma_start(
                out=G["sg"][blk * 512:(blk + 1) * 512, tok0:tok0 + NT].rearrange("(a p) t -> p a t", p=128), in_=o2[:]),
                reads=[b_o2])
        P.wait_all_dma()
        with nc.Block() as block:
            P.emit(block)


def stage_dnproj(nc, sync, G, hT, NT, seqs, tok0):
    with ExitStack() as es:
        cx = Ctx(nc, es)
        P = Prog(sync)
        W = WStream(P, cx, nbuf=3, nstg=4, engines=("pool", "dve", "act"))
        b_h = Buf()
        pA = [cx.ps([128, 512], F32, "pA") for _ in range(4)]
        b_pA = [Buf() for _ in range(4)]
        pNs = [cx.ps([128, 512], F32, "pN") for _ in range(2)]
        b_pNs = [Buf(), Buf()]
        pZ = cx.ps([128, 512], F32, "pZ")
        b_pZ = Buf()
        ys = [cx.sb([128, NT + 2], F32, "y") for _ in range(2)]
        zs_ = [cx.sb([128, NT], F32, "z") for _ in range(2)]
        sqs = [cx.sb([128, NT], BF16, "sq") for _ in range(2)]
        rins = [cx.sb([128, NT], F32, "rin") for _ in range(2)]
        b_rins = [Buf(), Buf()]
        xo = [cx.sb([128, NT], BF16, "xo") for _ in range(2)]
        b_ys, b_zs, b_sqs = [Buf(), Buf()], [Buf(), Buf()], [Buf(), Buf()]
        b_xo = [Buf(), Buf()]
        ones = cx.sb([128, 128], BF16, "ones")
        cw = cx.sb([128, 48, 3], F32, "cw")
        b_c = Buf()
        P.dma("sp", lambda e: e.dma_start(out=ones[:], in_=G["ones_bf"][:, :]), writes=[b_c])
        P.dma("sp", lambda e: e.dma_start(out=cw[:], in_=G["dn_cwT"][:, :, :]), writes=[b_c])
        nx = 0
        pi = [0]
        for X in range(3):
            for gi in range(4):
                w, b_w = W.load(G["w_in"][:, 12288 + X * 2048 + gi * 512:12288 + X * 2048 + (gi + 1) * 512], 512)
                for hh in range(4):
                    H = gi * 4 + hh
                    blk = X * 16 + H
                    o2, b_o2 = xo[nx % 2], b_xo[nx % 2]
                    y, z, sq = ys[nx % 2], zs_[nx % 2], sqs[nx % 2]
                    b_y, b_z, b_sq = b_ys[nx % 2], b_zs[nx % 2], b_sqs[nx % 2]
                    nx += 1
                    for (s0, T, kind, sidx) in seqs:
                        TT = min(T, 512)
                        for tt in range(T // TT):
                            a = s0 + tt * TT
                            ps, b_ps = pA[pi[0] % 4], b_pA[pi[0] % 4]
                            pi[0] += 1
                            mm_acc(P, ps[:, 0:TT], b_ps,
                                   [(w[:, kc, hh * 128:(hh + 1) * 128], hT[:, kc, a:a + TT]) for kc in range(KC)],
                                   [b_w, b_h])
                            P.op("act", lambda e, ps=ps, a=a, TT=TT, y=y: e.copy(out=y[:, 1 + a:1 + a + TT], in_=ps[:, 0:TT]),
                                 reads=[b_ps], writes=[b_y])
                        zs = z[:, s0:s0 + T]
                        P.op("dve", lambda e, zs=zs, s0=s0, T=T, blk=blk, y=y: e.tensor_scalar(
                            out=zs, in0=y[:, 1 + s0:1 + s0 + T], scalar1=cw[:, blk, 1:2], scalar2=None, op0=ALU.mult),
                            reads=[b_y, b_c], writes=[b_z])
                        P.op("dve", lambda e, s0=s0, T=T, blk=blk, y=y, z=z: e.scalar_tensor_tensor(
                            out=z[:, s0 + 1:s0 + T], in0=y[:, 1 + s0:s0 + T], scalar=cw[:, blk, 0:1],
                            in1=z[:, s0 + 1:s0 + T], op0=ALU.mult, op1=ALU.add), reads=[b_y, b_c, b_z], writes=[b_z])
                        P.op("dve", lambda e, s0=s0, T=T, blk=blk, y=y, z=z: e.scalar_tensor_tensor(
                            out=z[:, s0:s0 + T - 1], in0=y[:, 2 + s0:1 + s0 + T], scalar=cw[:, blk, 2:3],
                            in1=z[:, s0:s0 + T - 1], op0=ALU.mult, op1=ALU.add), reads=[b_y, b_c, b_z], writes=[b_z])
                    P.op("act", lambda e, z=z: e.activation(out=z[:], in_=z[:], func=AF.Silu), reads=[b_z], writes=[b_z])
                    if X == 2:
                        P.op("act", lambda e, o2=o2, z=z: e.copy(out=o2[:], in_=z[:]), reads=[b_z], writes=[b_o2])
                    else:
                        P.op("act", lambda e, z=z, sq=sq: e.activation(out=sq[:], in_=z[:], func=AF.Square), reads=[b_z],
                             writes=[b_sq])
                        TT = min(NT, 512)
                        rin, b_rin = rins[nx % 2], b_rins[nx % 2]
                        for tt in range(NT // TT):
                            tsl = slice(tt * TT, (tt + 1) * TT)
                            pN_, b_pN_ = pNs[tt % 2], b_pNs[tt % 2]
                            P.op("pe", lambda e, tsl=tsl, TT=TT, sq=sq, pN_=pN_: e.matmul(
                                pN_[:, 0:TT], lhsT=ones[:], rhs=sq[:, tsl], start=True, stop=True),
                                reads=[b_sq, b_c], writes=[b_pN_])
                            P.op("act", lambda e, tsl=tsl, TT=TT, pN_=pN_, rin=rin: e.activation(
                                out=rin[:, tsl], in_=pN_[:, 0:TT], func=AF.Ln, bias=L2_EPS, scale=1.0), reads=[b_pN_],
                                writes=[b_rin])
                        P.op("act", lambda e, rin=rin: e.activation(out=rin[:], in_=rin[:], func=AF.Exp, scale=-0.5),
                             reads=[b_rin], writes=[b_rin])
                        sc = (128.0 ** -0.5) if X == 0 else 1.0
                        P.op("dve", lambda e, o2=o2, sc=sc, z=z, rin=rin: e.scalar_tensor_tensor(
                            out=o2[:], in0=z[:], scalar=sc, in1=rin[:], op0=ALU.mult, op1=ALU.mult),
                            reads=[b_z, b_rin], writes=[b_o2])
                    P.dma(STQ, lambda e, X=X, H=H, o2=o2: e.dma_start(out=G["dqkv"][X, H, :, tok0:tok0 + NT], in_=o2[:]),
                          reads=[b_o2])
        gz = [cx.sb([128, 512], BF16, "gz") for _ in range(2)]
        b_gz = [Buf(), Buf()]
        for gi in range(4):
            w, b_w = W.load(G["w_in"][:, 18432 + gi * 512:18432 + (gi + 1) * 512], 512)
            for t in range(NT // 128):
                sl = t % 2
                mm_acc(P, pZ[:, :], b_pZ, [(hT[:, kc, t * 128:(t + 1) * 128], w[:, kc, :]) for kc in range(KC)],
                       [b_w, b_h])
                P.op("act", lambda e, sl=sl: e.activation(out=gz[sl][:], in_=pZ[:, :], func=AF.Silu), reads=[b_pZ],
                     writes=[b_gz[sl]])
                P.dma(STQ, lambda e, sl=sl, t=t, gi=gi: e.dma_start(
                    out=G["dzg"][tok0 + t * 128:tok0 + (t + 1) * 128, gi * 512:(gi + 1) * 512], in_=gz[sl][:]),
                    reads=[b_gz[sl]])
        w, b_w = W.load(G["w_in"][:, 20480:20544], 64)
        NCH = NT // 64
        bg = cx.sb([64, NCH, 64], F32, "bg")
        b_bg = Buf()
        ab = cx.sb([64, 2, 32], F32, "ab")
        P.dma("sp", lambda e: e.dma_start(out=ab[:], in_=G["dn_ab"][:, :, :]), writes=[b_c])
        P.op("act", lambda e: e.activation(out=ab[:, 0, :], in_=ab[:, 0, :], func=AF.Exp), reads=[b_c], writes=[b_c])
        P.op("dve", lambda e: e.tensor_scalar_mul(out=ab[:, 0, :], in0=ab[:, 0, :], scalar1=-1.0), reads=[b_c],
             writes=[b_c])
        for cc in range(NCH):
            mm_acc(P, pZ[0:64, 0:64], b_pZ, [(hT[:, kc, cc * 64:(cc + 1) * 64], w[:, kc, 0:64]) for kc in range(KC)],
                   [b_w, b_h])
            P.op("act", lambda e, cc=cc: e.activation(out=bg[:, cc, 0:32], in_=pZ[0:64, 0:32], func=AF.Sigmoid),
                 reads=[b_pZ], writes=[b_bg])
            P.op("dve", lambda e, cc=cc: e.tensor_tensor(out=bg[:, cc, 32:64], in0=pZ[0:64, 32:64], in1=ab[:, 1, :],
                                                         op=ALU.add), reads=[b_pZ, b_c], writes=[b_bg])
        P.op("act", lambda e: e.activation(out=bg[:, :, 32:64], in_=bg[:, :, 32:64], func=AF.Exp), reads=[b_bg],
             writes=[b_bg])
        P.op("act", lambda e: e.activation(out=bg[:, :, 32:64], in_=bg[:, :, 32:64], func=AF.Ln, bias=1.0, scale=1.0),
             reads=[b_bg], writes=[b_bg])
        P.op("dve", lambda e: e.tensor_tensor(out=bg[:, :, 32:64], in0=bg[:, :, 32:64],
                                              in1=ab[:, 0, :].unsqueeze(1).broadcast_to([64, NCH, 32]), op=ALU.mult),
             reads=[b_bg, b_c], writes=[b_bg])
        P.dma(STQ, lambda e: e.dma_start(out=G["dbg"][:, tok0 // 64:tok0 // 64 + NCH, :], in_=bg[:]), reads=[b_bg])
        P.wait_all_dma()
        with nc.Block() as block:
            P.emit(block)


class _Ch:
    pass


DN_SEQ = False
DN_ALT = False


def stage_dnrec(nc, sync, G, NT, seqs, tok0, ngroups=4):
    with ExitStack() as es:
        cx = Ctx(nc, es)
        P = Prog(sync)
        NCH = NT // 64
        c0 = tok0 // 64
        cst = cx.sb([64, 7, 64], F32, "cst")
        id4 = cx.sb([64, 4, 64], F32, "id4")
        ones = cx.sb([64, 128], F32, "ones")
        identb = cx.sb([128, 128], BF16, "identb")
        dnw = cx.sb([64, 512], F32, "dnw")
        b_dnw = Buf()
        b_c = Buf()
        P.dma("sp", lambda e: e.dma_start(out=cst[:], in_=G["dn_cst"][:, :, :]), writes=[b_c])
        P.dma("sp", lambda e: e.dma_start(out=identb[:], in_=G["ident_bf"][:, :]), writes=[b_c])
        P.op("pool", lambda e: e.memset(ones[:], 1.0), writes=[b_c])
        for hh in range(4):
            P.op("pool", lambda e, hh=hh: e.tensor_copy(out=id4[:, hh, :], in_=cst[:, 0, :]), reads=[b_c], writes=[b_c])
        pb = [cx.ps([128, 512], F32, "pb") for _ in range(7)]
        pTr = cx.ps([128, 1024], BF16, "pTr")
        b_pb = [Buf() for _ in range(7)]
        b_pTr = Buf()
        bg = cx.sb([64, NCH, 64], F32, "bg")
        gcs = cx.sb([64, NCH, 32], F32, "gcs")
        egc = cx.sb([64, NCH, 32], F32, "egc")
        ekl = cx.sb([64, NCH, 32], F32, "ekl")
        bege = cx.sb([64, NCH, 32], F32, "bege")
        eglS = cx.sb([128, NCH, 32], F32, "eglS")
        b_sc = Buf()
        P.dma("sp", lambda e: e.dma_start(out=bg[:], in_=G["dbg"][:, c0:c0 + NCH, :]), writes=[b_sc])
        CB = 16
        for q0 in range(0, NCH, CB):
            nq = min(CB, NCH - q0)
            for d in range(2):
                P.op("pe", lambda e, d=d, q0=q0, nq=nq: e.matmul(
                    pb[6][0:64, 0:nq * 16].rearrange("p (c n) -> p c n", n=16), lhsT=cst[:, 1 + d, :],
                    rhs=bg[:, q0:q0 + nq, 32 + d * 16:48 + d * 16], start=True, stop=True),
                    reads=[b_sc, b_c], writes=[b_pb[6]])
                P.op("act", lambda e, d=d, q0=q0, nq=nq: e.copy(
                    out=gcs[:, q0:q0 + nq, d * 16:(d + 1) * 16],
                    in_=pb[6][0:64, 0:nq * 16].rearrange("p (c n) -> p c n", n=16)), reads=[b_pb[6]], writes=[b_sc])
            P.op("pe", lambda e, q0=q0, nq=nq: e.matmul(
                pb[0][0:64, 0:nq * 32].rearrange("p (c n) -> p c n", n=32), lhsT=ones[:, 0:64],
                rhs=bg[:, q0:q0 + nq, 32:64], start=True, stop=True), reads=[b_sc, b_c], writes=[b_pb[0]])
            P.op("dve", lambda e, q0=q0, nq=nq: e.tensor_tensor(
                out=ekl[:, q0:q0 + nq, :], in0=pb[0][0:64, 0:nq * 32].rearrange("p (c n) -> p c n", n=32),
                in1=gcs[:, q0:q0 + nq, :], op=ALU.subtract), reads=[b_pb[0], b_sc], writes=[b_sc])
            P.op("pe", lambda e, q0=q0, nq=nq: e.matmul(
                pb[1][:, 0:nq * 32].rearrange("p (c n) -> p c n", n=32), lhsT=ones[:, :],
                rhs=bg[:, q0:q0 + nq, 32:64], start=True, stop=True), reads=[b_sc, b_c], writes=[b_pb[1]])
            P.op("act", lambda e, q0=q0, nq=nq: e.activation(
                out=eglS[:, q0:q0 + nq, :], in_=pb[1][:, 0:nq * 32].rearrange("p (c n) -> p c n", n=32), func=AF.Exp),
                reads=[b_pb[1]], writes=[b_sc])
        P.op("act", lambda e: e.activation(out=ekl[:], in_=ekl[:], func=AF.Exp), reads=[b_sc], writes=[b_sc])
        P.op("act", lambda e: e.activation(out=egc[:], in_=gcs[:], func=AF.Exp), reads=[b_sc], writes=[b_sc])
        P.op("dve", lambda e: e.tensor_tensor(out=bege[:], in0=bg[:, :, 0:32], in1=egc[:], op=ALU.mult), reads=[b_sc],
             writes=[b_sc])
        qkv = cx.sb([128, 3, 4, NT], BF16, "qkv")
        b_qkv = Buf()
        of = cx.sb([64, NCH, 512], F32, "of")
        b_of = [Buf() for _ in range(NCH)]

        def bc(ap2, n):
            return ap2.unsqueeze(2).broadcast_to([ap2.shape[0], 4, n])

        def mk_chain(i):
            ch = _Ch()
            ch.Y = [pb[3 * i + k] for k in range(3)]
            ch.bY = [b_pb[3 * i + k] for k in range(3)]
            if DN_ALT:
                ch.Y = [pb[0], pb[1], pb[2]]
                ch.bY = [b_pb[0], b_pb[1], b_pb[2]]
                ch.YB, ch.bYB, ch.oB = pb[3], b_pb[3], 0
                ch.YQ, ch.bYQ, ch.oQ = pb[4], b_pb[4], 0
            else:
                ch.YB, ch.bYB, ch.oB = ch.Y[2], ch.bY[2], 256
                ch.YQ, ch.bYQ, ch.oQ = ch.Y[0], ch.bY[0], 0

            def t64(name, dt=F32, n=64):
                return cx.sb([64, 4, n], dt, name + str(i)), Buf()

            ch.S = cx.sb([128, 4, 128], F32, "S%d" % i)
            ch.Sb = cx.sb([128, 4, 128], BF16, "Sb%d" % i)
            ch.b_S, ch.b_Sb = Buf(), Buf()
            ch.dg, ch.b_dg = t64("dg")
            ch.ndg, ch.b_ndg = t64("ndg")
            ch.E1, ch.b_E1 = t64("E1")
            ch.E2, ch.b_E2 = t64("E2")
            ch.AB = [cx.sb([64, 8, 64], F32, "AB%d_%d" % (k, i)) for k in range(2)]
            ch.bAB = [Buf(), Buf()]
            ch.Lm = [(ch.AB[k][:, 0:4, :], ch.bAB[k]) for k in range(2)]
            ch.Bm = [(ch.AB[k][:, 4:8, :], ch.bAB[k]) for k in range(2)]
            ch.Q, ch.b_Q = t64("Q")
            ch.Qb, ch.b_Qb = t64("Qb", BF16)
            ch.at, ch.b_at = t64("at", BF16)
            ch.kbg, ch.b_kbg = t64("kbg", BF16, 128)
            ch.kg, ch.b_kg = t64("kg", BF16, 128)
            ch.vb, ch.b_vb = t64("vb", BF16, 128)
            ch.u, ch.b_u = t64("u", F32, 128)
            ch.vn, ch.b_vn = t64("vn", BF16, 128)
            ch.ot, ch.b_ot = t64("ot", F32, 128)
            ch.o2, ch.b_o2 = t64("o2", F32, 128)
            ch.wT = cx.sb([128, 4, 64], BF16, "wT%d" % i)
            ch.b_wT = Buf()
            ch.St = cx.sb([128, 4, 128], F32, "St%d" % i)
            ch.b_St = Buf()
            ch.ss = cx.sb([64, 4], F32, "ss%d" % i)
            ch.b_ss = Buf()
            ch.gzt = cx.sb([64, 512], BF16, "gzt%d" % i)
            ch.b_gzt = Buf()
            ch.ogd = cx.sb([64, 512], BF16, "ogd%d" % i)
            ch.b_ogd = Buf()
            ch.ogT = cx.sb([128, 4, 64], BF16, "ogT%d" % i)
            ch.b_ogT = Buf()
            return ch

        chains = [mk_chain(0), mk_chain(1)]

        def step(ch, gi, d, s0, NC, s):
            Y, bY = ch.Y, ch.bY
            cols = slice(d * 16 + gi * 4, d * 16 + gi * 4 + 4)
            c = s if d == 0 else NC - 1 - s
            finalize = (c >= NC // 2) if d == 0 else (c < NC // 2)
            t0 = s0 + c * 64
            cc = t0 // 64
            kc_ = qkv[:, 1, :, t0:t0 + 64]
            qc_ = qkv[:, 0, :, t0:t0 + 64]
            vc_ = qkv[:, 2, :, t0:t0 + 64]
            h64 = lambda ap: ap.rearrange("p (h n) -> p h n", n=64)
            h128 = lambda ap: ap.rearrange("p (h n) -> p h n", n=128)
            for hh in range(4):
                P.op("pe", lambda e, hh=hh: e.matmul(Y[0][0:64, hh * 128:hh * 128 + 64], lhsT=kc_[:, hh, :],
                                                     rhs=kc_[:, hh, :], start=True, stop=True),
                     reads=[b_qkv], writes=[bY[0]])
                P.op("pe", lambda e, hh=hh: e.matmul(Y[0][0:64, hh * 128 + 64:hh * 128 + 128], lhsT=kc_[:, hh, :],
                                                     rhs=qc_[:, hh, :], start=True, stop=True),
                     reads=[b_qkv], writes=[bY[0]])
            GA = h128(Y[0][0:64, :])
            P.op("dve", lambda e: e.tensor_tensor(out=ch.dg[:], in0=id4[:], in1=bc(gcs[:, cc, cols], 64), op=ALU.mult),
                 reads=[b_c, b_sc], writes=[ch.b_dg])
            for hh in range(4):
                P.op("pe", lambda e, hh=hh: e.matmul(Y[1][0:64, hh * 64:(hh + 1) * 64], lhsT=ones[:, 0:64],
                                                     rhs=ch.dg[:, hh, :], start=True, stop=True),
                     reads=[ch.b_dg, b_c], writes=[bY[1]])
            yield
            Dm = h64(Y[1][0:64, 0:256])
            P.op("dve", lambda e: e.tensor_tensor(
                out=ch.E1[:], in0=cst[:, 3 + d, :].unsqueeze(1).broadcast_to([64, 4, 64]), in1=Dm, op=ALU.subtract),
                reads=[bY[1], b_c], writes=[ch.b_E1])
            P.op("dve", lambda e: e.tensor_tensor(
                out=ch.E2[:], in0=Dm, in1=cst[:, 5 + d, :].unsqueeze(1).broadcast_to([64, 4, 64]), op=ALU.add),
                reads=[bY[1], b_c], writes=[ch.b_E2])
            P.op("dve", lambda e: e.tensor_tensor(out=ch.E1[:], in0=ch.E1[:], in1=bc(gcs[:, cc, cols], 64), op=ALU.add),
                 reads=[ch.b_E1, b_sc], writes=[ch.b_E1])
            P.op("dve", lambda e: e.tensor_tensor(out=ch.E2[:], in0=ch.E2[:], in1=bc(gcs[:, cc, cols], 64),
                                                  op=ALU.subtract), reads=[ch.b_E2, b_sc], writes=[ch.b_E2])
            P.op("act", lambda e: e.activation(out=ch.E1[:], in_=ch.E1[:], func=AF.Exp), reads=[ch.b_E1],
                 writes=[ch.b_E1])
            P.op("act", lambda e: e.activation(out=ch.E2[:], in_=ch.E2[:], func=AF.Exp), reads=[ch.b_E2],
                 writes=[ch.b_E2])
            yield
            A0, b_A0 = ch.Lm[0]
            P.op("dve", lambda e: e.tensor_tensor(out=A0, in0=GA[:, :, 0:64], in1=ch.E1[:], op=ALU.mult),
                 reads=[bY[0], ch.b_E1], writes=[b_A0])
            P.op("dve", lambda e: e.tensor_tensor(out=A0, in0=A0, in1=bc(bg[:, cc, cols], 64), op=ALU.mult),
                 reads=[b_sc, b_A0], writes=[b_A0])
            P.op("dve", lambda e: e.tensor_tensor(out=ch.at[:], in0=GA[:, :, 64:128], in1=ch.E2[:], op=ALU.mult),
                 reads=[bY[0], ch.b_E2], writes=[ch.b_at])
            B0, b_B0 = ch.Bm[0]
            for hh in range(4):
                P.op("pe", lambda e, hh=hh: e.transpose(out=Y[1][0:64, hh * 64:(hh + 1) * 64], in_=A0[:, hh, :],
                                                        identity=cst[:, 0, :]), reads=[b_A0, b_c], writes=[bY[1]])
            yield
            for hh in range(4):
                P.op("pe", lambda e, hh=hh: e.transpose(out=pTr[0:64, hh * 128:(hh + 1) * 128], in_=kc_[:, hh, :],
                                                        identity=identb[:]), reads=[b_qkv, b_c], writes=[b_pTr])
                P.op("pe", lambda e, hh=hh: e.transpose(out=pTr[0:64, 512 + hh * 128:512 + (hh + 1) * 128],
                                                        in_=vc_[:, hh, :], identity=identb[:]),
                     reads=[b_qkv, b_c], writes=[b_pTr])
            P.op("act", lambda e: e.copy(out=B0, in_=Dm), reads=[bY[1]], writes=[b_B0])
            P.op("dve", lambda e: e.tensor_tensor(out=ch.Q[:], in0=id4[:], in1=B0, op=ALU.subtract),
                 reads=[b_c, b_B0], writes=[ch.b_Q])
            ktr = h128(pTr[0:64, 0:512])
            vtr = h128(pTr[0:64, 512:1024])
            P.op("dve", lambda e: e.tensor_tensor(out=ch.kbg[:], in0=ktr, in1=bc(bege[:, cc, cols], 128), op=ALU.mult),
                 reads=[b_pTr, b_sc], writes=[ch.b_kbg])
            P.op("dve", lambda e: e.tensor_tensor(out=ch.kg[:], in0=ktr, in1=bc(ekl[:, cc, cols], 128), op=ALU.mult),
                 reads=[b_pTr, b_sc], writes=[ch.b_kg])
            P.op("dve", lambda e: e.tensor_tensor(out=ch.vb[:], in0=vtr, in1=bc(bg[:, cc, cols], 128), op=ALU.mult),
                 reads=[b_pTr, b_sc], writes=[ch.b_vb])
            yield

            def qupd(Ak, b_Ak):
                for hh in range(4):
                    P.op("pe", lambda e, hh=hh: e.matmul(ch.YQ[0:64, ch.oQ + hh * 64:ch.oQ + (hh + 1) * 64],
                                                         lhsT=Ak[:, hh, :], rhs=ch.Q[:, hh, :], start=True, stop=True),
                         reads=[b_Ak, ch.b_Q], writes=[ch.bYQ])

            def qadd():
                P.op("dve", lambda e: e.tensor_tensor(out=ch.Q[:], in0=ch.Q[:],
                                                      in1=h64(ch.YQ[0:64, ch.oQ:ch.oQ + 256]), op=ALU.add),
                     reads=[ch.bYQ, ch.b_Q], writes=[ch.b_Q])

            cur = 0
            for lvl in range(1, 6):
                A, b_A = ch.Lm[cur]
                B, b_B = ch.Bm[cur]
                An, b_An = ch.Lm[1 - cur]
                Bn, b_Bn = ch.Bm[1 - cur]
                for hh in range(4):
                    P.op("pe", lambda e, hh=hh, A=A, B=B: e.matmul(Y[2][0:64, hh * 64:(hh + 1) * 64], lhsT=B[:, hh, :],
                                                                   rhs=A[:, hh, :], start=True, stop=True),
                         reads=[b_A, b_B], writes=[bY[2]])
                if lvl < 5:
                    for hh in range(4):
                        P.op("pe", lambda e, hh=hh, A=A, B=B: e.matmul(
                            ch.YB[0:64, ch.oB + hh * 64:ch.oB + (hh + 1) * 64], lhsT=A[:, hh, :], rhs=B[:, hh, :],
                            start=True, stop=True), reads=[b_A, b_B], writes=[ch.bYB])
                if lvl >= 2:
                    qupd(A, b_A)
                yield
                if lvl < 5 and not DN_ALT:
                    ABn = ch.AB[1 - cur]
                    P.op("act", lambda e, ABn=ABn: e.copy(out=ABn[:], in_=h64(Y[2][0:64, 0:512])), reads=[bY[2]],
                         writes=[b_An])
                else:
                    P.op("act", lambda e, An=An: e.copy(out=An, in_=h64(Y[2][0:64, 0:256])), reads=[bY[2]],
                         writes=[b_An])
                    if lvl < 5:
                        P.op("dve", lambda e, Bn=Bn: e.tensor_copy(out=Bn, in_=h64(ch.YB[0:64, ch.oB:ch.oB + 256])),
                             reads=[ch.bYB], writes=[b_Bn])
                if lvl >= 2:
                    qadd()
                yield
                cur = 1 - cur
            A, b_A = ch.Lm[cur]
            qupd(A, b_A)
            yield
            qadd()
            P.op("act", lambda e: e.copy(out=ch.Qb[:], in_=ch.Q[:]), reads=[ch.b_Q], writes=[ch.b_Qb])
            yield
            for hh in range(4):
                P.op("pe", lambda e, hh=hh: e.matmul(Y[0][0:64, hh * 128:(hh + 1) * 128], lhsT=ch.Qb[:, hh, :],
                                                     rhs=ch.vb[:, hh, :], start=True, stop=True),
                     reads=[ch.b_Qb, ch.b_vb], writes=[bY[0]])
                P.op("pe", lambda e, hh=hh: e.matmul(Y[1][:, hh * 64:(hh + 1) * 64], lhsT=ch.kbg[:, hh, :],
                                                     rhs=ch.Qb[:, hh, :], start=True, stop=True),
                     reads=[ch.b_Qb, ch.b_kbg], writes=[bY[1]])
            yield
            P.op("act", lambda e: e.copy(out=ch.u[:], in_=h128(Y[0][0:64, :])), reads=[bY[0]], writes=[ch.b_u])
            P.op("act", lambda e: e.copy(out=ch.wT[:], in_=h64(Y[1][:, 0:256])), reads=[bY[1]], writes=[ch.b_wT])
            yield
            for hh in range(4):
                P.op("pe", lambda e, hh=hh: e.matmul(Y[0][0:64, hh * 128:(hh + 1) * 128], lhsT=ch.wT[:, hh, :],
                                                     rhs=ch.Sb[:, hh, :], start=True, stop=True),
                     reads=[ch.b_wT, ch.b_Sb], writes=[bY[0]])
            for hh in range(4):
                P.op("pe", lambda e, hh=hh: e.matmul(Y[2][0:64, hh * 128:(hh + 1) * 128], lhsT=qc_[:, hh, :],
                                                     rhs=ch.Sb[:, hh, :], start=True, stop=True),
                     reads=[b_qkv, ch.b_Sb], writes=[bY[2]])
            yield
            P.op("dve", lambda e: e.tensor_tensor(out=ch.vn[:], in0=ch.u[:], in1=h128(Y[0][0:64, :]), op=ALU.subtract),
                 reads=[ch.b_u, bY[0]], writes=[ch.b_vn])
            P.op("dve", lambda e: e.tensor_tensor(out=ch.ot[:], in0=h128(Y[2][0:64, :]), in1=bc(egc[:, cc, cols], 128),
                                                  op=ALU.mult), reads=[bY[2], b_sc], writes=[ch.b_ot])
            yield
            for hh in range(4):
                P.op("pe", lambda e, hh=hh: e.matmul(Y[1][0:64, hh * 128:(hh + 1) * 128], lhsT=ch.at[:, hh, :],
                                                     rhs=ch.vn[:, hh, :], start=True, stop=True),
                     reads=[ch.b_at, ch.b_vn], writes=[bY[1]])
                P.op("pe", lambda e, hh=hh: e.matmul(Y[2][:, hh * 128:(hh + 1) * 128], lhsT=ch.kg[:, hh, :],
                                                     rhs=ch.vn[:, hh, :], start=True, stop=True),
                     reads=[ch.b_kg, ch.b_vn], writes=[bY[2]])
            P.op("dve", lambda e: e.tensor_tensor(out=ch.St[:], in0=ch.S[:],
                                                  in1=eglS[:, cc, cols].unsqueeze(2).broadcast_to([128, 4, 128]),
                                                  op=ALU.mult), reads=[ch.b_S, b_sc], writes=[ch.b_St])
            yield
            P.op("dve", lambda e: e.tensor_tensor(out=ch.S[:], in0=ch.St[:], in1=h128(Y[2][:, :]), op=ALU.add),
                 reads=[ch.b_St, bY[2]], writes=[ch.b_S])
            P.op("act", lambda e: e.copy(out=ch.Sb[:], in_=ch.S[:]), reads=[ch.b_S], writes=[ch.b_Sb])
            ofc = h128(of[:, cc, :])
            if not finalize:
                P.op("dve", lambda e: e.tensor_tensor(out=ofc, in0=ch.ot[:], in1=h128(Y[1][0:64, :]), op=ALU.add),
                     reads=[ch.b_ot, bY[1]], writes=[b_of[cc]])
                yield
            else:
                P.dma("sp", lambda e: e.dma_start(out=ch.gzt[:],
                                                  in_=G["dzg"][tok0 + t0:tok0 + t0 + 64, gi * 512:(gi + 1) * 512]),
                      writes=[ch.b_gzt])
                P.op("dve", lambda e: e.tensor_tensor(out=ch.ot[:], in0=ch.ot[:], in1=h128(Y[1][0:64, :]), op=ALU.add),
                     reads=[ch.b_ot, bY[1]], writes=[ch.b_ot])
                P.op("dve", lambda e: e.tensor_tensor(out=ch.o2[:], in0=ch.ot[:], in1=ofc, op=ALU.add),
                     reads=[ch.b_ot, b_of[cc]], writes=[ch.b_o2])
                yield
                P.op("dve", lambda e: e.tensor_tensor(out=ch.ot[:], in0=ch.o2[:], in1=ch.o2[:], op=ALU.mult),
                     reads=[ch.b_o2], writes=[ch.b_ot])
                P.op("dve", lambda e: e.reduce_sum(out=ch.ss[:], in_=ch.ot[:], axis=AX.X), reads=[ch.b_ot],
                     writes=[ch.b_ss])
                P.op("act", lambda e: e.activation(out=ch.ss[:], in_=ch.ss[:], func=AF.Sqrt, bias=RMS_EPS,
                                                   scale=1.0 / 128.0), reads=[ch.b_ss], writes=[ch.b_ss])
                yield
                P.op("dve", lambda e: e.reciprocal(out=ch.ss[:], in_=ch.ss[:]), reads=[ch.b_ss], writes=[ch.b_ss])
                P.op("dve", lambda e: e.tensor_tensor(out=ch.o2[:], in0=ch.o2[:], in1=bc(ch.ss[:], 128), op=ALU.mult),
                     reads=[ch.b_ss, ch.b_o2], writes=[ch.b_o2])
                P.op("dve", lambda e: e.tensor_tensor(out=ch.o2[:], in0=ch.o2[:],
                                                      in1=h128(dnw[:, :]), op=ALU.mult),
                     reads=[b_dnw, ch.b_o2], writes=[ch.b_o2])
                P.op("dve", lambda e: e.tensor_tensor(out=h128(ch.ogd[:]), in0=ch.o2[:], in1=h128(ch.gzt[:]),
                                                      op=ALU.mult), reads=[ch.b_o2, ch.b_gzt], writes=[ch.b_ogd])
                yield
                for hh in range(4):
                    P.op("pe", lambda e, hh=hh: e.transpose(out=pTr[:, hh * 64:(hh + 1) * 64],
                                                            in_=ch.ogd[:, hh * 128:(hh + 1) * 128],
                                                            identity=identb[0:64, 0:64]),
                         reads=[ch.b_ogd, b_c], writes=[b_pTr])
                P.op("act", lambda e: e.copy(out=ch.ogT[:].rearrange("p h n -> p (h n)"), in_=pTr[:, 0:256]),
                     reads=[b_pTr], writes=[ch.b_ogT])
                P.dma(STQ, lambda e: e.dma_start(
                    out=G["og"][4096 + gi * 512:4096 + (gi + 1) * 512, tok0 + t0:tok0 + t0 + 64].rearrange(
                        "(h p) t -> p h t", p=128), in_=ch.ogT[:]), reads=[ch.b_ogT])
                yield

        for gi in range(ngroups):
            P.dma("sp", lambda e, gi=gi: e.dma_start(out=dnw[:], in_=G["dnw_bc"][:, gi * 512:(gi + 1) * 512]),
                  writes=[b_dnw])
            for X in range(3):
                P.dma("sp", lambda e, X=X, gi=gi: e.dma_start(
                    out=qkv[:, X, :, :], in_=G["dqkv"][X, gi * 4:(gi + 1) * 4, :, tok0:tok0 + NT].rearrange("h p t -> p h t")),
                    writes=[b_qkv])
            for (s0, T, kind, sidx) in seqs:
                NC = T // 64
                for d in range(2):
                    ch = chains[d]
                    if kind == "S":
                        P.dma("sp", lambda e, d=d, gi=gi, ch=ch: e.dma_start(
                            out=ch.S[:], in_=G["state_dn"][d, gi * 4:(gi + 1) * 4].rearrange("h k v -> k h v")),
                            writes=[ch.b_S])
                    else:
                        P.op("pool", lambda e, ch=ch: e.memset(ch.S[:], 0.0), writes=[ch.b_S])
                    P.op("act", lambda e, ch=ch: e.copy(out=ch.Sb[:], in_=ch.S[:]), reads=[ch.b_S], writes=[ch.b_Sb])
                for s in range(NC):
                    gens = [step(chains[0], gi, 0, s0, NC, s), step(chains[1], gi, 1, s0, NC, s)]
                    live = [True, True]
                    if DN_SEQ:
                        for g_ in gens:
                            for _ in g_:
                                pass
                        live = [False, False]
                    while any(live):
                        for k in range(2):
                            if live[k]:
                                try:
                                    next(gens[k])
                                except StopIteration:
                                    live[k] = False
                if kind == "P":
                    for d in range(2):
                        P.dma(STQ, lambda e, d=d, gi=gi, sidx=sidx: e.dma_start(
                            out=G["new_dn"][sidx, d, gi * 4:(gi + 1) * 4].rearrange("h k v -> k h v"), in_=chains[d].S[:]),
                            reads=[chains[d].b_S])
        P.wait_all_dma()
        with nc.Block() as block:
            P.emit(block)


ALPHA = 2.0 ** 0.25


def stage_modrow(nc, sync, G):
    with ExitStack() as es:
        cx = Ctx(nc, es)
        P = Prog(sync)
        scT = cx.sb([128, KC, 2], F32, "scT")
        brow = cx.sb([2, 4096], F32, "brow")
        mrow = cx.sb([2, 4096], F32, "mrow")
        wb = [cx.sb([128, KC, 512], F32, "wada") for _ in range(2)]
        ps = [cx.ps([128, 512], F32, "psm") for _ in range(2)]
        b_sc, b_br, b_mr = Buf(), Buf(), Buf()
        b_wb = [Buf(), Buf()]
        b_ps = [Buf(), Buf()]
        P.dma("sp", lambda e: e.dma_start(out=scT[:], in_=G["condT"][:, :, :]), writes=[b_sc])
        P.dma("sp", lambda e: e.dma_start(out=brow[:], in_=G["b_adarow"][:, :]), writes=[b_br])
        P.op("act", lambda e: e.activation(out=scT[:], in_=scT[:], func=AF.Silu), reads=[b_sc], writes=[b_sc])
        wv = G["w_ada"].rearrange("(kc p) n -> p kc n", p=128)
        for i, nb in enumerate(list(range(8, 12)) + list(range(20, 24))):
            sl = i % 2
            P.dma("sp", lambda e, nb=nb, sl=sl: e.dma_start(out=wb[sl][:], in_=wv[:, :, nb * 512:(nb + 1) * 512]),
                  writes=[b_wb[sl]])
            mm_acc(P, ps[sl][0:2, :], b_ps[sl], [(scT[:, kc, :], wb[sl][:, kc, :]) for kc in range(KC)],
                   [b_wb[sl], b_sc])
            P.op("dve", lambda e, i=i, sl=sl: e.tensor_tensor(out=mrow[:, i * 512:(i + 1) * 512], in0=ps[sl][0:2, :],
                                                               in1=brow[:, i * 512:(i + 1) * 512], op=ALU.add),
                 reads=[b_ps[sl], b_br], writes=[b_mr])
        P.dma(STQ, lambda e: e.dma_start(out=G["modrow"][:, :], in_=mrow[:]), reads=[b_mr])
        P.wait_all_dma()
        with nc.Block() as block:
            P.emit(block)


def stage_d1(nc, sync, G):
    with ExitStack() as es:
        cx = Ctx(nc, es)
        P = Prog(sync)
        W = WStream(P, cx, nbuf=4, nstg=4, engines=("act", "dve", "pool"))
        ogr = [cx.sb([128, 32, 512], BF16, "ogr") for _ in range(2)]
        ogd = cx.sb([128, 16, 512], BF16, "ogd")
        sgt = [cx.sb([128, 8, 512], BF16, "sgt") for _ in range(2)]
        b_ogr = [Buf(), Buf()]
        b_ogd = Buf()
        b_sg = [Buf(), Buf()]
        yo = [cx.sb([128, 4, 512], BF16, "yo") for _ in range(2)]
        b_yo = [Buf(), Buf()]
        t1 = cx.sb([128, 512], F32, "t1")
        t2 = cx.sb([128, 512], F32, "t2")
        b_t1, b_t2 = Buf(), Buf()
        pr = [cx.ps([128, 512], F32, "pr") for _ in range(4)]
        pd = [cx.ps([128, 512], F32, "pd") for _ in range(4)]
        b_pr = [Buf() for _ in range(4)]
        b_pd = [Buf() for _ in range(4)]
        it = 0
        for cb in range(4):
            cs_ = slice(cb * 512, (cb + 1) * 512)
            w0, b_w0 = W.load(G["w_br"][0:2048, cs_], 512)
            w1, b_w1 = W.load(G["w_br"][2048:4096, cs_], 512)
            w2, b_w2 = W.load(G["w_bd"][:, cs_], 512)
            for tt in range(NTOK // 512):
                tsl = slice(tt * 512, (tt + 1) * 512)
                k = it % 2
                it += 1
                orr, b_orr = ogr[k], b_ogr[k]
                sg_, b_sg_ = sgt[k], b_sg[k]
                P.dma("sp", lambda e, tsl=tsl, orr=orr: e.dma_start(
                    out=orr[:], in_=G["og"][0:4096, tsl].rearrange("(a p) t -> p a t", p=128)), writes=[b_orr])
                P.dma("sp", lambda e, tsl=tsl: e.dma_start(
                    out=ogd[:], in_=G["og"][4096:6144, tsl].rearrange("(a p) t -> p a t", p=128)), writes=[b_ogd])
                P.dma("sp", lambda e, tsl=tsl, sg_=sg_, cb=cb: e.dma_start(
                    out=sg_[:, 0:4, :], in_=G["sg"][cb * 512:(cb + 1) * 512, tsl].rearrange("(a p) t -> p a t", p=128)),
                    writes=[b_sg_])
                P.dma("sp", lambda e, tsl=tsl, sg_=sg_, cb=cb: e.dma_start(
                    out=sg_[:, 4:8, :],
                    in_=G["sg"][2048 + cb * 512:2048 + (cb + 1) * 512, tsl].rearrange("(a p) t -> p a t", p=128)),
                    writes=[b_sg_])
                o2, b_o2 = yo[k], b_yo[k]
                for sub in range(4):
                    ss_ = slice(sub * 128, (sub + 1) * 128)
                    pairs = [(w0[:, kc, ss_], orr[:, kc, :]) for kc in range(KC)]
                    pairs += [(w1[:, kc, ss_], orr[:, 16 + kc, :]) for kc in range(KC)]
                    mm_acc(P, pr[sub][:, :], b_pr[sub], pairs, [b_w0, b_w1, b_orr])
                for sub in range(4):
                    ss_ = slice(sub * 128, (sub + 1) * 128)
                    mm_acc(P, pd[sub][:, :], b_pd[sub], [(w2[:, kc, ss_], ogd[:, kc, :]) for kc in range(KC)],
                           [b_w2, b_ogd])
                for sub in range(4):
                    P.op("dve", lambda e, sub=sub, sg_=sg_: e.tensor_tensor(out=t1[:], in0=pr[sub][:, :], in1=sg_[:, sub, :],
                                                                            op=ALU.mult), reads=[b_pr[sub], b_sg_],
                         writes=[b_t1])
                    P.op("dve", lambda e, sub=sub, sg_=sg_: e.tensor_tensor(out=t2[:], in0=pd[sub][:, :],
                                                                            in1=sg_[:, 4 + sub, :], op=ALU.mult),
                         reads=[b_pd[sub], b_sg_], writes=[b_t2])
                    P.op("pool", lambda e, o2=o2, sub=sub: e.tensor_tensor(out=o2[:, sub, :], in0=t1[:], in1=t2[:],
                                                                             op=ALU.add), reads=[b_t1, b_t2], writes=[b_o2])
                P.dma(STQ, lambda e, o2=o2, cs_=cs_, tsl=tsl: e.dma_start(
                    out=G["yT"][cs_, tsl].rearrange("(a p) t -> p a t", p=128), in_=o2[:]), reads=[b_o2])
        P.wait_all_dma()
        with nc.Block() as block:
            P.emit(block)


def row_bcast(P, cx, G, c, lo, dst, b_dst, sel, b_sel, mr, b_mr, ps, b_ps):
    P.dma("sp", lambda e: e.dma_start(out=mr[:], in_=G["modrow"][:, lo:lo + 2048]), writes=[b_mr])
    for j in range(4):
        P.op("pe", lambda e, j=j: e.matmul(ps[:, :], lhsT=sel[:, c, :], rhs=mr[:, j * 512:(j + 1) * 512], start=True,
                                            stop=True), reads=[b_sel, b_mr], writes=[b_ps])
        P.op("act", lambda e, j=j: e.copy(out=dst[:, j * 512:(j + 1) * 512], in_=ps[:, :]), reads=[b_ps], writes=[b_dst])


def ln_affine_store(P, r_t, b_r, scr, gt, bt, b_gb, out_t, b_out):
    ln_rows(P, None, r_t, b_r, out_t, b_out, scr)
    P.op("pool", lambda e: e.tensor_tensor(out=out_t[:], in0=out_t[:], in1=gt[:], op=ALU.mult), reads=[b_gb, b_out],
         writes=[b_out])
    P.op("pool", lambda e: e.tensor_tensor(out=out_t[:], in0=out_t[:], in1=bt[:], op=ALU.add), reads=[b_gb, b_out],
         writes=[b_out])


def stage_d2(nc, sync, G):
    with ExitStack() as es:
        cx = Ctx(nc, es)
        P = Prog(sync)
        W = WStream(P, cx, nbuf=2, nstg=4, engines=("act", "act", "pool"))
        yts = [cx.sb([128, KC, 512], BF16, "yt") for _ in range(2)]
        b_yts = [Buf(), Buf()]
        rs = [cx.sb([128, 4, D], F32, "r") for _ in range(2)]
        b_rs = [[Buf() for _ in range(4)] for _ in range(2)]
        grow = cx.sb([128, D], F32, "grow")
        b_grow = Buf()
        lg_ = cx.sb([128, D], F32, "lng")
        lb_ = cx.sb([128, D], F32, "lnb")
        b_gb = Buf()
        x1 = cx.sb([128, D], F32, "x1")
        b_x1 = Buf()
        xn = cx.sb([128, D], BF16, "xn")
        b_xn = Buf()
        h2 = cx.sb([128, KC, 128], BF16, "h2")
        b_h2 = Buf()
        sel = cx.sb([2, 2, 128], F32, "sel")
        mr = cx.sb([2, 2048], F32, "mr")
        b_sel, b_mr = Buf(), Buf()
        ident = cx.sb([128, 128], BF16, "ident")
        scr = {"st": cx.sb([128, 4, 6], F32), "mv": cx.sb([128, 2], F32), "rstd": cx.sb([128, 1], F32), "b_st": Buf()}
        ps = [cx.ps([128, 512], F32, "pm") for _ in range(4)]
        b_ps = [Buf() for _ in range(4)]
        pT = [cx.ps([128, 1024], BF16, "pT") for _ in range(2)]
        b_pT = [Buf(), Buf()]
        modT = G["modT"]
        P.dma("sp", lambda e: e.dma_start(out=sel[:], in_=G["sel"][:, :, :]), writes=[b_sel])
        P.dma("sp", lambda e: e.dma_start(out=ident[:], in_=G["ident_bf"][:, :]), writes=[b_sel])
        P.dma("sp", lambda e: e.dma_start(out=lg_[:], in_=G["ln1g_bc"][:, :]), writes=[b_gb])
        P.dma("sp", lambda e: e.dma_start(out=lb_[:], in_=G["ln1b_bc"][:, :]), writes=[b_gb])
        mtmp = cx.sb([128, 512], F32, "mtmp")
        b_mtmp = Buf()
        state = {"last_c": None}

        def mm_phase(tt):
            c = 0 if tt == 0 else 1
            k = tt % 2
            tsl = slice(tt * 512, (tt + 1) * 512)
            rr, b_rr, yt_, b_yt_ = rs[k], b_rs[k], yts[k], b_yts[k]
            if c != state["last_c"]:
                row_bcast(P, cx, G, c, 0, grow, b_grow, sel, b_sel, mr, b_mr, ps[0], b_ps[0])
                state["last_c"] = c
            P.dma("sp", lambda e: e.dma_start(out=yt_[:], in_=G["yT"][:, tsl].rearrange("(a p) t -> p a t", p=128)),
                  writes=[b_yt_])
            for ts in range(4):
                P.dma("sp", lambda e, ts=ts: e.dma_start(
                    out=rr[:, ts, :], in_=G["x"][tt * 512 + ts * 128:tt * 512 + (ts + 1) * 128, :]), writes=[b_rr[ts]])
            for cb in range(4):
                cs_ = slice(cb * 512, (cb + 1) * 512)
                w, b_w = W.load(G["w_o"][:, cs_], 512)
                for ts in range(4):
                    mm_acc(P, ps[ts][:, :], b_ps[ts], [(yt_[:, kc, ts * 128:(ts + 1) * 128], w[:, kc, :]) for kc in range(KC)],
                           [b_w, b_yt_])
                    P.op("dve", lambda e, ts=ts, cs_=cs_: e.tensor_tensor(out=mtmp[:], in0=ps[ts][:, :], in1=grow[:, cs_],
                                                                           op=ALU.mult),
                         reads=[b_ps[ts], b_grow], writes=[b_mtmp])
                    P.op("dve", lambda e, ts=ts, cs_=cs_: e.scalar_tensor_tensor(
                        out=rr[:, ts, cs_], in0=rr[:, ts, cs_], scalar=ALPHA, in1=mtmp[:], op0=ALU.mult, op1=ALU.add),
                        reads=[b_mtmp, b_rr[ts]], writes=[b_rr[ts]])

        def ln_phase(tt):
            c = 0 if tt == 0 else 1
            k = tt % 2
            rr, b_rr = rs[k], b_rs[k]
            for ts in range(4):
                tok = tt * 512 + ts * 128
                ln_affine_store(P, rr[:, ts, :], b_rr[ts], scr, lg_, lb_, b_gb, x1, b_x1)
                P.dma(STQ, lambda e, tok=tok: e.dma_start(out=G["x1"][tok:tok + 128, :], in_=x1[:]), reads=[b_x1])
                ln_rows(P, None, x1, b_x1, xn, b_xn, scr)
                for g in range(2):
                    for j in range(8):
                        kc = g * 8 + j
                        P.op("pe", lambda e, g=g, j=j, kc=kc: e.transpose(
                            out=pT[g][:, j * 128:(j + 1) * 128], in_=xn[:, kc * 128:(kc + 1) * 128], identity=ident[:]),
                            reads=[b_xn, b_sel], writes=[b_pT[g]])
                    for j in range(8):
                        kc = g * 8 + j
                        P.op("act", lambda e, g=g, j=j, kc=kc, c=c: e.activation(
                            out=h2[:, kc, :], in_=pT[g][:, j * 128:(j + 1) * 128], func=AF.Identity,
                            scale=modT[:, 64 + kc, c:c + 1], bias=modT[:, 48 + kc, c:c + 1]), reads=[b_pT[g]],
                            writes=[b_h2])
                P.dma(STQ, lambda e, tok=tok: e.dma_start(
                    out=G["h2T"][:, tok:tok + 128].rearrange("(a p) t -> p a t", p=128), in_=h2[:]), reads=[b_h2])

        NTT = NTOK // 512
        for tt in range(NTT):
            mm_phase(tt)
            if tt > 0:
                ln_phase(tt - 1)
        ln_phase(NTT - 1)
        P.wait_all_dma()
        with nc.Block() as block:
            P.emit(block)


SEQS_ALL = [(0, 256), (256, 256), (512, 2048)]


def stage_e1(nc, sync, G):
    with ExitStack() as es:
        cx = Ctx(nc, es)
        P = Prog(sync)
        W = WStream(P, cx, nbuf=4, nstg=4, engines=("act", "act", "pool", "act"))
        h2 = cx.sb([128, KC, NTOK], BF16, "h2")
        b_h2 = Buf()
        P.dma("sp", lambda e: e.dma_start(out=h2[:, :, 0:1280], in_=G["h2T"][:, 0:1280].rearrange("(a p) t -> p a t", p=128)),
              writes=[b_h2])
        P.dma("sp", lambda e: e.dma_start(out=h2[:, :, 1280:2560],
                                          in_=G["h2T"][:, 1280:2560].rearrange("(a p) t -> p a t", p=128)), writes=[b_h2])
        cw = cx.sb([128, 88, 4], F32, "cw")
        b_c = Buf()
        P.dma("sp", lambda e: e.dma_start(out=cw[:], in_=G["ffn_cwT"][:, :, :]), writes=[b_c])
        y = cx.sb([128, NTOK], F32, "y")
        za = cx.sb([128, NTOK], F32, "za")
        zv = cx.sb([128, NTOK], F32, "zv")
        go = [cx.sb([128, NTOK], BF16, "go") for _ in range(2)]
        b_y, b_za, b_zv = Buf(), Buf(), Buf()
        b_go = [Buf(), Buf()]
        ps = [cx.ps([128, 512], F32, "pe") for _ in range(5)]
        b_ps = [Buf() for _ in range(5)]

        def conv(zt, b_zt, blk):
            P.op("dve", lambda e: e.tensor_scalar(out=zt[:], in0=y[:], scalar1=cw[:, blk, 1:2], scalar2=cw[:, blk, 3:4],
                                                   op0=ALU.mult, op1=ALU.add), reads=[b_y, b_c], writes=[b_zt])
            for (s0, T) in SEQS_ALL:
                P.op("dve", lambda e, s0=s0, T=T: e.scalar_tensor_tensor(
                    out=zt[:, s0 + 1:s0 + T], in0=y[:, s0:s0 + T - 1], scalar=cw[:, blk, 0:1], in1=zt[:, s0 + 1:s0 + T],
                    op0=ALU.mult, op1=ALU.add), reads=[b_y, b_c, b_zt], writes=[b_zt])
                P.op("dve", lambda e, s0=s0, T=T: e.scalar_tensor_tensor(
                    out=zt[:, s0:s0 + T - 1], in0=y[:, s0 + 1:s0 + T], scalar=cw[:, blk, 2:3], in1=zt[:, s0:s0 + T - 1],
                    op0=ALU.mult, op1=ALU.add), reads=[b_y, b_c, b_zt], writes=[b_zt])

        for jb in range(11):
            wa, b_wa = W.load(G["w_up"][:, jb * 512:(jb + 1) * 512], 512)
            wv_, b_wv = W.load(G["w_up"][:, 5632 + jb * 512:5632 + (jb + 1) * 512], 512)
            for sub in range(4):
                fb = jb * 4 + sub
                ss_ = slice(sub * 128, (sub + 1) * 128)
                for (wt, b_w, zt, b_zt, blk) in ((wa, b_wa, za, b_za, fb), (wv_, b_wv, zv, b_zv, 44 + fb)):
                    for tt in range(5):
                        mm_acc(P, ps[tt][:, :], b_ps[tt],
                               [(wt[:, kc, ss_], h2[:, kc, tt * 512:(tt + 1) * 512]) for kc in range(KC)], [b_w, b_h2])
                        P.op("act", lambda e, tt=tt: e.copy(out=y[:, tt * 512:(tt + 1) * 512], in_=ps[tt][:, :]),
                             reads=[b_ps[tt]], writes=[b_y])
                    conv(zt, b_zt, blk)
                P.op("act", lambda e: e.activation(out=za[:], in_=za[:], func=AF.Silu), reads=[b_za], writes=[b_za])
                o2, b_o2 = go[fb % 2], b_go[fb % 2]
                P.op("dve", lambda e, o2=o2: e.tensor_tensor(out=o2[:], in0=za[:], in1=zv[:], op=ALU.mult),
                     reads=[b_za, b_zv], writes=[b_o2])
                P.dma(STQ, lambda e, o2=o2, fb=fb: e.dma_start(out=G["gT"][fb * 128:(fb + 1) * 128, :], in_=o2[:]),
                      reads=[b_o2])
        P.wait_all_dma()
        with nc.Block() as block:
            P.emit(block)


def stage_e2(nc, sync, G):
    with ExitStack() as es:
        cx = Ctx(nc, es)
        P = Prog(sync)
        W = WStream(P, cx, nbuf=4, nstg=4, engines=("act", "dve", "act", "pool"))
        gt = cx.sb([128, 44, 512], BF16, "gt")
        b_gt = Buf()
        r = cx.sb([128, 4, D], F32, "r")
        b_r = [Buf() for _ in range(4)]
        grow = cx.sb([128, D], F32, "grow")
        b_grow = Buf()
        lg_ = cx.sb([128, D], F32, "lng")
        lb_ = cx.sb([128, D], F32, "lnb")
        b_gb = Buf()
        tmp = cx.sb([128, 512], F32, "tmp")
        b_tmp = Buf()
        yo = cx.sb([128, D], F32, "yo")
        b_yo = Buf()
        sel = cx.sb([2, 2, 128], F32, "sel")
        mr = cx.sb([2, 2048], F32, "mr")
        b_sel, b_mr = Buf(), Buf()
        scr = {"st": cx.sb([128, 4, 6], F32), "mv": cx.sb([128, 2], F32), "rstd": cx.sb([128, 1], F32), "b_st": Buf()}
        ps = [cx.ps([128, 512], F32, "pm") for _ in range(8)]
        b_ps = [Buf() for _ in range(8)]
        P.dma("sp", lambda e: e.dma_start(out=sel[:], in_=G["sel"][:, :, :]), writes=[b_sel])
        P.dma("sp", lambda e: e.dma_start(out=lg_[:], in_=G["ln2g_bc"][:, :]), writes=[b_gb])
        P.dma("sp", lambda e: e.dma_start(out=lb_[:], in_=G["ln2b_bc"][:, :]), writes=[b_gb])
        last_c = None
        KG = [(0, 16), (16, 16), (32, 12)]
        for tt in range(NTOK // 512):
            c = 0 if tt == 0 else 1
            tsl = slice(tt * 512, (tt + 1) * 512)
            if c != last_c:
                row_bcast(P, cx, G, c, 2048, grow, b_grow, sel, b_sel, mr, b_mr, ps[0], b_ps[0])
                last_c = c
            P.dma("sp", lambda e, tsl=tsl: e.dma_start(out=gt[:], in_=G["gT"][:, tsl].rearrange("(a p) t -> p a t", p=128)),
                  writes=[b_gt])
            for ts in range(4):
                P.dma("sp", lambda e, ts=ts, tt=tt: e.dma_start(
                    out=r[:, ts, :], in_=G["x1"][tt * 512 + ts * 128:tt * 512 + (ts + 1) * 128, :]), writes=[b_r[ts]])
            for cb in range(4):
                cs_ = slice(cb * 512, (cb + 1) * 512)
                pb0 = (cb % 2) * 4
                for gi, (k0, kn) in enumerate(KG):
                    w, b_w = W.load(G["w_down"][k0 * 128:(k0 + kn) * 128, cs_], 512, kcn=kn)
                    for ts in range(4):
                        for kc in range(kn):
                            P.op("pe", lambda e, ts=ts, kc=kc, k0=k0, w=w, gi=gi, kn=kn, pb0=pb0: e.matmul(
                                ps[pb0 + ts][:, :], lhsT=gt[:, k0 + kc, ts * 128:(ts + 1) * 128], rhs=w[:, kc, :],
                                start=(gi == 0 and kc == 0), stop=(gi == 2 and kc == kn - 1)),
                                reads=[b_w, b_gt], writes=[b_ps[pb0 + ts]])
                for ts in range(4):
                    P.op("dve", lambda e, ts=ts, cs_=cs_, pb0=pb0: e.tensor_tensor(
                        out=tmp[:], in0=ps[pb0 + ts][:, :], in1=grow[:, cs_], op=ALU.mult), reads=[b_ps[pb0 + ts], b_grow],
                        writes=[b_tmp])
                    P.op("dve", lambda e, ts=ts, cs_=cs_: e.scalar_tensor_tensor(
                        out=r[:, ts, cs_], in0=r[:, ts, cs_], scalar=ALPHA, in1=tmp[:], op0=ALU.mult, op1=ALU.add),
                        reads=[b_tmp, b_r[ts]], writes=[b_r[ts]])
            for ts in range(4):
                tok = tt * 512 + ts * 128
                ln_affine_store(P, r[:, ts, :], b_r[ts], scr, lg_, lb_, b_gb, yo, b_yo)
                P.dma(STQ, lambda e, tok=tok: e.dma_start(out=G["y"][tok:tok + 128, :], in_=yo[:]), reads=[b_yo])
        P.wait_all_dma()
        with nc.Block() as block:
            P.emit(block)


def build(debug=None):
    nc = bass.Bass("TRN2", target_bir_lowering=False)
    G = {}

    def din(name, shape, dt=F32):
        G[name] = nc.dram_tensor(name, list(shape), dt, kind="ExternalInput").ap()

    def dout(name, shape, dt=F32):
        G[name] = nc.dram_tensor(name, list(shape), dt, kind="ExternalOutput").ap()

    def dscr(name, shape, dt=F32):
        G[name] = nc.dram_tensor(name, list(shape), dt, kind="ExternalOutput" if debug else "Internal").ap()

    din("x", [NTOK, D])
    din("condT", [128, KC, 2])
    din("w_ada", [D, 6 * D])
    din("b_adaT", [128, 96])
    din("b_adarow", [2, 6 * D])
    din("ident_bf", [128, 128], BF16)
    din("ones_bf", [128, 128], BF16)
    din("w_in", [D, 24640])
    din("lg_bc", [128, 16])
    din("ret_cst", [128, 6, 128])
    din("ret_pcol", [128, 3])
    din("rnw_bc", [128, 4096])
    din("rope_cs", [128, 2, 2048])
    din("state_ret", [2, 8, 256, 512])
    din("state_dn", [2, 16, 128, 128])
    din("dn_cwT", [128, 48, 3])
    din("dn_ab", [64, 2, 32])
    din("dn_cst", [64, 7, 64])
    din("dnw_bc", [64, 2048])
    din("sel", [2, 2, 128])
    din("w_br", [4096, D])
    din("w_bd", [D, D])
    din("w_o", [D, D])
    din("ln1g_bc", [128, D])
    din("ln1b_bc", [128, D])
    din("w_up", [D, 11264])
    din("ffn_cwT", [128, 88, 4])
    din("w_down", [5632, D])
    din("ln2g_bc", [128, D])
    din("ln2b_bc", [128, D])
    dout("y", [NTOK, D])
    dout("new_ret", [2, 2, 8, 256, 512])
    dout("new_dn", [2, 2, 16, 128, 128])
    dscr("og", [6144, NTOK], BF16)
    dscr("sg", [4096, NTOK], BF16)
    dscr("sb_scr", [16, 2, 128, 512], BF16)
    dscr("dqkv", [3, 16, 128, NTOK], BF16)
    dscr("dzg", [NTOK, 2048], BF16)
    dscr("dbg", [64, NTOK // 64, 64], F32)
    dscr("modrow", [2, 4096], F32)
    dscr("yT", [D, NTOK], BF16)
    dscr("x1", [NTOK, D], F32)
    dscr("h2T", [D, NTOK], BF16)
    dscr("gT", [5632, NTOK], BF16)
    upto = debug if isinstance(debug, str) else "all"
    order = ["ada", "ret", "gates", "dnproj", "dnrec", "d1", "d2", "e1", "all"]
    lim = order.index(upto)
    with ExitStack() as es:
        sync = Sync(nc, es)
        modT = es.enter_context(nc.sbuf_tensor("modT", [128, 96, 2], F32))
        G["modT"] = modT
        stage_ada(nc, sync, G)
        passes = [(512, 2048, 1, [(0, 2048, "S", None)], True),
                  (0, 512, 0, [(0, 256, "P", 0), (256, 256, "P", 1)], False)]
        if lim >= 1:
            for (tok0, NT, cond, seqs, rope) in passes:
                with ExitStack() as es2:
                    hT = es2.enter_context(nc.sbuf_tensor("hT%d" % tok0, [128, KC, 2048], BF16))
                    stage_ln1(nc, sync, G, hT, tok0, NT // 128, cond)
                    stage_ret(nc, sync, G, hT, None, NT, seqs, tok0, rope)
                    if lim >= 2:
                        stage_gates(nc, sync, G, hT, NT, tok0)
                    if lim >= 3:
                        stage_dnproj(nc, sync, G, hT, NT, seqs, tok0)
                if lim >= 4:
                    stage_dnrec(nc, sync, G, NT, seqs, tok0)
        if lim >= 5:
            stage_d1(nc, sync, G)
        if lim >= 6:
            stage_d2(nc, sync, G)
        if lim >= 7:
            stage_e1(nc, sync, G)
        if lim >= 8:
            stage_e2(nc, sync, G)
    return nc


_PERM = np.concatenate([np.arange(0, 256, 2), np.arange(1, 256, 2)])


def shared_inputs(inp):
    f = np.float32
    m = {}
    m["w_ada"] = np.ascontiguousarray(inp["w_ada"][0])
    b = inp["b_ada"][0]
    m["b_adaT"] = np.ascontiguousarray(b.reshape(96, 128).T)
    m["b_adarow"] = np.ascontiguousarray(np.stack([b, b], 0))
    m["ident_bf"] = np.eye(128, dtype=f).astype(ml_dtypes.bfloat16)
    m["ones_bf"] = np.ones((128, 128), dtype=f).astype(ml_dtypes.bfloat16)
    w_in = np.array(inp["w_in"][0])
    for h in range(8):
        for base in (0, 2048):
            blk = w_in[:, base + h * 256:base + (h + 1) * 256]
            w_in[:, base + h * 256:base + (h + 1) * 256] = blk[:, _PERM]
    m["w_in"] = np.ascontiguousarray(w_in)
    m["lg_bc"] = np.ascontiguousarray(np.tile(inp["ret_log_decay"][0].reshape(1, 16), (128, 1)))
    j = np.arange(128)[:, None].astype(f)
    i = np.arange(128)[None, :].astype(f)
    cst = np.zeros((128, 6, 128), f)
    cst[:, 0] = np.maximum(i - j, 0)
    cst[:, 1] = np.maximum(j - i, 0)
    cst[:, 2] = (i >= j) / 16.0
    cst[:, 3] = (j >= i) / 16.0
    cst[:, 4] = i + 1 + 0 * j
    cst[:, 5] = 128 - i + 0 * j
    m["ret_cst"] = cst
    p = np.arange(128).astype(f)
    m["ret_pcol"] = np.ascontiguousarray(np.stack([127 - p, p, 128 + 0 * p], 1))
    m["rnw_bc"] = np.ascontiguousarray(np.tile(inp["ret_norm_w"][0][None, :], (128, 1)))
    t = np.arange(2048)
    row = (t // 64).astype(f)
    col = (t % 64).astype(f)
    inv = (np.float32(10000.0) ** (-np.arange(64, dtype=f) / np.float32(64))).astype(f)
    ang = np.concatenate([row[:, None] * inv[None, :], col[:, None] * inv[None, :]], axis=-1).astype(f)
    m["rope_cs"] = np.ascontiguousarray(np.stack([np.cos(ang).T, np.sin(ang).T], 1).astype(f))
    cw = inp["dn_conv_w"][0]
    m["dn_cwT"] = np.ascontiguousarray(cw.reshape(3, 48, 128).transpose(2, 1, 0))
    ab = np.stack([inp["dn_A_log"][0].reshape(32), inp["dn_dt_bias"][0].reshape(32)], 0)
    m["dn_ab"] = np.ascontiguousarray(np.tile(ab[None], (64, 1, 1)))
    a = np.arange(64)[:, None]
    bb = np.arange(64)[None, :]
    NEG = -30000.0
    dc = np.zeros((64, 7, 64), f)
    dc[:, 0] = (a == bb)
    dc[:, 1] = (a <= bb)
    dc[:, 2] = (a >= bb)
    dc[:, 3] = np.where(a > bb, 0.0, NEG)
    dc[:, 4] = np.where(a < bb, 0.0, NEG)
    dc[:, 5] = np.where(bb >= a, 0.0, NEG)
    dc[:, 6] = np.where(bb <= a, 0.0, NEG)
    m["dn_cst"] = dc
    m["dnw_bc"] = np.ascontiguousarray(np.tile(inp["dn_norm_w"][0][None, :], (64, 1)))
    sel = np.zeros((2, 2, 128), f)
    sel[0, 0] = 1
    sel[1, 1] = 1
    m["sel"] = sel
    m["w_br"] = np.ascontiguousarray(inp["w_br"][0])
    m["w_bd"] = np.ascontiguousarray(inp["w_bd"][0])
    m["w_o"] = np.ascontiguousarray(inp["w_o"][0])
    m["w_up"] = np.ascontiguousarray(inp["w_up"][0])
    m["w_down"] = np.ascontiguousarray(inp["w_down"][0])
    for k in ("ln1_g", "ln1_b", "ln2_g", "ln2_b"):
        m[k.replace("_", "") + "_bc"] = np.ascontiguousarray(np.tile(inp[k][0][None, :], (128, 1)))
    fc = np.concatenate([inp["ffn_conv_w"][0], inp["ffn_conv_b"]], 0)
    m["ffn_cwT"] = np.ascontiguousarray(fc.reshape(4, 88, 128).transpose(2, 1, 0))
    return m


def core_inputs(core, inp, shared):
    m = dict(shared)
    xp = inp["x_prompt"][2 * core:2 * core + 2].reshape(512, D)
    m["x"] = np.ascontiguousarray(np.concatenate([xp, inp["x_sample"][core]], axis=0))
    cond = np.stack([inp["c_ctx"], inp["c"][core]], axis=0)
    m["condT"] = np.ascontiguousarray(cond.reshape(2, KC, 128).transpose(2, 1, 0))
    m["state_ret"] = np.ascontiguousarray(inp["state_ret"][core, 0][:, :, _PERM, :])
    m["state_dn"] = np.ascontiguousarray(inp["state_dn"][core, 0])
    return m


_NC_CACHE = {}


def kernel(**inp):
    inp = {k: np.asarray(v) for k, v in inp.items()}
    if "nc" not in _NC_CACHE:
        _NC_CACHE["nc"] = build()
    nc = _NC_CACHE["nc"]
    shared = shared_inputs(inp)
    in_maps = [core_inputs(c, inp, shared) for c in range(8)]
    res = run_bass_kernel_spmd(nc, in_maps, core_ids=list(range(8)))
    y_p = np.zeros((16, 256, D), np.float32)
    y_s = np.zeros((8, 2048, D), np.float32)
    n_ret = np.zeros((16, 1, 2, 8, 256, 512), np.float32)
    n_dn = np.zeros((16, 1, 2, 16, 128, 128), np.float32)
    for c in range(8):
        r = res.results[c]
        y = np.asarray(r["y"])
        y_p[2 * c:2 * c + 2] = y[0:512].reshape(2, 256, D)
        y_s[c] = y[512:]
        nr = np.asarray(r["new_ret"])
        n_ret[2 * c:2 * c + 2, 0][:, :, :, _PERM, :] = nr
        n_dn[2 * c:2 * c + 2, 0] = np.asarray(r["new_dn"])
    return (y_p, y_s, n_ret, n_dn)
```
